# Optimizing a Trainium2 kernel written in Bass

```python
import jax
import jax.numpy as jnp
from jax import lax
import numpy as np

D_MODEL = 1024
BATCH = 4
SEQ = 4096
DEPTH = 1
DEC_BATCH = 32
DEC_SEQ = 8
PAST_LEN = 16384
PAGE_SIZE = 128

PLE_DIM = 256
RET_HEADS = 4
RET_DK = 128
RET_DV = 128
RET_WIDTH = RET_HEADS * RET_DV
RET_CHUNK = 128
NSA_HEADS = 8
NSA_KV_HEADS = 2
NSA_GROUP = NSA_HEADS // NSA_KV_HEADS
NSA_HD = 64
NSA_WIDTH = NSA_HEADS * NSA_HD
KV_WIDTH = NSA_KV_HEADS * NSA_HD
CMP_BLOCK = 32
SLC_BLOCK = 64
TOP_N = 16
WINDOW = 512
Q_BLOCK = 128
MIX_WIDTH = RET_WIDTH + NSA_WIDTH
SPLIT_SIZES = (RET_HEADS * RET_DK, RET_HEADS * RET_DK, RET_WIDTH, RET_WIDTH, NSA_WIDTH, KV_WIDTH, KV_WIDTH, KV_WIDTH, KV_WIDTH, KV_WIDTH, KV_WIDTH, 3 * NSA_HEADS, NSA_WIDTH)
PROJ_WIDTH = sum(SPLIT_SIZES)
ROPE_THETA = 10000.0
RMS_EPS = 1e-6
GN_EPS = 1e-5
NEG_INF = -1e9
FORCED_SCORE = 1e4
ATTN_SCALE = NSA_HD ** -0.5

kernel_name = 'hybrid_retention_nsa_decode_step'

F32 = jnp.float32


def rmsnorm(x, g):
    xf = x.astype(F32)
    y = xf * lax.rsqrt(jnp.mean(xf * xf, axis=-1, keepdims=True) + RMS_EPS)
    return (y * g.astype(F32)).astype(x.dtype)


def rope(x, pos):
    half = x.shape[-1] // 2
    inv = ROPE_THETA ** (-jnp.arange(half, dtype=F32) / half)
    ang = pos.astype(F32)[:, None] * inv[None, :]
    cos = jnp.cos(ang)[:, None, :]
    sin = jnp.sin(ang)[:, None, :]
    xf = x.astype(F32)
    x1, x2 = xf[..., :half], xf[..., half:]
    return jnp.concatenate([x1 * cos - x2 * sin, x2 * cos + x1 * sin], axis=-1).astype(x.dtype)


def in_proj(x, norm_g, w_in):
    B, T = x.shape[:2]
    h = rmsnorm(x, norm_g)
    points = [int(v) for v in np.cumsum(SPLIT_SIZES)[:-1]]
    rq, rk, rv, rg, nq, ck, cv, sk, sv, wk, wv, ngl, ng = jnp.split(h @ w_in, points, axis=-1)
    r = lambda a, H, D: a.reshape(B, T, H, D)
    return (r(rq, RET_HEADS, RET_DK), r(rk, RET_HEADS, RET_DK), r(rv, RET_HEADS, RET_DV), rg,
            r(nq, NSA_HEADS, NSA_HD), r(ck, NSA_KV_HEADS, NSA_HD), r(cv, NSA_KV_HEADS, NSA_HD),
            r(sk, NSA_KV_HEADS, NSA_HD), r(sv, NSA_KV_HEADS, NSA_HD),
            r(wk, NSA_KV_HEADS, NSA_HD), r(wv, NSA_KV_HEADS, NSA_HD), ngl, ng)


def retention_chunk(state, q, k, v):
    C = q.shape[1]
    log_g = jnp.log(1.0 - 2.0 ** (-5.0 - jnp.arange(RET_HEADS, dtype=F32)))
    i = jnp.arange(C, dtype=F32)
    diff = i[:, None] - i[None, :]
    causal = diff >= 0
    dmask = jnp.where(causal[None], jnp.exp(jnp.where(causal, diff, 0.0)[None] * log_g[:, None, None]), 0.0)
    s = jnp.einsum('bihd,bjhd->bhij', q, k) * dmask[None]
    intra = jnp.einsum('bhij,bjhe->bihe', s, v)
    q_decay = jnp.exp((i[:, None] + 1.0) * log_g[None, :])
    cross = jnp.einsum('bihd,bhde->bihe', q * q_decay[None, :, :, None], state)
    k_decay = jnp.exp((C - 1.0 - i)[:, None] * log_g[None, :])
    new_state = jnp.exp(C * log_g)[None, :, None, None] * state + jnp.einsum('bjhd,bjhe->bhde', k * k_decay[None, :, :, None], v)
    return new_state, intra + cross


def retention_mix(rq, rk, rv, pos, state):
    B, T = rq.shape[:2]
    q = rope(rq, pos).astype(F32)
    k = rope(rk, pos).astype(F32) * (RET_DK ** -0.5)
    v = rv.astype(F32)
    C = RET_CHUNK if T % RET_CHUNK == 0 else T
    nC = T // C
    to_chunks = lambda a: a.reshape(B, nC, C, *a.shape[2:]).swapaxes(0, 1)
    final, o = lax.scan(lambda st, xs: retention_chunk(st, *xs), state.astype(F32), (to_chunks(q), to_chunks(k), to_chunks(v)))
    return final, o.swapaxes(0, 1).reshape(B, T, RET_HEADS, RET_DV)


def retention_out(o, gate, gn_g, gn_b):
    B, T = o.shape[:2]
    mu = jnp.mean(o, axis=-1, keepdims=True)
    var = jnp.mean(jnp.square(o - mu), axis=-1, keepdims=True)
    on = ((o - mu) * lax.rsqrt(var + GN_EPS)).reshape(B, T, RET_WIDTH)
    return (on * gn_g.astype(F32) + gn_b.astype(F32)) * jax.nn.silu(gate.astype(F32))


def compress(rows, pe, w1, w2):
    B, L = rows.shape[:2]
    nc = L // CMP_BLOCK
    blk = rows[:, :nc * CMP_BLOCK].reshape(B, nc, CMP_BLOCK, NSA_KV_HEADS, NSA_HD).astype(F32) + pe.astype(F32)
    hid = jax.nn.gelu(jnp.einsum('bcjkd,kjde->bcke', blk, w1))
    return jnp.einsum('bcke,kef->bckf', hid, w2)


def cmp_attend(q, qpos, kc, vc):
    nc = kc.shape[1]
    cend = (jnp.arange(nc) + 1) * CMP_BLOCK - 1
    mask = (cend[None, :] <= qpos[:, None])[None, :, None, None, :]
    s = jnp.einsum('btkgd,bckd->btkgc', q, kc).astype(F32) * ATTN_SCALE
    p = jax.nn.softmax(jnp.where(mask, s, NEG_INF), axis=-1) * mask.astype(F32)
    return jnp.einsum('btkgc,bckd->btkgd', p, vc.astype(F32)), p


def select_blocks(p_cmp, qpos, L):
    nslc = -(-L // SLC_BLOCK)
    ratio = SLC_BLOCK // CMP_BLOCK
    imp = p_cmp.sum(axis=3)
    imp = jnp.pad(imp, ((0, 0), (0, 0), (0, 0), (0, nslc * ratio - imp.shape[-1])))
    imp = imp.reshape(*imp.shape[:-1], nslc, ratio).sum(axis=-1)
    b = jnp.arange(nslc)
    valid = (b[None, :] * SLC_BLOCK <= qpos[:, None])[None, :, None, :]
    forced = ((b[None, :] == 0) | (b[None, :] == (qpos // SLC_BLOCK)[:, None]))[None, :, None, :]
    score = jnp.where(forced, FORCED_SCORE, jnp.where(valid, imp, NEG_INF))
    _, idx = lax.top_k(score, min(TOP_N, nslc))
    return idx


def slc_attend(q, qpos, k_full, v_full, idx):
    B, L = k_full.shape[:2]
    T = q.shape[1]
    nslc = -(-L // SLC_BLOCK)
    padw = ((0, 0), (0, nslc * SLC_BLOCK - L), (0, 0), (0, 0))
    to_blocks = lambda a: jnp.pad(a, padw).reshape(B, nslc, SLC_BLOCK, NSA_KV_HEADS, NSA_HD).transpose(0, 3, 1, 2, 4)
    kb, vb = to_blocks(k_full), to_blocks(v_full)
    qb = Q_BLOCK if T % Q_BLOCK == 0 else T
    nqb = T // qb
    bi = jnp.arange(B)[:, None, None, None]
    hi = jnp.arange(NSA_KV_HEADS)[None, None, :, None]

    def block(args):
        qi, pi, ii = args
        n = ii.shape[-1]
        ks = kb[bi, hi, ii].reshape(B, qb, NSA_KV_HEADS, n * SLC_BLOCK, NSA_HD)
        vs = vb[bi, hi, ii].reshape(B, qb, NSA_KV_HEADS, n * SLC_BLOCK, NSA_HD)
        kpos = (ii[..., None] * SLC_BLOCK + jnp.arange(SLC_BLOCK)).reshape(B, qb, NSA_KV_HEADS, n * SLC_BLOCK)
        mask = (kpos <= pi[None, :, None, None])[:, :, :, None, :]
        s = jnp.einsum('bqkgd,bqknd->bqkgn', qi, ks).astype(F32) * ATTN_SCALE
        p = jax.nn.softmax(jnp.where(mask, s, NEG_INF), axis=-1)
        return jnp.einsum('bqkgn,bqknd->bqkgd', p, vs.astype(F32))

    qs = q.reshape(B, nqb, qb, NSA_KV_HEADS, NSA_GROUP, NSA_HD).swapaxes(0, 1)
    ps = qpos.reshape(nqb, qb)
    iss = idx.reshape(B, nqb, qb, NSA_KV_HEADS, idx.shape[-1]).swapaxes(0, 1)
    out = lax.map(block, (qs, ps, iss))
    return out.swapaxes(0, 1).reshape(B, T, NSA_KV_HEADS, NSA_GROUP, NSA_HD)


def win_attend(q, qpos, k, v, kpos):
    dist = qpos[:, :, None] - kpos[:, None, :]
    mask = ((dist >= 0) & (dist <= WINDOW) & (kpos[:, None, :] >= 0))[None, :, :, None, None, :]
    s = jnp.einsum('bnqkgd,bnmkd->bnqkgm', q, k).astype(F32) * ATTN_SCALE
    p = jax.nn.softmax(jnp.where(mask, s, NEG_INF), axis=-1)
    return jnp.einsum('bnqkgm,bnmkd->bnqkgd', p, v.astype(F32))


def window_prompt(q_rot, wk_r, wv, pos):
    B, S = q_rot.shape[:2]
    qb = Q_BLOCK if S % Q_BLOCK == 0 else S
    nb = S // qb
    idx = (jnp.arange(nb) * qb)[:, None] + jnp.arange(WINDOW + qb)[None, :]
    padw = ((0, 0), (WINDOW, 0), (0, 0), (0, 0))
    kblk = jnp.pad(wk_r, padw)[:, idx]
    vblk = jnp.pad(wv, padw)[:, idx]
    qg = q_rot.reshape(B, nb, qb, NSA_KV_HEADS, NSA_GROUP, NSA_HD)
    o = win_attend(qg, pos.reshape(nb, qb), kblk, vblk, idx - WINDOW)
    return o.reshape(B, S, NSA_KV_HEADS, NSA_GROUP, NSA_HD)


def nsa_combine(q, q_rot, qpos, ck_full, cv_full, sk_full, sv_full, o_win, gate_logits, gate,
                pe_k, w1_k, w2_k, pe_v, w1_v, w2_v):
    B, T = q.shape[:2]
    L = ck_full.shape[1]
    kc = compress(ck_full, pe_k, w1_k, w2_k)
    vc = compress(cv_full, pe_v, w1_v, w2_v)
    o_cmp, p_cmp = cmp_attend(q.reshape(B, T, NSA_KV_HEADS, NSA_GROUP, NSA_HD), qpos, kc, vc)
    idx = select_blocks(p_cmp, qpos, L)
    o_slc = slc_attend(q_rot.reshape(B, T, NSA_KV_HEADS, NSA_GROUP, NSA_HD), qpos, sk_full, sv_full, idx)
    g = jax.nn.sigmoid(gate_logits.astype(F32)).reshape(B, T, 3, NSA_KV_HEADS, NSA_GROUP, 1)
    o = g[:, :, 0] * o_cmp + g[:, :, 1] * o_slc + g[:, :, 2] * o_win
    return o.reshape(B, T, NSA_WIDTH) * jax.nn.silu(gate.astype(F32))


def mix_prompt(x, norm_g, w_in, gn_g, gn_b, pe_k, w1_k, w2_k, pe_v, w1_v, w2_v):
    B, S = x.shape[:2]
    pos = jnp.arange(S)
    rq, rk, rv, rg, nq, ck, cv, sk, sv, wk, wv, ngl, ng = in_proj(x, norm_g, w_in)
    ret_state, o_ret = retention_mix(rq, rk, rv, pos, jnp.zeros((B, RET_HEADS, RET_DK, RET_DV), F32))
    y_ret = retention_out(o_ret, rg, gn_g, gn_b)
    q_rot, sk_r, wk_r = rope(nq, pos), rope(sk, pos), rope(wk, pos)
    o_win = window_prompt(q_rot, wk_r, wv, pos)
    y_nsa = nsa_combine(nq, q_rot, pos, ck, cv, sk_r, sv, o_win, ngl, ng, pe_k, w1_k, w2_k, pe_v, w1_v, w2_v)
    y = jnp.concatenate([y_ret, y_nsa], axis=-1).astype(x.dtype)
    wb = min(WINDOW, S)
    return y, (ret_state, ck, cv, sk_r, sv, wk_r[:, S - wb:], wv[:, S - wb:])


def mix_sample(x, c_ck, c_cv, c_sk, c_sv, win_k, win_v, ret_state, page_table,
               norm_g, w_in, gn_g, gn_b, pe_k, w1_k, w2_k, pe_v, w1_v, w2_v):
    DB, T = x.shape[:2]
    P = page_table.shape[1] * PAGE_SIZE
    pos = P + jnp.arange(T)
    rq, rk, rv, rg, nq, ck, cv, sk, sv, wk, wv, ngl, ng = in_proj(x, norm_g, w_in)
    ret_new, o_ret = retention_mix(rq, rk, rv, pos, ret_state)
    y_ret = retention_out(o_ret, rg, gn_g, gn_b)
    q_rot, sk_r, wk_r = rope(nq, pos), rope(sk, pos), rope(wk, pos)
    past = lambda c: c[page_table].reshape(DB, P, NSA_KV_HEADS, NSA_HD)
    ck_full = jnp.concatenate([past(c_ck), ck], axis=1)
    cv_full = jnp.concatenate([past(c_cv), cv], axis=1)
    sk_full = jnp.concatenate([past(c_sk), sk_r], axis=1)
    sv_full = jnp.concatenate([past(c_sv), sv], axis=1)
    wb = win_k.shape[1]
    kw = jnp.concatenate([win_k, wk_r], axis=1)
    vw = jnp.concatenate([win_v, wv], axis=1)
    kpos = (P - wb + jnp.arange(wb + T))[None, :]
    o_win = win_attend(q_rot.reshape(DB, 1, T, NSA_KV_HEADS, NSA_GROUP, NSA_HD), pos[None, :], kw[:, None], vw[:, None], kpos)
    o_win = o_win.reshape(DB, T, NSA_KV_HEADS, NSA_GROUP, NSA_HD)
    y_nsa = nsa_combine(nq, q_rot, pos, ck_full, cv_full, sk_full, sv_full, o_win, ngl, ng, pe_k, w1_k, w2_k, pe_v, w1_v, w2_v)
    y = jnp.concatenate([y_ret, y_nsa], axis=-1).astype(x.dtype)
    return y, (ret_new, ck, cv, sk_r, sv, kw[:, T:], vw[:, T:])


def finish(x, y_mix, w_out, p_l, norm_ple, w_ple_gate, w_ple):
    x = x + y_mix @ w_out
    gate = jax.nn.sigmoid((rmsnorm(x, norm_ple) @ w_ple_gate).astype(F32))
    return (x.astype(F32) + gate * (p_l @ w_ple).astype(F32)).astype(x.dtype)


def setup_inputs(seed: int = 0) -> dict:
    key = jax.random.key(seed)
    ks = jax.random.split(key, 32)
    nrm = lambda k, shape, s: s * jax.random.normal(k, shape, F32)
    n_pages = PAST_LEN // PAGE_SIZE
    used = DEC_BATCH * n_pages
    n_pool = used + max(1, used // 4)
    kvsh = (DEPTH, n_pool, PAGE_SIZE, NSA_KV_HEADS, NSA_HD)
    wb = min(WINDOW, PAST_LEN)
    page_table = jax.random.permutation(ks[9], n_pool)[:used].reshape(DEC_BATCH, n_pages).astype(jnp.int32)
    return {
        'x_prompt': nrm(ks[0], (BATCH, SEQ, D_MODEL), 1.0),
        'x_sample': nrm(ks[1], (DEC_BATCH, DEC_SEQ, D_MODEL), 1.0),
        'cache_cmp_k': nrm(ks[2], kvsh, 1.0),
        'cache_cmp_v': nrm(ks[3], kvsh, 1.0),
        'cache_slc_k': nrm(ks[4], kvsh, 1.0),
        'cache_slc_v': nrm(ks[5], kvsh, 1.0),
        'state_win_k': nrm(ks[6], (DEPTH, DEC_BATCH, wb, NSA_KV_HEADS, NSA_HD), 1.0),
        'state_win_v': nrm(ks[7], (DEPTH, DEC_BATCH, wb, NSA_KV_HEADS, NSA_HD), 1.0),
        'state_ret': nrm(ks[8], (DEPTH, DEC_BATCH, RET_HEADS, RET_DK, RET_DV), 0.5),
        'page_table': page_table,
        'p_prompt': nrm(ks[10], (DEPTH, BATCH, SEQ, PLE_DIM), 1.0),
        'p_sample': nrm(ks[11], (DEPTH, DEC_BATCH, DEC_SEQ, PLE_DIM), 1.0),
        'norm_mix': 1.0 + nrm(ks[12], (DEPTH, D_MODEL), 0.02),
        'w_in': nrm(ks[13], (DEPTH, D_MODEL, PROJ_WIDTH), D_MODEL ** -0.5),
        'ret_gn_g': 1.0 + nrm(ks[14], (DEPTH, RET_WIDTH), 0.02),
        'ret_gn_b': nrm(ks[15], (DEPTH, RET_WIDTH), 0.02),
        'cmp_pe_k': nrm(ks[16], (DEPTH, CMP_BLOCK, NSA_KV_HEADS, NSA_HD), 0.5),
        'cmp_w1_k': nrm(ks[17], (DEPTH, NSA_KV_HEADS, CMP_BLOCK, NSA_HD, NSA_HD), (CMP_BLOCK * NSA_HD) ** -0.5),
        'cmp_w2_k': nrm(ks[18], (DEPTH, NSA_KV_HEADS, NSA_HD, NSA_HD), NSA_HD ** -0.5),
        'cmp_pe_v': nrm(ks[19], (DEPTH, CMP_BLOCK, NSA_KV_HEADS, NSA_HD), 0.5),
        'cmp_w1_v': nrm(ks[20], (DEPTH, NSA_KV_HEADS, CMP_BLOCK, NSA_HD, NSA_HD), (CMP_BLOCK * NSA_HD) ** -0.5),
        'cmp_w2_v': nrm(ks[21], (DEPTH, NSA_KV_HEADS, NSA_HD, NSA_HD), NSA_HD ** -0.5),
        'w_out': nrm(ks[22], (DEPTH, MIX_WIDTH, D_MODEL), MIX_WIDTH ** -0.5),
        'norm_ple': 1.0 + nrm(ks[23], (DEPTH, D_MODEL), 0.02),
        'w_ple_gate': nrm(ks[24], (DEPTH, D_MODEL, D_MODEL), D_MODEL ** -0.5),
        'w_ple': nrm(ks[25], (DEPTH, PLE_DIM, D_MODEL), PLE_DIM ** -0.5),
        'norm_f': 1.0 + nrm(ks[26], (D_MODEL,), 0.02),
    }


def reference(x_prompt, x_sample, cache_cmp_k, cache_cmp_v, cache_slc_k, cache_slc_v, state_win_k, state_win_v,
              state_ret, page_table, p_prompt, p_sample, norm_mix, w_in, ret_gn_g, ret_gn_b,
              cmp_pe_k, cmp_w1_k, cmp_w2_k, cmp_pe_v, cmp_w1_v, cmp_w2_v, w_out, norm_ple, w_ple_gate, w_ple, norm_f):
    xp, xs = x_prompt, x_sample
    st_p, st_s = [], []
    for l in range(DEPTH):
        cw = (cmp_pe_k[l], cmp_w1_k[l], cmp_w2_k[l], cmp_pe_v[l], cmp_w1_v[l], cmp_w2_v[l])
        yp, sp = mix_prompt(xp, norm_mix[l], w_in[l], ret_gn_g[l], ret_gn_b[l], *cw)
        ys, ss = mix_sample(xs, cache_cmp_k[l], cache_cmp_v[l], cache_slc_k[l], cache_slc_v[l],
                            state_win_k[l], state_win_v[l], state_ret[l], page_table,
                            norm_mix[l], w_in[l], ret_gn_g[l], ret_gn_b[l], *cw)
        xp = finish(xp, yp, w_out[l], p_prompt[l], norm_ple[l], w_ple_gate[l], w_ple[l])
        xs = finish(xs, ys, w_out[l], p_sample[l], norm_ple[l], w_ple_gate[l], w_ple[l])
        st_p.append(sp)
        st_s.append(ss)
    ret_p, cmp_k_p, cmp_v_p, slc_k_p, slc_v_p, win_k_p, win_v_p = [jnp.stack(a) for a in zip(*st_p)]
    ret_s, cmp_k_s, cmp_v_s, slc_k_s, slc_v_s, win_k_s, win_v_s = [jnp.stack(a) for a in zip(*st_s)]
    y_prompt = rmsnorm(xp, norm_f)
    y_sample = rmsnorm(xs, norm_f)
    return (y_prompt, y_sample, ret_p, cmp_k_p, cmp_v_p, slc_k_p, slc_v_p, win_k_p, win_v_p,
            ret_s, cmp_k_s, cmp_v_s, slc_k_s, slc_v_s, win_k_s, win_v_s)
```

```python
import contextlib
import numpy as np
import concourse.bass as bass
import concourse.mybir as mybir
from concourse.bass_utils import run_bass_kernel_spmd

F32 = mybir.dt.float32
BF16 = mybir.dt.bfloat16
I32 = mybir.dt.int32
AF = mybir.ActivationFunctionType
ALU = mybir.AluOpType
AX = mybir.AxisListType

NEG = -30000.0
DEC_T = 8
SB_PER_CORE = 4
C_RQ, C_RK, C_RV, C_RG, C_NQ, C_CK, C_CV, C_SK, C_SV, C_WK, C_WV, C_NGL, C_NG = (
    0, 512, 1024, 1536, 2048, 2560, 2688, 2816, 2944, 3072, 3200, 3328, 3352)
PROJ = 3864


class Buf:
    __slots__ = ("w", "r", "const", "psum")

    def __init__(self, const=False):
        self.w = None
        self.r = []
        self.const = const
        self.psum = False


class T:
    def __init__(self, t, const=False):
        self.t = t
        self.b = Buf(const)


def bcast(ap, pos, n):
    l = [list(x) for x in ap.ap]
    l.insert(pos, [0, n])
    return bass.AP(tensor=ap.tensor, offset=ap.offset, ap=l)


import os as _os
NOSAME = bool(int(_os.environ.get('KNOSAME', '0')))


class FW:
    NDMA = 24

    def __init__(self, nc, es):
        self.nc = nc
        self.eng = {"pe": nc.tensor, "act": nc.scalar, "dve": nc.vector,
                    "pool": nc.gpsimd, "sp": nc.sync}
        self.sem = {k: es.enter_context(nc.semaphore("sem_" + k)) for k in self.eng}
        self.cnt = {k: 0 for k in self.eng}
        self.dsem = [es.enter_context(nc.semaphore("dsem%d" % i)) for i in range(self.NDMA)]
        self.dcnt = [0] * self.NDMA
        self.dnext = 0
        self.waited = {k: {} for k in self.eng}

    def sb(self, es, name, shape, dt, const=False):
        return T(es.enter_context(self.nc.sbuf_tensor("s_" + name, list(shape), dt)), const)

    def ps(self, es, name, shape, dt):
        t = T(es.enter_context(self.nc.psum_tensor("p_" + name, list(shape), dt)))
        t.b.psum = True
        return t

    def _wait(self, e, dep):
        sem, val = dep
        w = self.waited[e]
        key = id(sem)
        if w.get(key, 0) >= val:
            return
        self.eng[e].wait_ge(sem, val)
        w[key] = val

    def _deps(self, e, reads, writes):
        mysem = self.sem[e]
        nosame = NOSAME
        for b in reads:
            if b.w is not None and not ((e == "pe" or nosame) and b.w[0] is mysem):
                self._wait(e, b.w)
            if b.psum:
                for d in b.r:
                    if d[0] is not mysem:
                        self._wait(e, d)
        for b in writes:
            if b.w is not None and not ((e == "pe" or nosame) and b.w[0] is mysem):
                self._wait(e, b.w)
            for d in b.r:
                if d[0] is not mysem:
                    self._wait(e, d)

    def _rec(self, tok, reads, writes):
        for b in reads:
            if not b.const:
                b.r.append(tok)
                if len(b.r) > 64:
                    last = {}
                    for d in b.r:
                        last[id(d[0])] = d
                    b.r = list(last.values())
        for b in writes:
            b.w = tok
            b.r = []

    def op(self, e, fn, reads=(), writes=()):
        self._deps(e, reads, writes)
        ins = fn(self.eng[e])
        self.cnt[e] += 1
        ins.then_inc(self.sem[e], 1)
        self._rec((self.sem[e], self.cnt[e]), reads, writes)

    def dma(self, q, fn, reads=(), writes=()):
        i = self.dnext
        self.dnext = (self.dnext + 1) % self.NDMA
        sem = self.dsem[i]
        if self.dcnt[i] > 0:
            self._wait(q, (sem, self.dcnt[i]))
        self._deps(q, reads, writes)
        ins = fn(self.eng[q])
        self.dcnt[i] += 16
        ins.then_inc(sem, 16)
        self._rec((sem, self.dcnt[i]), reads, writes)

    def barrier(self):
        for e in self.eng:
            for p in self.eng:
                if p != e and self.cnt[p] > 0:
                    self._wait(e, (self.sem[p], self.cnt[p]))
            for i in range(self.NDMA):
                if self.dcnt[i] > 0:
                    self._wait(e, (self.dsem[i], self.dcnt[i]))


def build(SEQ, NPG, NPOOL, STOP=99):
    import os
    SUB = int(os.environ.get('KSUB', '99'))
    SUB5 = int(os.environ.get('KSUB5', '99'))
    SUBQ = int(os.environ.get('KSUBQ', '99'))
    KDBG = int(os.environ.get('KDBG', '0'))
    NB = SEQ // 128
    NP = NB // 2
    NPo = NP * 128
    NC = NB * 4
    NSB = NB * 2
    assert NC <= 128 and NSB > 16 and 2 * NPG >= 16
    nc = bass.Bass("TRN2", target_bir_lowering=False)
    es0 = contextlib.ExitStack()
    fw = FW(nc, es0)

    def din(name, shape, dt=F32):
        return nc.dram_tensor(name, list(shape), dt, kind="ExternalInput").ap()

    def dout(name, shape, dt=F32):
        return nc.dram_tensor(name, list(shape), dt, kind="ExternalOutput").ap()

    xb = din("xb", [SEQ, 1024]); x_own = din("x_own", [NPo, 1024]); pp_own = din("pp_own", [NPo, 256])
    rope_kv = din("rope_kv", [SEQ, 320]); rope_own = din("rope_own", [NPo, 320]); rope_s = din("rope_s", [8, 320])
    xs = din("xs", [32, 1024]); pps = din("pps", [32, 256])
    caches = [din(n, [NPOOL * 8, 2048]) for n in ("c_ck", "c_cv", "c_sk", "c_sv")]
    win_k = din("win_k", [4, 512, 128]); win_v = din("win_v", [4, 512, 128])
    st_ret = din("st_ret", [4, 4, 128, 128]); ptab = din("ptab", [4, NPG], I32)
    w_in = din("w_in", [1024, PROJ]); w_out = din("w_out", [1024, 1024]); w_gate = din("w_gate", [1024, 1024])
    w_ple = din("w_ple", [256, 1024])
    norm_mix = din("norm_mix", [1024]); norm_ple = din("norm_ple", [1024]); norm_f = din("norm_f", [1024])
    gn_g = din("gn_g", [512]); gn_b = din("gn_b", [512])
    pe_kv = [din("pe_k", [32, 128]), din("pe_v", [32, 128])]
    w1_kv = [din("w1_k", [2, 32, 64, 64]), din("w1_v", [2, 32, 64, 64])]
    w2_kv = [din("w2_k", [2, 64, 64]), din("w2_v", [2, 64, 64])]
    mTB = din("mTB", [128, 2, 128]); mWB = din("mWB", [128, 6, 128])
    mCB = din("mCB", [NP, 128, 128]); mCBq = din("mCBq", [NP, 128, 128]); mFB = din("mFB", [NP, 128, NSB])
    hsel_d = din("hsel", [128, 8])
    dmaskT_d = din("dmaskT", [128, 4, 128]); qdecT_d = din("qdecT", [128, 4, 128]); kdec_d = din("kdec", [128, 4])
    dmask8T_d = din("dmask8T", [8, 4, 8]); qdec8T_d = din("qdec8T", [128, 4, 8]); kdec8_d = din("kdec8", [8, 4])
    selG_d = din("selG", [32, 4, 8]); expand_d = din("expand", [NSB, SEQ]); mWBs = din("mWBs", [128, 8]); mNB8 = din("mNB8", [8, 8])
    gC = [float((1.0 - 2.0 ** (-5.0 - h)) ** 128) for h in range(4)]
    gC8 = [float((1.0 - 2.0 ** (-5.0 - h)) ** 8) for h in range(4)]

    y_own = dout("y_own", [NPo, 1024]); ret_p = dout("ret_p", [4, 128, 128])
    kvout = dout("kvout", [SEQ, 4, 128]); winout = dout("winout", [512, 2, 128])
    y_s = dout("y_s", [32, 1024]); ret_s = dout("ret_s", [4, 4, 128, 128])
    kv_s = dout("kv_s", [32, 4, 128]); win_s = dout("win_s", [4, 2, 512, 128])
    wscr = nc.dram_tensor("wscr", [2, 1024, 1024], BF16, kind="Internal").ap()
    OUTB = Buf()
    SCRB = Buf()

    dbgst = [None]

    def dbg(name, src, n, cols, rb):
        if not KDBG:
            return
        d = dout("dbg_" + name, [n, cols])
        st = dbgst[0]
        fw.op("dve", lambda e: e.tensor_copy(out=st.t[0:n, 0:cols], in_=src), [rb], [st.b])
        fw.dma("sp", lambda e: e.dma_start(out=d, in_=st.t[0:n, 0:cols]), [st.b], [OUTB])

    def mm(out, lhsT, rhs, start, stop, reads, wb):
        fw.op("pe", lambda e: e.matmul(out, lhsT=lhsT, rhs=rhs, start=start, stop=stop,
                                       skip_group_check=True), reads, [wb])

    def tr(out, in_, ident, reads, wb):
        fw.op("pe", lambda e: e.transpose(out=out, in_=in_, identity=ident), reads, [wb])

    def dve(fn, reads, writes):
        fw.op("dve", fn, reads, writes)

    def act(fn, reads, writes):
        fw.op("act", fn, reads, writes)

    def ld(out, in_, wb, q="sp", reads=()):
        fw.dma(q, lambda e: e.dma_start(out=out, in_=in_), reads, [wb])

    S = lambda name, shape, dt=F32, const=False: fw.sb(es0, name, shape, dt, const)
    Wi = S("Wi", [128, 8, PROJ], BF16, True)
    Wp = S("Wp", [128, 2, 1024], BF16, True)
    W1 = [S("W1k", [128, 32, 128], BF16, True), S("W1v", [128, 32, 128], BF16, True)]
    W2 = [S("W2k", [128, 128], BF16, True), S("W2v", [128, 128], BF16, True)]
    peT = S("peT", [128, 2, 32], F32, True)
    gmix = S("gmix", [128, 8], F32, True); gple = S("gple", [128, 8], F32, True)
    gf_rep = S("gf_rep", [128, 1024], F32, True)
    gng_rep = S("gng_rep", [128, 512], F32, True); gnb_rep = S("gnb_rep", [128, 512], F32, True)
    ident = S("ident", [128, 128], F32, True); identb = S("identb", [128, 128], BF16, True)
    zerob = S("zerob", [1, 512], BF16, True)
    dmaskT = S("dmaskT", [128, 4, 128], F32, True); qdecT = S("qdecT", [128, 4, 128], F32, True)
    kdec = S("kdec", [128, 4], F32, True); hsel = S("hsel", [128, 8], F32, True)
    dmask8T = S("dmask8T", [8, 4, 8], F32, True); qdec8T = S("qdec8T", [128, 4, 8], F32, True)
    kdec8 = S("kdec8", [8, 4], F32, True)
    wb = [S("wbA", [128, 8, 512], BF16), S("wbB", [128, 8, 512], BF16)]
    wbuf = wb[0]
    xt = S("xt", [128, 1024]); xo = S("xo", [128, 1024]); xsbf = S("xsbf", [128, 1024], BF16)
    hT = S("hT", [128, 8, 128], BF16); hT2 = S("hT2", [128, 8, 128], BF16)
    ss = S("ss", [128, 8]); ropet = S("ropet", [128, 320])
    rT1 = S("rT1", [128, 512]); rT2 = S("rT2", [128, 512]); rO = S("rO", [128, 512])
    tokbf = S("tokbf", [128, 1024], BF16)
    kvst = S("kvst", [128, 4, 128]); wst = S("wst", [128, 2, 128])
    Kd = S("Kd", [128, 4, 128], BF16); Vbf = S("Vbf", [128, 4, 128], BF16)
    Sst = S("Sst", [128, 4, 128]); Sown = S("Sown", [128, 4, 128]); Sownb = S("Sownb", [128, 4, 128], BF16)
    blkT = S("blkT", [128, 2, 128], BF16)
    gl = [S("gl%d" % i, [128, 128]) for i in range(3)]
    hidb = S("hidb", [128, 2, 128], BF16)
    vcst = S("vcst", [128, 128], BF16)
    qT = S("qT", [128, 4, 128], BF16); qdT = S("qdT", [128, 4, 128], BF16); kT = S("kT", [128, 4, 128], BF16)
    vown = S("vown", [128, 4, 128], BF16); sTm = S("sTm", [128, 4, 128], BF16)
    QTu = S("QTu", [128, 4, 128], BF16); QTr = S("QTr", [128, 4, 128], BF16)
    sig = S("sig", [128, 24]); sgn = S("sgn", [128, 512])
    ET = [S("ET0", [128, 512], BF16), S("ET1", [128, 512], BF16)]
    e2 = S("e2", [128, 4, 128]); rsum = S("rsum", [128, 8]); imp = S("imp", [128, 128])
    score = S("score", [128, 256]); sc2 = S("sc2", [128, 256]); m8 = S("m8", [128, 16]); selb = S("selb", [128, 256])
    selT = S("selT", [128, 2, 128], BF16)
    ynsa = S("ynsa", [128, 2, 4, 64]); ytmp = S("ytmp", [128, 4, 64]); fac = S("fac", [128, 8])
    ymix = S("ymix", [128, 1024], BF16)
    ppt = S("ppt", [128, 256]); ppb = S("ppb", [128, 256], BF16); ppT = S("ppT", [128, 2, 128], BF16)
    gst = S("gst", [128, 8])
    if KDBG:
        dbgst[0] = S("dbgst", [128, 1024])

    PJ = [fw.ps(es0, "PJ0", [128, 512], F32), fw.ps(es0, "PJ1", [128, 512], F32)]
    PT = fw.ps(es0, "PT", [128, 8, 128], BF16)
    PT32 = fw.ps(es0, "PT32", [128, 512], F32)
    PS = [fw.ps(es0, "PS0", [128, 512], F32), fw.ps(es0, "PS1", [128, 512], F32)]
    PO = fw.ps(es0, "PO", [128, 512], F32)
    PM = fw.ps(es0, "PM", [128, 4, 128], F32)

    esw = contextlib.ExitStack()
    stg = fw.sb(esw, "stg", [128, PROJ], F32)
    fw.op("pool", lambda e: e.memset(ident.t[:], 1.0), [], [ident.b])
    fw.op("pool", lambda e: e.affine_select(out=ident.t[:], in_=ident.t[:], pattern=[[-1, 128]],
                                            compare_op=ALU.is_equal, fill=0.0, base=0, channel_multiplier=1),
          [ident.b], [ident.b])
    dve(lambda e: e.tensor_copy(out=identb.t[:], in_=ident.t[:]), [ident.b], [identb.b])
    fw.op("pool", lambda e: e.memset(zerob.t[:], 0.0), [], [zerob.b])
    for kc in range(8):
        ld(stg.t[:, :], w_in[kc * 128:(kc + 1) * 128, :], stg.b)
        act(lambda e: e.copy(out=Wi.t[:, kc, :], in_=stg.t[:, :]), [stg.b], [Wi.b])
    for wi_, wsrc in enumerate((w_out, w_gate)):
        for nn in range(2):
            for kc in range(8):
                ld(stg.t[:, 0:512], wsrc[kc * 128:(kc + 1) * 128, nn * 512:(nn + 1) * 512], stg.b)
                dve(lambda e: e.tensor_copy(out=wbuf.t[:, kc, :], in_=stg.t[:, 0:512]), [stg.b], [wbuf.b])
            fw.dma("sp", lambda e: e.dma_start(out=wscr[wi_][:, nn * 512:(nn + 1) * 512].rearrange("(k p) n -> p k n", p=128),
                                               in_=wbuf.t[:, :, :]), [wbuf.b], [SCRB])
    for kc in range(2):
        ld(stg.t[:, 0:1024], w_ple[kc * 128:(kc + 1) * 128, :], stg.b)
        dve(lambda e: e.tensor_copy(out=Wp.t[:, kc, :], in_=stg.t[:, 0:1024]), [stg.b], [Wp.b])
    for kv in range(2):
        for jh2 in range(2):
            fw.op("pool", lambda e: e.memset(stg.t[:, 0:2048], 0.0), [], [stg.b])
            for k in range(2):
                for jq in range(2):
                    j0 = jh2 * 16 + jq * 8
                    fw.dma("sp", lambda e: e.dma_start(
                        out=stg.t[k * 64:(k + 1) * 64, 0:2048].rearrange("p (j m) -> p j m", j=16)[:, jq * 8:(jq + 1) * 8, k * 64:(k + 1) * 64],
                        in_=w1_kv[kv][k, j0:j0 + 8].rearrange("j d e -> d j e")), [], [stg.b])
            dve(lambda e: e.tensor_copy(out=W1[kv].t[:, jh2 * 16:(jh2 + 1) * 16, :].rearrange("p j m -> p (j m)"), in_=stg.t[:, 0:2048]),
                [stg.b], [W1[kv].b])
        fw.op("pool", lambda e: e.memset(stg.t[:, 0:128], 0.0), [], [stg.b])
        for k in range(2):
            ld(stg.t[k * 64:(k + 1) * 64, k * 64:(k + 1) * 64], w2_kv[kv][k], stg.b)
        dve(lambda e: e.tensor_copy(out=W2[kv].t[:, :], in_=stg.t[:, 0:128]), [stg.b], [W2[kv].b])
        ld(stg.t[0:32, 0:128], pe_kv[kv], stg.b)
        tr(PT32.t[:, 0:32], stg.t[0:32, 0:128], ident.t[0:32, 0:32], [stg.b, ident.b], PT32.b)
        act(lambda e: e.copy(out=peT.t[:, kv, :], in_=PT32.t[:, 0:32]), [PT32.b], [peT.b])
    for gt_, gd_ in ((gmix, norm_mix), (gple, norm_ple)):
        ld(stg.t[0:8, 0:128], gd_.rearrange("(k p) -> k p", p=128), stg.b)
        tr(PT32.t[:, 0:8], stg.t[0:8, 0:128], ident.t[0:8, 0:8], [stg.b, ident.b], PT32.b)
        act(lambda e: e.copy(out=gt_.t[:, :], in_=PT32.t[:, 0:8]), [PT32.b], [gt_.b])

    def rep(v, n):
        return bass.AP(tensor=v.tensor, offset=v.offset, ap=[[0, 128], [1, n]])
    ld(gf_rep.t[:, :], rep(norm_f, 1024), gf_rep.b)
    ld(gng_rep.t[:, :], rep(gn_g, 512), gng_rep.b)
    ld(gnb_rep.t[:, :], rep(gn_b, 512), gnb_rep.b)
    for t_, d_ in ((dmaskT, dmaskT_d), (qdecT, qdecT_d), (kdec, kdec_d), (hsel, hsel_d),
                   (dmask8T, dmask8T_d), (qdec8T, qdec8T_d), (kdec8, kdec8_d)):
        ld(t_.t[:], d_, t_.b)

    fw.barrier()
    esw.close()

    def norm_T(x, n, gcol, out_hT):
        act(lambda e: e.activation(out=xsbf.t[0:n, :], in_=x.t[0:n, :], func=AF.Square, accum_out=ss.t[0:n, 0:1]),
            [x.b], [xsbf.b, ss.b])
        dve(lambda e: e.tensor_scalar(out=ss.t[0:n, 0:1], in0=ss.t[0:n, 0:1], scalar1=1.0 / 1024, scalar2=1e-6,
                                      op0=ALU.mult, op1=ALU.add), [ss.b], [ss.b])
        act(lambda e: e.activation(out=ss.t[0:n, 0:1], in_=ss.t[0:n, 0:1], func=AF.Sqrt), [ss.b], [ss.b])
        dve(lambda e: e.reciprocal(out=ss.t[0:n, 0:1], in_=ss.t[0:n, 0:1]), [ss.b], [ss.b])
        dve(lambda e: e.tensor_scalar(out=xsbf.t[0:n, :], in0=x.t[0:n, :], scalar1=ss.t[0:n, 0:1], scalar2=None,
                                      op0=ALU.mult), [x.b, ss.b], [xsbf.b])
        for kc in range(8):
            tr(PT.t[:, kc, 0:n], xsbf.t[0:n, kc * 128:(kc + 1) * 128], identb.t[0:n, 0:n], [xsbf.b, identb.b], PT.b)
        dve(lambda e: e.tensor_tensor(out=out_hT.t[:, :, 0:n], in0=PT.t[:, :, 0:n], in1=bcast(gcol.t[:, :], 2, n),
                                      op=ALU.mult), [PT.b, gcol.b], [out_hT.b])

    def proj(h, n, c0, w, bank):
        for kc in range(8):
            mm(bank.t[0:n, 0:w], h.t[:, kc, 0:n], Wi.t[:, kc, c0:c0 + w], kc == 0, kc == 7, [h.b, Wi.b], bank.b)

    def rope(src, sb_, dst, db_, n, H, half, c0):
        s4 = src.rearrange("p (h t f) -> p h t f", h=H, t=2)
        d4 = dst.rearrange("p (h t f) -> p h t f", h=H, t=2)
        cs = bcast(ropet.t[0:n, c0:c0 + half], 1, H)
        sn = bcast(ropet.t[0:n, c0 + half:c0 + 2 * half], 1, H)
        t1 = rT1.t[0:n, 0:H * half].rearrange("p (h f) -> p h f", h=H)
        t2 = rT2.t[0:n, 0:H * half].rearrange("p (h f) -> p h f", h=H)
        x1, x2 = s4[:, :, 0, :], s4[:, :, 1, :]
        dve(lambda e: e.tensor_tensor(out=t1, in0=x1, in1=cs, op=ALU.mult), [sb_, ropet.b], [rT1.b])
        dve(lambda e: e.tensor_tensor(out=t2, in0=x2, in1=sn, op=ALU.mult), [sb_, ropet.b], [rT2.b])
        dve(lambda e: e.tensor_tensor(out=d4[:, :, 0, :], in0=t1, in1=t2, op=ALU.subtract), [rT1.b, rT2.b], [db_])
        dve(lambda e: e.tensor_tensor(out=t1, in0=x2, in1=cs, op=ALU.mult), [sb_, ropet.b], [rT1.b])
        dve(lambda e: e.tensor_tensor(out=t2, in0=x1, in1=sn, op=ALU.mult), [sb_, ropet.b], [rT2.b])
        dve(lambda e: e.tensor_tensor(out=d4[:, :, 1, :], in0=t1, in1=t2, op=ALU.add), [rT1.b, rT2.b], [db_])

    def gelu_to(src, sb_, dst, db_, np_, n):
        a, b, c = gl[0], gl[1], gl[2]
        act(lambda e: e.copy(out=a.t[0:np_, 0:n], in_=src), [sb_], [a.b])
        dve(lambda e: e.tensor_tensor(out=b.t[0:np_, 0:n], in0=a.t[0:np_, 0:n], in1=a.t[0:np_, 0:n], op=ALU.mult), [a.b], [b.b])
        dve(lambda e: e.tensor_scalar(out=b.t[0:np_, 0:n], in0=b.t[0:np_, 0:n], scalar1=0.044715, scalar2=1.0,
                                      op0=ALU.mult, op1=ALU.add), [b.b], [b.b])
        dve(lambda e: e.tensor_tensor(out=b.t[0:np_, 0:n], in0=b.t[0:np_, 0:n], in1=a.t[0:np_, 0:n], op=ALU.mult), [a.b, b.b], [b.b])
        act(lambda e: e.activation(out=c.t[0:np_, 0:n], in_=b.t[0:np_, 0:n], func=AF.Tanh, scale=0.7978845608028654), [b.b], [c.b])
        dve(lambda e: e.tensor_scalar(out=c.t[0:np_, 0:n], in0=c.t[0:np_, 0:n], scalar1=0.5, scalar2=0.5,
                                      op0=ALU.mult, op1=ALU.add), [c.b], [c.b])
        dve(lambda e: e.tensor_tensor(out=dst, in0=c.t[0:np_, 0:n], in1=a.t[0:np_, 0:n], op=ALU.mult), [a.b, c.b], [db_])

    def retention_q(n, dm, qd, Sb):
        for i in range(8):
            tr(PT.t[:, i, 0:n], tokbf.t[0:n, i * 128:(i + 1) * 128], identb.t[0:n, 0:n], [tokbf.b, identb.b], PT.b)
        act(lambda e: e.copy(out=qT.t[:, :, 0:n], in_=PT.t[:, 0:4, 0:n]), [PT.b], [qT.b])
        dve(lambda e: e.tensor_tensor(out=qdT.t[:, :, 0:n], in0=PT.t[:, 0:4, 0:n], in1=qd.t[:, :, 0:n], op=ALU.mult),
            [PT.b, qd.b], [qdT.b])
        act(lambda e: e.copy(out=kT.t[:, :, 0:n], in_=PT.t[:, 4:8, 0:n]), [PT.b], [kT.b])
        for h in range(4):
            mm(PM.t[0:n, h, 0:n], kT.t[:, h, 0:n], qT.t[:, h, 0:n], True, True, [kT.b, qT.b], PM.b)
        dve(lambda e: e.tensor_tensor(out=sTm.t[0:n, :, 0:n], in0=PM.t[0:n, :, 0:n], in1=dm.t[0:n, :, 0:n], op=ALU.mult),
            [PM.b, dm.b], [sTm.b])
        o = PJ[1]
        for h in range(4):
            mm(o.t[0:n, h * 128:(h + 1) * 128], sTm.t[0:n, h, 0:n], vown.t[0:n, h, :], True, False, [sTm.b, vown.b], o.b)
            mm(o.t[0:n, h * 128:(h + 1) * 128], qdT.t[:, h, 0:n], Sb.t[:, h, :], False, True, [qdT.b, Sb.b], o.b)
        ov = rT1.t[0:n, :].rearrange("p (h e) -> p h e", h=4)
        sq = rT2.t[0:n, :].rearrange("p (h e) -> p h e", h=4)
        act(lambda e: e.copy(out=rT1.t[0:n, :], in_=o.t[0:n, :]), [o.b], [rT1.b])
        dve(lambda e: e.tensor_reduce(out=gst.t[0:n, 0:4], in_=ov, axis=AX.X, op=ALU.add), [rT1.b], [gst.b])
        dve(lambda e: e.tensor_tensor(out=sq, in0=ov, in1=ov, op=ALU.mult), [rT1.b], [rT2.b])
        dve(lambda e: e.tensor_reduce(out=gst.t[0:n, 4:8], in_=sq, axis=AX.X, op=ALU.add), [rT2.b], [gst.b])
        dve(lambda e: e.tensor_scalar(out=gst.t[0:n, 0:8], in0=gst.t[0:n, 0:8], scalar1=1.0 / 128, scalar2=None, op0=ALU.mult),
            [gst.b], [gst.b])
        dve(lambda e: e.tensor_tensor(out=fac.t[0:n, 0:4], in0=gst.t[0:n, 0:4], in1=gst.t[0:n, 0:4], op=ALU.mult), [gst.b], [fac.b])
        dve(lambda e: e.tensor_tensor(out=gst.t[0:n, 4:8], in0=gst.t[0:n, 4:8], in1=fac.t[0:n, 0:4], op=ALU.subtract),
            [gst.b, fac.b], [gst.b])
        dve(lambda e: e.tensor_scalar(out=gst.t[0:n, 4:8], in0=gst.t[0:n, 4:8], scalar1=1e-5, scalar2=None, op0=ALU.add),
            [gst.b], [gst.b])
        act(lambda e: e.activation(out=gst.t[0:n, 4:8], in_=gst.t[0:n, 4:8], func=AF.Sqrt), [gst.b], [gst.b])
        dve(lambda e: e.reciprocal(out=gst.t[0:n, 4:8], in_=gst.t[0:n, 4:8]), [gst.b], [gst.b])
        dve(lambda e: e.tensor_tensor(out=ov, in0=ov, in1=bcast(gst.t[0:n, 0:4], 2, 128), op=ALU.subtract), [rT1.b, gst.b], [rT1.b])
        dve(lambda e: e.tensor_tensor(out=ov, in0=ov, in1=bcast(gst.t[0:n, 4:8], 2, 128), op=ALU.mult), [rT1.b, gst.b], [rT1.b])
        dve(lambda e: e.tensor_tensor(out=rT1.t[0:n, :], in0=rT1.t[0:n, :], in1=gng_rep.t[0:n, :], op=ALU.mult), [rT1.b, gng_rep.b], [rT1.b])
        dve(lambda e: e.tensor_tensor(out=rT1.t[0:n, :], in0=rT1.t[0:n, :], in1=gnb_rep.t[0:n, :], op=ALU.add), [rT1.b, gnb_rep.b], [rT1.b])

    def attend(n, k, Q, tiles, br, first):
        mm(PO.t[0:n, 0:260], zerob.t[0:1, 0:n], zerob.t[0:1, 0:260], True, False, [zerob.b], PO.b)
        nt = len(tiles)

        def pv(i):
            KTa, Va, nk, rds, biases = tiles[i]
            et = ET[i % 2]
            for g in range(4):
                mm(PO.t[0:n, g * 65:(g + 1) * 65], et.t[0:nk, g * n:(g + 1) * n], Va, False, i == nt - 1, rds + [et.b], PO.b)
        for i, (KTa, Va, nk, rds, biases) in enumerate(tiles):
            ps_, et = PS[i % 2], ET[i % 2]
            outv = ps_.t[0:nk, 0:4 * n].rearrange("p (g q) -> p g q", g=4)
            mm(outv, KTa, Q.t[k * 64:(k + 1) * 64, :, 0:n], True, len(biases) == 0, rds + [Q.b], ps_.b)
            for bi, (bl, br_, brd) in enumerate(biases):
                mm(outv, bl, br_, False, bi == len(biases) - 1, brd, ps_.b)
            act(lambda e: e.activation(out=et.t[0:nk, 0:4 * n], in_=ps_.t[0:nk, 0:4 * n], func=AF.Exp, scale=0.125),
                [ps_.b], [et.b])
            if i > 0:
                pv(i - 1)
        pv(nt - 1)
        pov = PO.t[0:n, 0:260].rearrange("p (g c) -> p g c", g=4)
        dve(lambda e: e.tensor_scalar(out=fac.t[0:n, 0:4], in0=pov[:, :, 64], scalar1=1e-30, scalar2=None, op0=ALU.add), [PO.b], [fac.b])
        dve(lambda e: e.reciprocal(out=fac.t[0:n, 0:4], in_=fac.t[0:n, 0:4]), [fac.b], [fac.b])
        dve(lambda e: e.tensor_tensor(out=fac.t[0:n, 0:4], in0=fac.t[0:n, 0:4], in1=sig.t[0:n, br * 8 + k * 4:br * 8 + k * 4 + 4],
                                      op=ALU.mult), [fac.b, sig.b], [fac.b])
        if first:
            dve(lambda e: e.tensor_tensor(out=ynsa.t[0:n, k, :, :], in0=pov[:, :, 0:64], in1=bcast(fac.t[0:n, 0:4], 2, 64),
                                          op=ALU.mult), [PO.b, fac.b], [ynsa.b])
        else:
            dve(lambda e: e.tensor_tensor(out=ytmp.t[0:n, :, :], in0=pov[:, :, 0:64], in1=bcast(fac.t[0:n, 0:4], 2, 64),
                                          op=ALU.mult), [PO.b, fac.b], [ytmp.b])
            dve(lambda e: e.tensor_tensor(out=ynsa.t[0:n, k, :, :], in0=ynsa.t[0:n, k, :, :], in1=ytmp.t[0:n, :, :],
                                          op=ALU.add), [ynsa.b, ytmp.b], [ynsa.b])

    def nq_prep(n, c_n):
        src = PJ[1]
        pv = lambda ap: ap.rearrange("p (two m d) -> p m two d", two=2, m=4)
        dv = lambda ap: ap.rearrange("p (m two d) -> p m two d", two=2, m=4)
        act(lambda e: e.copy(out=dv(tokbf.t[0:n, 0:512]), in_=pv(src.t[0:n, 0:512])), [src.b], [tokbf.b])
        rope(src.t[0:n, 0:512], src.b, rO.t[0:n, 0:512], rO.b, n, 8, 32, c_n)
        act(lambda e: e.copy(out=dv(tokbf.t[0:n, 512:1024]), in_=pv(rO.t[0:n, 0:512])), [rO.b], [tokbf.b])
        for i in range(8):
            tr(PT.t[:, i, 0:n], tokbf.t[0:n, i * 128:(i + 1) * 128], identb.t[0:n, 0:n], [tokbf.b, identb.b], PT.b)
        act(lambda e: e.copy(out=QTu.t[:, :, 0:n], in_=PT.t[:, 0:4, 0:n]), [PT.b], [QTu.b])
        act(lambda e: e.copy(out=QTr.t[:, :, 0:n], in_=PT.t[:, 4:8, 0:n]), [PT.b], [QTr.b])

    def prefetch_fin(n, ppsrc):
        for nn in range(2):
            ld(wb[nn].t[:, :, :], wscr[0][:, nn * 512:(nn + 1) * 512].rearrange("(k p) n -> p k n", p=128), wb[nn].b, reads=[SCRB])
        ld(ppt.t[0:n, :], ppsrc, ppt.b)

    def finish(n, x, ppsrc, ydst):
        for i in range(8):
            tr(PT.t[:, i, 0:n], ymix.t[0:n, i * 128:(i + 1) * 128], identb.t[0:n, 0:n], [ymix.b, identb.b], PT.b)
        act(lambda e: e.copy(out=hT2.t[:, :, 0:n], in_=PT.t[:, :, 0:n]), [PT.b], [hT2.b])
        for nn in range(2):
            for kc in range(8):
                mm(PJ[nn].t[0:n, :], hT2.t[:, kc, 0:n], wb[nn].t[:, kc, :], kc == 0, kc == 7,
                   [hT2.b, wb[nn].b], PJ[nn].b)
            ld(wb[nn].t[:, :, :], wscr[1][:, nn * 512:(nn + 1) * 512].rearrange("(k p) n -> p k n", p=128), wb[nn].b, reads=[SCRB])
            dve(lambda e: e.tensor_tensor(out=x.t[0:n, nn * 512:(nn + 1) * 512], in0=x.t[0:n, nn * 512:(nn + 1) * 512],
                                          in1=PJ[nn].t[0:n, :], op=ALU.add), [x.b, PJ[nn].b], [x.b])
        if KDBG and n == 128 and not hasattr(finish, "done"):
            dbg("x2", x.t[:, :], 128, 1024, x.b)
        norm_T(x, n, gple, hT)
        dve(lambda e: e.tensor_copy(out=ppb.t[0:n, :], in_=ppt.t[0:n, :]), [ppt.b], [ppb.b])
        for nn in range(2):
            for kc in range(8):
                mm(PJ[nn].t[0:n, :], hT.t[:, kc, 0:n], wb[nn].t[:, kc, :], kc == 0, kc == 7,
                   [hT.b, wb[nn].b], PJ[nn].b)
            act(lambda e: e.activation(out=xt.t[0:n, nn * 512:(nn + 1) * 512], in_=PJ[nn].t[0:n, :], func=AF.Sigmoid),
                [PJ[nn].b], [xt.b])
        for i in range(2):
            tr(PT.t[:, i, 0:n], ppb.t[0:n, i * 128:(i + 1) * 128], identb.t[0:n, 0:n], [ppb.b, identb.b], PT.b)
        act(lambda e: e.copy(out=ppT.t[:, :, 0:n], in_=PT.t[:, 0:2, 0:n]), [PT.b], [ppT.b])
        for nn in range(2):
            for kc in range(2):
                mm(PJ[nn].t[0:n, :], ppT.t[:, kc, 0:n], Wp.t[:, kc, nn * 512:(nn + 1) * 512], kc == 0, kc == 1,
                   [ppT.b, Wp.b], PJ[nn].b)
            dve(lambda e: e.tensor_tensor(out=xt.t[0:n, nn * 512:(nn + 1) * 512], in0=xt.t[0:n, nn * 512:(nn + 1) * 512],
                                          in1=PJ[nn].t[0:n, :], op=ALU.mult), [xt.b, PJ[nn].b], [xt.b])
        dve(lambda e: e.tensor_tensor(out=x.t[0:n, :], in0=x.t[0:n, :], in1=xt.t[0:n, :], op=ALU.add), [x.b, xt.b], [x.b])
        if KDBG and n == 128 and not hasattr(finish, "done"):
            dbg("x3", x.t[:, :], 128, 1024, x.b)
            finish.done = True
        act(lambda e: e.activation(out=xsbf.t[0:n, :], in_=x.t[0:n, :], func=AF.Square, accum_out=ss.t[0:n, 0:1]),
            [x.b], [xsbf.b, ss.b])
        dve(lambda e: e.tensor_scalar(out=ss.t[0:n, 0:1], in0=ss.t[0:n, 0:1], scalar1=1.0 / 1024, scalar2=1e-6,
                                      op0=ALU.mult, op1=ALU.add), [ss.b], [ss.b])
        act(lambda e: e.activation(out=ss.t[0:n, 0:1], in_=ss.t[0:n, 0:1], func=AF.Sqrt), [ss.b], [ss.b])
        dve(lambda e: e.reciprocal(out=ss.t[0:n, 0:1], in_=ss.t[0:n, 0:1]), [ss.b], [ss.b])
        dve(lambda e: e.tensor_scalar(out=xt.t[0:n, :], in0=x.t[0:n, :], scalar1=ss.t[0:n, 0:1], scalar2=None, op0=ALU.mult),
            [x.b, ss.b], [xt.b])
        dve(lambda e: e.tensor_tensor(out=xt.t[0:n, :], in0=xt.t[0:n, :], in1=gf_rep.t[0:n, :], op=ALU.mult), [xt.b, gf_rep.b], [xt.b])
        fw.dma("sp", lambda e: e.dma_start(out=ydst, in_=xt.t[0:n, :]), [xt.b], [OUTB])

    def gates_and_ret_out(n):
        proj(hT, n, C_RG, 512, PJ[0])
        act(lambda e: e.activation(out=rT2.t[0:n, :], in_=PJ[0].t[0:n, :], func=AF.Silu), [PJ[0].b], [rT2.b])
        dve(lambda e: e.tensor_tensor(out=ymix.t[0:n, 0:512], in0=rT1.t[0:n, :], in1=rT2.t[0:n, :], op=ALU.mult),
            [rT1.b, rT2.b], [ymix.b])
        proj(hT, n, C_NGL, 24, PJ[0])
        act(lambda e: e.activation(out=sig.t[0:n, :], in_=PJ[0].t[0:n, 0:24], func=AF.Sigmoid), [PJ[0].b], [sig.b])
        proj(hT, n, C_NG, 512, PJ[0])
        act(lambda e: e.activation(out=sgn.t[0:n, :], in_=PJ[0].t[0:n, :], func=AF.Silu), [PJ[0].b], [sgn.b])

    def nsa_out(n):
        dve(lambda e: e.tensor_tensor(out=ymix.t[0:n, 512:1024], in0=ynsa.t[0:n, :, :, :].rearrange("p a b c -> p (a b c)"),
                                      in1=sgn.t[0:n, :], op=ALU.mult), [ynsa.b, sgn.b], [ymix.b])

    def top_sel(n, nblk, kth):
        dve(lambda e: e.max(out=m8.t[0:n, 0:8], in_=score.t[0:n, 0:nblk]), [score.b], [m8.b])
        dve(lambda e: e.match_replace(out=sc2.t[0:n, 0:nblk], in_to_replace=m8.t[0:n, 0:8], in_values=score.t[0:n, 0:nblk],
                                      imm_value=-1e30), [score.b, m8.b], [sc2.b])
        dve(lambda e: e.max(out=m8.t[0:n, 8:16], in_=sc2.t[0:n, 0:nblk]), [sc2.b], [m8.b])
        dve(lambda e: e.tensor_scalar(out=selb.t[0:n, 0:nblk], in0=score.t[0:n, 0:nblk], scalar1=m8.t[0:n, kth - 1:kth],
                                      scalar2=NEG, op0=ALU.is_lt, op1=ALU.mult), [score.b, m8.b], [selb.b])

    esp = contextlib.ExitStack()
    P_ = lambda name, shape, dt=F32, const=False: fw.sb(esp, name, shape, dt, const)
    KTs = P_("KTs", [128, SEQ], BF16); Vs = P_("Vs", [128, NB, 2, 65], BF16)
    KTw = P_("KTw", [128, 8 * 128], BF16); Vw = P_("Vw", [128, 8, 2, 65], BF16)
    KCT = P_("KCT", [128, 128], BF16); VC = P_("VC", [128, 2, 65], BF16)
    expand = P_("expand", [128, SEQ], BF16, True)
    TB = P_("TB", [128, 2, 128], BF16, True); WB = P_("WB", [128, 6, 128], BF16, True)
    CBp = P_("CBp", [128, 128], BF16); CBqp = P_("CBqp", [128, 128], BF16); FBp = P_("FBp", [128, NSB])
    mst = P_("mst", [128, 1024])
    for tt in (Vs, Vw, VC):
        fw.op("pool", lambda e: e.memset(tt.t[:], 1.0), [], [tt.b])
    fw.op("pool", lambda e: e.memset(Sst.t[:], 0.0), [], [Sst.b])
    fw.op("pool", lambda e: e.memset(KCT.t[:], 0.0), [], [KCT.b])
    fw.op("pool", lambda e: e.memset(KTs.t[:], 0.0), [], [KTs.b])
    fw.op("pool", lambda e: e.memset(KTw.t[:], 0.0), [], [KTw.b])
    fw.op("pool", lambda e: e.memset(expand.t[:], 0.0), [], [expand.b])
    fw.op("pool", lambda e: e.memset(selT.t[:], 0.0), [], [selT.b])
    for c0 in range(0, SEQ, 1024):
        w = min(1024, SEQ - c0)
        ld(mst.t[0:NSB, 0:w], expand_d[:, c0:c0 + w], mst.b)
        dve(lambda e: e.tensor_copy(out=expand.t[0:NSB, c0:c0 + w], in_=mst.t[0:NSB, 0:w]), [mst.b], [expand.b])
    ld(mst.t[:, 0:256], mTB.rearrange("p a b -> p (a b)"), mst.b)
    dve(lambda e: e.tensor_copy(out=TB.t[:].rearrange("p a b -> p (a b)"), in_=mst.t[:, 0:256]), [mst.b], [TB.b])
    ld(mst.t[:, 0:768], mWB.rearrange("p a b -> p (a b)"), mst.b)
    dve(lambda e: e.tensor_copy(out=WB.t[:].rearrange("p a b -> p (a b)"), in_=mst.t[:, 0:768]), [mst.b], [WB.b])

    def kv_block(t, p):
        ld(xt.t[:, :], xb[t * 128:(t + 1) * 128, :], xt.b)
        ld(ropet.t[:, :], rope_kv[t * 128:(t + 1) * 128, :], ropet.b)
        norm_T(xt, 128, gmix, hT)
        if SUB <= 1:
            return
        proj(hT, 128, C_RK, 512, PJ[0])
        rope(PJ[0].t[:, :], PJ[0].b, rO.t[:, :], rO.b, 128, 4, 64, 128)
        dve(lambda e: e.tensor_tensor(out=Kd.t[:, :, :], in0=rO.t[:, :].rearrange("p (h d) -> p h d", h=4),
                                      in1=bcast(kdec.t[:, 0:4], 2, 128), op=ALU.mult), [rO.b, kdec.b], [Kd.b])
        proj(hT, 128, C_RV, 512, PJ[1])
        act(lambda e: e.copy(out=Vbf.t[:, :, :].rearrange("p h d -> p (h d)"), in_=PJ[1].t[:, :]), [PJ[1].b], [Vbf.b])
        for h in range(4):
            mm(PM.t[:, h, :], Kd.t[:, h, :], Vbf.t[:, h, :], True, True, [Kd.b, Vbf.b], PM.b)
        if SUB <= 2:
            return
        if t == 2 * p:
            for h in range(4):
                dve(lambda e: e.tensor_scalar(out=Sown.t[:, h, :], in0=Sst.t[:, h, :], scalar1=hsel.t[:, h:h + 1], scalar2=None,
                                              op0=ALU.mult), [Sst.b, hsel.b], [Sown.b])
                dve(lambda e: e.scalar_tensor_tensor(out=Sown.t[:, h, :], in0=PM.t[:, h, :], scalar=hsel.t[:, 4 + h:5 + h],
                                                     in1=Sown.t[:, h, :], op0=ALU.mult, op1=ALU.add),
                    [PM.b, hsel.b, Sown.b], [Sown.b])
            act(lambda e: e.copy(out=Sownb.t[:, :, :], in_=Sown.t[:, :, :]), [Sown.b], [Sownb.b])
        for h in range(4):
            dve(lambda e: e.scalar_tensor_tensor(out=Sst.t[:, h, :], in0=Sst.t[:, h, :], scalar=gC[h], in1=PM.t[:, h, :],
                                                 op0=ALU.mult, op1=ALU.add), [Sst.b, PM.b], [Sst.b])
        if SUB <= 3:
            return
        proj(hT, 128, C_CK, 512, PJ[0])
        act(lambda e: e.copy(out=kvst.t[:, 0:2, :], in_=PJ[0].t[:, 0:256].rearrange("p (a b) -> p a b", a=2)), [PJ[0].b], [kvst.b])
        act(lambda e: e.copy(out=kvst.t[:, 3, :], in_=PJ[0].t[:, 384:512]), [PJ[0].b], [kvst.b])
        rope(PJ[0].t[:, 256:384], PJ[0].b, kvst.t[:, 2, :], kvst.b, 128, 2, 32, 256)
        proj(hT, 128, C_WK, 256, PJ[1])
        act(lambda e: e.copy(out=wst.t[:, 1, :], in_=PJ[1].t[:, 128:256]), [PJ[1].b], [wst.b])
        rope(PJ[1].t[:, 0:128], PJ[1].b, wst.t[:, 0, :], wst.b, 128, 2, 32, 256)
        if SUB <= 4:
            return
        fw.dma("sp", lambda e: e.dma_start(out=kvout[t * 128:(t + 1) * 128, :, :], in_=kvst.t[:, :, :]), [kvst.b], [OUTB])
        if t >= NB - 4:
            tw = t - (NB - 4)
            fw.dma("sp", lambda e: e.dma_start(out=winout[tw * 128:(tw + 1) * 128, :, :], in_=wst.t[:, :, :]), [wst.b], [OUTB])
        if SUB <= 5:
            return
        dve(lambda e: e.tensor_copy(out=tokbf.t[:, 0:384], in_=kvst.t[:, 0:3, :].rearrange("p a b -> p (a b)")), [kvst.b], [tokbf.b])
        dve(lambda e: e.tensor_copy(out=tokbf.t[:, 384:512], in_=wst.t[:, 0, :]), [wst.b], [tokbf.b])
        for i in range(4):
            tr(PT.t[:, i, :], tokbf.t[:, i * 128:(i + 1) * 128], identb.t[:, :], [tokbf.b, identb.b], PT.b)
        if SUB5 <= 1:
            return
        act(lambda e: e.copy(out=KTs.t[:, t * 128:(t + 1) * 128], in_=PT.t[:, 2, :]), [PT.b], [KTs.b])
        act(lambda e: e.copy(out=KTw.t[:, (t % 8) * 128:(t % 8 + 1) * 128], in_=PT.t[:, 3, :]), [PT.b], [KTw.b])
        if SUB5 <= 2:
            return
        for kv in range(2):
            dve(lambda e: e.tensor_tensor(out=blkT.t[:, kv, :].rearrange("p (c j) -> p c j", c=4),
                                          in0=PT.t[:, kv, :].rearrange("p (c j) -> p c j", c=4),
                                          in1=bcast(peT.t[:, kv, :], 1, 4), op=ALU.add), [PT.b, peT.b], [blkT.b])
        if SUB5 <= 3:
            return
        dve(lambda e: e.tensor_copy(out=Vs.t[:, t, :, 0:64], in_=kvst.t[:, 3, :].rearrange("p (k d) -> p k d", k=2)), [kvst.b], [Vs.b])
        dve(lambda e: e.tensor_copy(out=Vw.t[:, t % 8, :, 0:64], in_=wst.t[:, 1, :].rearrange("p (k d) -> p k d", k=2)), [wst.b], [Vw.b])
        if SUB <= 6:
            return
        for kv in range(2):
            bv = blkT.t[:, kv, :].rearrange("p (c j) -> p c j", c=4)
            for j in range(32):
                mm(PT32.t[:, kv * 4:kv * 4 + 4], W1[kv].t[:, j, :], bv[:, :, j], j == 0, j == 31, [W1[kv].b, blkT.b], PT32.b)
            gelu_to(PT32.t[:, kv * 4:kv * 4 + 4], PT32.b, hidb.t[:, kv, 0:4], hidb.b, 128, 4)
        if SUB <= 7:
            return
        mm(PT32.t[:, 8:12], W2[0].t[:, :], hidb.t[:, 0, 0:4], True, True, [W2[0].b, hidb.b], PT32.b)
        act(lambda e: e.copy(out=KCT.t[:, 4 * t:4 * t + 4], in_=PT32.t[:, 8:12]), [PT32.b], [KCT.b])
        mm(PT32.t[0:4, 16:144], hidb.t[:, 1, 0:4], W2[1].t[:, :], True, True, [W2[1].b, hidb.b], PT32.b)
        act(lambda e: e.copy(out=vcst.t[0:4, :], in_=PT32.t[0:4, 16:144]), [PT32.b], [vcst.b])
        if SUB <= 8:
            return
        fw.dma("sp", lambda e: e.dma_start(out=VC.t[4 * t:4 * t + 4, :, 0:64],
                                           in_=vcst.t[0:4, :].rearrange("p (k d) -> p k d", k=2)), [vcst.b], [VC.b])

    def q_block(p):
        n = 128
        prefetch_fin(n, pp_own[p * 128:(p + 1) * 128, :])
        ld(ropet.t[:, :], rope_own[p * 128:(p + 1) * 128, :], ropet.b)
        ld(mst.t[:, 0:128], mCB[p], mst.b)
        dve(lambda e: e.tensor_copy(out=CBp.t[:, :], in_=mst.t[:, 0:128]), [mst.b], [CBp.b])
        ld(mst.t[:, 128:256], mCBq[p], mst.b)
        dve(lambda e: e.tensor_copy(out=CBqp.t[:, :], in_=mst.t[:, 128:256]), [mst.b], [CBqp.b])
        ld(FBp.t[:, :], mFB[p], FBp.b)
        norm_T(xo, n, gmix, hT)
        proj(hT, n, C_RQ, 512, PJ[0])
        rope(PJ[0].t[:, :], PJ[0].b, tokbf.t[:, 0:512], tokbf.b, n, 4, 64, 0)
        proj(hT, n, C_RK, 512, PJ[1])
        rope(PJ[1].t[:, :], PJ[1].b, tokbf.t[:, 512:1024], tokbf.b, n, 4, 64, 128)
        proj(hT, n, C_RV, 512, PJ[0])
        act(lambda e: e.copy(out=vown.t[:, :, :].rearrange("p h d -> p (h d)"), in_=PJ[0].t[:, :]), [PJ[0].b], [vown.b])
        if SUBQ <= 1:
            return
        retention_q(n, dmaskT, qdecT, Sownb)
        if p == 0:
            dbg("retn", rT1.t[:, :], 128, 512, rT1.b)
        if SUBQ <= 2:
            return
        gates_and_ret_out(n)
        if p == 0:
            dbg("yret", ymix.t[:, 0:512], 128, 512, ymix.b)
            dbg("sig", sig.t[:, :], 128, 24, sig.b)
            dbg("sgn", sgn.t[:, :], 128, 512, sgn.b)
        proj(hT, n, C_NQ, 512, PJ[1])
        nq_prep(n, 256)
        if SUBQ <= 3:
            return
        for k in range(2):
            pj = PJ[0]
            for g in range(4):
                mm(pj.t[:, g * 128:g * 128 + NC], QTu.t[k * 64:(k + 1) * 64, g, :], KCT.t[k * 64:(k + 1) * 64, 0:NC], True, False,
                   [QTu.b, KCT.b], pj.b)
                mm(pj.t[:, g * 128:g * 128 + NC], identb.t[:, :], CBqp.t[:, 0:NC], False, True, [identb.b, CBqp.b], pj.b)
            for g in range(4):
                act(lambda e: e.activation(out=e2.t[:, g, 0:NC], in_=pj.t[:, g * 128:g * 128 + NC], func=AF.Exp, scale=0.125,
                                           accum_out=rsum.t[:, g:g + 1]), [pj.b], [e2.b, rsum.b])
            dve(lambda e: e.tensor_scalar(out=rsum.t[:, 4:8], in0=rsum.t[:, 0:4], scalar1=1e-30, scalar2=None, op0=ALU.add), [rsum.b], [rsum.b])
            dve(lambda e: e.reciprocal(out=rsum.t[:, 4:8], in_=rsum.t[:, 4:8]), [rsum.b], [rsum.b])
            dve(lambda e: e.tensor_scalar(out=imp.t[:, 0:NC], in0=e2.t[:, 0, 0:NC], scalar1=rsum.t[:, 4:5], scalar2=None, op0=ALU.mult),
                [e2.b, rsum.b], [imp.b])
            for g in range(1, 4):
                dve(lambda e: e.scalar_tensor_tensor(out=imp.t[:, 0:NC], in0=e2.t[:, g, 0:NC], scalar=rsum.t[:, 4 + g:5 + g],
                                                     in1=imp.t[:, 0:NC], op0=ALU.mult, op1=ALU.add), [e2.b, rsum.b, imp.b], [imp.b])
            iv = imp.t[:, 0:NC].rearrange("p (b two) -> p b two", two=2)
            dve(lambda e: e.tensor_tensor(out=score.t[:, 0:NSB], in0=iv[:, :, 0], in1=iv[:, :, 1], op=ALU.add), [imp.b], [score.b])
            dve(lambda e: e.tensor_tensor(out=score.t[:, 0:NSB], in0=score.t[:, 0:NSB], in1=FBp.t[:, :], op=ALU.add),
                [score.b, FBp.b], [score.b])
            top_sel(n, NSB, 16)
            tr(PT32.t[0:NSB, 256:384], selb.t[:, 0:NSB], ident.t[:, :], [selb.b, ident.b], PT32.b)
            act(lambda e: e.copy(out=selT.t[0:NSB, k, :], in_=PT32.t[0:NSB, 256:384]), [PT32.b], [selT.b])
        if SUBQ <= 4:
            return
        for k in range(2):
            ks = slice(k * 64, (k + 1) * 64)
            tiles = [(KCT.t[ks, 0:NC], VC.t[0:NC, k, :], NC, [KCT.b, VC.b],
                      [(identb.t[:, 0:NC], bcast(CBp.t[:, :], 1, 4), [identb.b, CBp.b])])]
            attend(n, k, QTu, tiles, 0, True)
            if p == 0:
                dbg("ycmp%d" % k, ynsa.t[:, k, :, :].rearrange("p a b -> p (a b)"), 128, 256, ynsa.b)
            tiles = []
            for kt in range(2 * p + 2):
                b_ = [(expand.t[:, kt * 128:(kt + 1) * 128], bcast(selT.t[:, k, :], 1, 4), [expand.b, selT.b])]
                if kt >= 2 * p:
                    b_.append((identb.t[:, :], bcast(TB.t[:, kt - 2 * p, :], 1, 4), [identb.b, TB.b]))
                tiles.append((KTs.t[ks, kt * 128:(kt + 1) * 128], Vs.t[:, kt, k, :], 128, [KTs.b, Vs.b], b_))
            attend(n, k, QTr, tiles, 1, False)
            if p == 0:
                dbg("yslc%d" % k, ynsa.t[:, k, :, :].rearrange("p a b -> p (a b)"), 128, 256, ynsa.b)
            tiles = []
            for i in range(6):
                kt = 2 * p - 4 + i
                if kt < 0:
                    continue
                r8 = kt % 8
                tiles.append((KTw.t[ks, r8 * 128:(r8 + 1) * 128], Vw.t[:, r8, k, :], 128, [KTw.b, Vw.b],
                              [(identb.t[:, :], bcast(WB.t[:, i, :], 1, 4), [identb.b, WB.b])]))
            attend(n, k, QTr, tiles, 2, False)
        if SUBQ <= 5:
            return
        if p == 0:
            dbg("ynsa", ynsa.t[:, :, :, :].rearrange("p a b c -> p (a b c)"), 128, 512, ynsa.b)
            dbg("selb", selb.t[:, 0:NSB], 128, NSB, selb.b)
            dbg("score", score.t[:, 0:NSB], 128, NSB, score.b)
        nsa_out(n)
        if p == 0:
            dbg("ymix", ymix.t[:, :], 128, 1024, ymix.b)
        finish(n, xo, pp_own[p * 128:(p + 1) * 128, :], y_own[p * 128:(p + 1) * 128, :])

    for p in range(NP):
        if STOP <= 0:
            break
        ld(xo.t[:, :], x_own[p * 128:(p + 1) * 128, :], xo.b)
        kv_block(2 * p, p)
        if STOP <= 1:
            break
        kv_block(2 * p + 1, p)
        if STOP <= 2:
            break
        q_block(p)
        if STOP <= 3:
            break
    for h in range(4):
        fw.dma("sp", lambda e: e.dma_start(out=ret_p[h], in_=Sst.t[:, h, :]), [Sst.b], [OUTB])
    fw.barrier()
    esp.close()

    ess = contextlib.ExitStack()

    class _V0:
        pass
    Q_ = lambda name, shape, dt=F32, const=False: fw.sb(ess, name, shape, dt, const)
    G = [Q_("G%d" % i, [NPG, 2048], BF16) for i in range(3)]
    idx = Q_("idx", [128, 1], I32); idx4 = Q_("idx4", [128, 8], I32)
    bT = Q_("bT", [128, 16, NPG], BF16)
    KCTs = Q_("KCTs", [128, 4, NPG], BF16); VCs = Q_("VCs", [NPG, 4, 2, 65], BF16)
    Vb = Q_("Vb", [NPG, 16, 2, 65], BF16); KTr = [Q_("KTr0", [128, 4, NPG], BF16), Q_("KTr1", [128, 4, NPG], BF16)]
    e2s = _V0(); e2s.t = e2.t[0:8, :, :].rearrange("p a b -> p (a b)")[:, 0:4 * NPG]; e2s.b = e2.b; imps = Q_("imps", [8, 4 * NPG]); scs = Q_("scs", [8, 2, 2 * NPG])
    selTs = Q_("selTs", [128, 2, 2, 8], BF16)
    WBs = Q_("WBs", [128, 8], BF16, True); NB8 = Q_("NB8", [128, 8], BF16, True)
    xs_t = _V0(); xs_t.t = xo.t[0:8, :]; xs_t.b = xo.b
    class _V:
        pass
    winK = _V(); winK.t = xt.t[:, 0:512].rearrange("p (t c) -> p t c", t=4); winK.b = xt.b
    winV = _V(); winV.t = xt.t[:, 512:1024].rearrange("p (t c) -> p t c", t=4); winV.b = xt.b
    winKb = Q_("winKb", [128, 4, 128], BF16); KTwin = Q_("KTwin", [128, 4, 128], BF16); Vwin = Q_("Vwin", [128, 4, 2, 65], BF16)
    newK = Q_("newK", [128, 2, 8], BF16); newV = Q_("newV", [8, 2, 2, 65], BF16)
    mst2 = Q_("mst2", [128, 8])
    Yn = Q_("Yn", [32, 128]); selG = Q_("selG", [32, 4, 8], F32, True)
    ld(selG.t[:], selG_d, selG.b)
    ld(mst2.t[:, :], mWBs, mst2.b)
    dve(lambda e: e.tensor_copy(out=WBs.t[:, :], in_=mst2.t[:, :]), [mst2.b], [WBs.b])
    fw.op("pool", lambda e: e.memset(NB8.t[:], 0.0), [], [NB8.b])
    fw.op("pool", lambda e: e.memset(selTs.t[:], 0.0), [], [selTs.b])
    ld(mst2.t[0:8, :], mNB8, mst2.b)
    dve(lambda e: e.tensor_copy(out=NB8.t[0:8, :], in_=mst2.t[0:8, :]), [mst2.b], [NB8.b])
    for tt in (VCs, Vb, Vwin, newV):
        fw.op("pool", lambda e: e.memset(tt.t[:], 1.0), [], [tt.b])
    gi = [0]
    rc = [0]

    def gather(cache, q4):
        g = G[gi[0] % 3]
        gi[0] += 1
        fw.dma("pool", lambda e: e.indirect_dma_start(out=g.t[:, :], out_offset=None, in_=cache,
                                                      in_offset=bass.IndirectOffsetOnAxis(ap=idx4.t[0:NPG, q4:q4 + 1], axis=0)),
               [idx4.b], [g.b])
        return g

    def sample_batch(b):
        n = 8
        ld(idx.t[0:NPG, :], ptab[b].rearrange("(p o) -> p o", o=1), idx.b)
        for q4 in range(8):
            dve(lambda e: e.tensor_scalar(out=idx4.t[0:NPG, q4:q4 + 1], in0=idx.t[0:NPG, 0:1], scalar1=8, scalar2=q4,
                                          op0=ALU.mult, op1=ALU.add), [idx.b], [idx4.b])
        ld(xs_t.t[:, :], xs[b * 8:(b + 1) * 8, :], xs_t.b)
        prefetch_fin(n, pps[b * 8:(b + 1) * 8, :])
        ld(ropet.t[0:8, :], rope_s[:, :], ropet.b)
        norm_T(xs_t, n, gmix, hT)
        proj(hT, n, C_CK, 512, PJ[0])
        act(lambda e: e.copy(out=kvst.t[0:n, 0:2, :], in_=PJ[0].t[0:n, 0:256].rearrange("p (a b) -> p a b", a=2)), [PJ[0].b], [kvst.b])
        act(lambda e: e.copy(out=kvst.t[0:n, 3, :], in_=PJ[0].t[0:n, 384:512]), [PJ[0].b], [kvst.b])
        rope(PJ[0].t[0:n, 256:384], PJ[0].b, kvst.t[0:n, 2, :], kvst.b, n, 2, 32, 256)
        proj(hT, n, C_WK, 256, PJ[1])
        act(lambda e: e.copy(out=wst.t[0:n, 1, :], in_=PJ[1].t[0:n, 128:256]), [PJ[1].b], [wst.b])
        rope(PJ[1].t[0:n, 0:128], PJ[1].b, wst.t[0:n, 0, :], wst.b, n, 2, 32, 256)
        fw.dma("sp", lambda e: e.dma_start(out=kv_s[b * 8:(b + 1) * 8, :, :], in_=kvst.t[0:n, :, :]), [kvst.b], [OUTB])
        fw.dma("sp", lambda e: e.dma_start(out=win_s[b, :, 504:512, :].rearrange("a t c -> t a c"), in_=wst.t[0:n, :, :]), [wst.b], [OUTB])
        fw.dma("sp", lambda e: e.dma_start(out=win_s[b, 0, 0:504, :], in_=win_k[b, 8:512, :]), [], [OUTB])
        fw.dma("sp", lambda e: e.dma_start(out=win_s[b, 1, 0:504, :], in_=win_v[b, 8:512, :]), [], [OUTB])
        dve(lambda e: e.tensor_copy(out=tokbf.t[0:n, 0:128], in_=kvst.t[0:n, 2, :]), [kvst.b], [tokbf.b])
        dve(lambda e: e.tensor_copy(out=tokbf.t[0:n, 128:256], in_=wst.t[0:n, 0, :]), [wst.b], [tokbf.b])
        for i in range(2):
            tr(PT.t[:, i, 0:n], tokbf.t[0:n, i * 128:(i + 1) * 128], identb.t[0:n, 0:n], [tokbf.b, identb.b], PT.b)
        act(lambda e: e.copy(out=newK.t[:, :, :], in_=PT.t[:, 0:2, 0:n]), [PT.b], [newK.b])
        dve(lambda e: e.tensor_copy(out=newV.t[0:n, 0, :, 0:64], in_=kvst.t[0:n, 3, :].rearrange("p (k d) -> p k d", k=2)), [kvst.b], [newV.b])
        dve(lambda e: e.tensor_copy(out=newV.t[0:n, 1, :, 0:64], in_=wst.t[0:n, 1, :].rearrange("p (k d) -> p k d", k=2)), [wst.b], [newV.b])
        ld(Sst.t[:, :, :], st_ret[b].rearrange("h d e -> d h e"), Sst.b)
        act(lambda e: e.copy(out=Sownb.t[:, :, :], in_=Sst.t[:, :, :]), [Sst.b], [Sownb.b])
        proj(hT, n, C_RQ, 512, PJ[0])
        rope(PJ[0].t[0:n, :], PJ[0].b, tokbf.t[0:n, 0:512], tokbf.b, n, 4, 64, 0)
        proj(hT, n, C_RK, 512, PJ[1])
        rope(PJ[1].t[0:n, :], PJ[1].b, rO.t[0:n, :], rO.b, n, 4, 64, 128)
        dve(lambda e: e.tensor_copy(out=tokbf.t[0:n, 512:1024], in_=rO.t[0:n, :]), [rO.b], [tokbf.b])
        dve(lambda e: e.tensor_tensor(out=Kd.t[0:n, :, :], in0=rO.t[0:n, :].rearrange("p (h d) -> p h d", h=4),
                                      in1=bcast(kdec8.t[0:n, 0:4], 2, 128), op=ALU.mult), [rO.b, kdec8.b], [Kd.b])
        proj(hT, n, C_RV, 512, PJ[0])
        act(lambda e: e.copy(out=vown.t[0:n, :, :].rearrange("p h d -> p (h d)"), in_=PJ[0].t[0:n, :]), [PJ[0].b], [vown.b])
        for h in range(4):
            mm(PM.t[:, h, :], Kd.t[0:n, h, :], vown.t[0:n, h, :], True, True, [Kd.b, vown.b], PM.b)
        for h in range(4):
            dve(lambda e: e.scalar_tensor_tensor(out=Sst.t[:, h, :], in0=Sst.t[:, h, :], scalar=gC8[h], in1=PM.t[:, h, :],
                                                 op0=ALU.mult, op1=ALU.add), [Sst.b, PM.b], [Sst.b])
        fw.dma("sp", lambda e: e.dma_start(out=ret_s[b].rearrange("h d e -> d h e"), in_=Sst.t[:, :, :]), [Sst.b], [OUTB])
        retention_q(n, dmask8T, qdec8T, Sownb)
        gates_and_ret_out(n)
        proj(hT, n, C_NQ, 512, PJ[1])
        nq_prep(n, 256)
        for kv in range(2):
            for q4 in range(4):
                for h2 in range(2):
                    g = gather(caches[kv], q4 * 2 + h2)
                    for j0 in range(0, 16, 4):
                        bank = (PT32, PJ[0])[rc[0] % 2]
                        rc[0] += 1
                        for jj in range(4):
                            j = j0 + jj
                            tr(bank.t[:, :].bitcast(BF16)[:, jj * 128:jj * 128 + NPG], g.t[:, j * 128:(j + 1) * 128], identb.t[0:NPG, 0:NPG],
                               [g.b, identb.b], bank.b)
                        jg0 = h2 * 16 + j0
                        dve(lambda e: e.tensor_tensor(out=bT.t[:, j0:j0 + 4, :],
                                                      in0=bank.t[:, :].bitcast(BF16)[:, 0:512].rearrange("p (a b) -> p a b", a=4)[:, :, 0:NPG],
                                                      in1=bcast(peT.t[:, kv, jg0:jg0 + 4], 2, NPG), op=ALU.add),
                            [bank.b, peT.b], [bT.b])
                    for j in range(16):
                        jg = h2 * 16 + j
                        mm(PS[0].t[:, 0:NPG], W1[kv].t[:, jg, :], bT.t[:, j, :], jg == 0, jg == 31, [W1[kv].b, bT.b], PS[0].b)
                gelu_to(PS[0].t[:, 0:NPG], PS[0].b, hidb.t[:, 0, 0:NPG], hidb.b, 128, NPG)
                if kv == 0:
                    mm(PS[1].t[:, 0:NPG], W2[0].t[:, :], hidb.t[:, 0, 0:NPG], True, True, [W2[0].b, hidb.b], PS[1].b)
                    act(lambda e: e.copy(out=KCTs.t[:, q4, :], in_=PS[1].t[:, 0:NPG]), [PS[1].b], [KCTs.b])
                else:
                    mm(PS[1].t[0:NPG, 0:128], hidb.t[:, 0, 0:NPG], W2[1].t[:, :], True, True, [W2[1].b, hidb.b], PS[1].b)
                    act(lambda e: e.copy(out=VCs.t[:, q4, :, 0:64], in_=PS[1].t[0:NPG, 0:128].rearrange("p (k d) -> p k d", k=2)),
                        [PS[1].b], [VCs.b])
        for k in range(2):
            ks = slice(k * 64, (k + 1) * 64)
            for g in range(4):
                pj = PJ[g % 2]
                mm(pj.t[0:n, 0:4 * NPG].rearrange("p (a b) -> p a b", a=4), QTu.t[ks, g, 0:n], KCTs.t[ks, :, :], True, True, [QTu.b, KCTs.b], pj.b)
                act(lambda e: e.activation(out=e2s.t[:, :], in_=pj.t[0:n, 0:4 * NPG], func=AF.Exp, scale=0.125,
                                           accum_out=rsum.t[0:n, g:g + 1]), [pj.b], [e2s.b, rsum.b])
                dve(lambda e: e.reciprocal(out=rsum.t[0:n, 4 + g:5 + g], in_=rsum.t[0:n, g:g + 1]), [rsum.b], [rsum.b])
                if g == 0:
                    dve(lambda e: e.tensor_scalar(out=imps.t[:, :], in0=e2s.t[:, :], scalar1=rsum.t[0:n, 4:5], scalar2=None, op0=ALU.mult),
                        [e2s.b, rsum.b], [imps.b])
                else:
                    dve(lambda e: e.scalar_tensor_tensor(out=imps.t[:, :], in0=e2s.t[:, :], scalar=rsum.t[0:n, 4 + g:5 + g],
                                                         in1=imps.t[:, :], op0=ALU.mult, op1=ALU.add), [e2s.b, rsum.b, imps.b], [imps.b])
            iv = imps.t[:, :].rearrange("p (hf two g) -> p hf two g", hf=2, two=2)
            dve(lambda e: e.tensor_tensor(out=scs.t[:, k, :].rearrange("p (hf g) -> p hf g", hf=2), in0=iv[:, :, 0, :], in1=iv[:, :, 1, :],
                                          op=ALU.add), [imps.b], [scs.b])
            dve(lambda e: e.tensor_scalar(out=scs.t[:, k, 0:1], in0=scs.t[:, k, 0:1], scalar1=1e4, scalar2=None, op0=ALU.add),
                [scs.b], [scs.b])
            dve(lambda e: e.tensor_copy(out=score.t[0:n, 0:2 * NPG], in_=scs.t[:, k, :]), [scs.b], [score.b])
            top_sel(n, 2 * NPG, 15)
            dve(lambda e: e.tensor_scalar(out=selb.t[0:n, 0:2 * NPG], in0=selb.t[0:n, 0:2 * NPG], scalar1=0.0, scalar2=None,
                                          op0=ALU.is_equal), [selb.b], [selb.b])
            for hf in range(2):
                tr(PT32.t[0:NPG, (hf * 2 + k) * 8:(hf * 2 + k) * 8 + 8], selb.t[0:n, hf * NPG:(hf + 1) * NPG], ident.t[0:n, 0:n],
                   [selb.b, ident.b], PT32.b)
        act(lambda e: e.copy(out=selTs.t[0:NPG, :, :, :].rearrange("p a b c -> p (a b c)"), in_=PT32.t[0:NPG, 0:32]), [PT32.b], [selTs.b])
        ld(winK.t[:, :, :], win_k[b].rearrange("(t p) c -> p t c", p=128), winK.b)
        ld(winV.t[:, :, :], win_v[b].rearrange("(t p) c -> p t c", p=128), winV.b)
        dve(lambda e: e.tensor_copy(out=winKb.t[:, :, :], in_=winK.t[:, :, :]), [winK.b], [winKb.b])
        for i in range(4):
            tr(PT.t[:, i, :], winKb.t[:, i, :], identb.t[:, :], [winKb.b, identb.b], PT.b)
        act(lambda e: e.copy(out=KTwin.t[:, :, :], in_=PT.t[:, 0:4, :]), [PT.b], [KTwin.b])
        dve(lambda e: e.tensor_copy(out=Vwin.t[:, :, :, 0:64], in_=winV.t[:, :, :].rearrange("p t (k d) -> p t k d", k=2)), [winV.b], [Vwin.b])
        for k in range(2):
            ks = slice(k * 64, (k + 1) * 64)
            tiles = [(KCTs.t[ks, q4, :], VCs.t[:, q4, k, :], NPG, [KCTs.b, VCs.b], []) for q4 in range(4)]
            attend(n, k, QTu, tiles, 0, True)
            tiles = [(KTwin.t[ks, i, :], Vwin.t[:, i, k, :], 128, [KTwin.b, Vwin.b],
                      ([(identb.t[:, :], bcast(WBs.t[:, :], 1, 4), [identb.b, WBs.b])] if i == 0 else [])) for i in range(4)]
            tiles.append((newK.t[ks, 1, :], newV.t[0:8, 1, k, :], 8, [newK.b, newV.b],
                          [(identb.t[:, 0:8], bcast(NB8.t[:, :], 1, 4), [identb.b, NB8.b])]))
            attend(n, k, QTr, tiles, 2, False)
        slc_rows(b, n)

    def slc_rows(b, n):
        mm(PO.t[0:32, 0:130], zerob.t[0:1, 0:32], zerob.t[0:1, 0:130], True, False, [zerob.b], PO.b)
        rounds = [(o, r0) for o in range(8) for r0 in range(0, 16, 4)]
        gks = {}

        def T_(ri):
            o, r0 = rounds[ri]
            if r0 == 0:
                gks[o] = gather(caches[2], o)
                gv = gather(caches[3], o)
                dve(lambda e: e.tensor_copy(out=Vb.t[:, :, :, 0:64], in_=gv.t[:, :].rearrange("p (r k d) -> p r k d", r=16, k=2)),
                    [gv.b], [Vb.b])
            gk = gks[o]
            bank = (PT32, PJ[1])[ri % 2]
            ktr = KTr[ri % 2]
            for rr in range(4):
                r = r0 + rr
                tr(bank.t[:, :].bitcast(BF16)[:, rr * 128:rr * 128 + NPG], gk.t[:, r * 128:(r + 1) * 128], identb.t[0:NPG, 0:NPG], [gk.b, identb.b], bank.b)
            act(lambda e: e.copy(out=ktr.t[:, 0:4, :], in_=bank.t[:, :].bitcast(BF16)[:, 0:512].rearrange("p (a b) -> p a b", a=4)[:, :, 0:NPG]),
                [bank.b], [ktr.b])

        def S_(ri):
            o, r0 = rounds[ri]
            hf = o // 4
            ktr = KTr[ri % 2]
            for k in range(2):
                ks = slice(k * 64, (k + 1) * 64)
                for rr in range(4):
                    outv = PS[k].t[0:NPG, rr * 32:rr * 32 + 32].rearrange("p (g q) -> p g q", g=4)
                    mm(outv, ktr.t[ks, rr, :], QTr.t[ks, :, 0:n], True, True, [ktr.b, QTr.b], PS[k].b)
                act(lambda e: e.activation(out=ET[k].t[0:NPG, 0:128], in_=PS[k].t[0:NPG, 0:128], func=AF.Exp, scale=0.125),
                    [PS[k].b], [ET[k].b])
                ev = ET[k].t[0:NPG, 0:128].rearrange("p (r g q) -> p r g q", r=4, g=4)
                dve(lambda e: e.tensor_tensor(out=ev, in0=ev, in1=bcast(bcast(selTs.t[0:NPG, hf, k, :], 1, 4), 1, 4), op=ALU.mult),
                    [ET[k].b, selTs.b], [ET[k].b])

        def PV_(ri):
            o, r0 = rounds[ri]
            for rr in range(4):
                r = r0 + rr
                for k in range(2):
                    mm(PO.t[0:32, k * 65:(k + 1) * 65], ET[k].t[0:NPG, rr * 32:(rr + 1) * 32], Vb.t[:, r, k, :], False, False,
                       [ET[k].b, Vb.b], PO.b)
        T_(0)
        for ri in range(len(rounds)):
            S_(ri)
            nxt_same = ri + 1 < len(rounds) and rounds[ri + 1][1] != 0
            if nxt_same:
                T_(ri + 1)
            PV_(ri)
            if ri + 1 < len(rounds) and not nxt_same:
                T_(ri + 1)
        for k in range(2):
            ks = slice(k * 64, (k + 1) * 64)
            outv = PS[k].t[0:8, 0:32].rearrange("p (g q) -> p g q", g=4)
            mm(outv, newK.t[ks, 0, :], QTr.t[ks, :, 0:n], True, False, [newK.b, QTr.b], PS[k].b)
            mm(outv, identb.t[:, 0:8], bcast(NB8.t[:, :], 1, 4), False, True, [identb.b, NB8.b], PS[k].b)
            act(lambda e: e.activation(out=ET[k].t[0:8, 0:32], in_=PS[k].t[0:8, 0:32], func=AF.Exp, scale=0.125), [PS[k].b], [ET[k].b])
            mm(PO.t[0:32, k * 65:(k + 1) * 65], ET[k].t[0:8, 0:32], newV.t[0:8, 0, k, :], False, True, [ET[k].b, newV.b], PO.b)
        pov = PO.t[0:32, 0:130].rearrange("p (k c) -> p k c", k=2)
        dve(lambda e: e.tensor_scalar(out=fac.t[0:32, 0:2], in0=pov[:, :, 64], scalar1=1e-30, scalar2=None, op0=ALU.add), [PO.b], [fac.b])
        dve(lambda e: e.reciprocal(out=fac.t[0:32, 0:2], in_=fac.t[0:32, 0:2]), [fac.b], [fac.b])
        dve(lambda e: e.tensor_tensor(out=Yn.t[:, :].rearrange("p (k d) -> p k d", k=2), in0=pov[:, :, 0:64],
                                      in1=bcast(fac.t[0:32, 0:2], 2, 64), op=ALU.mult), [PO.b, fac.b], [Yn.b])
        for g in range(4):
            mm(PJ[0].t[0:8, g * 128:(g + 1) * 128], selG.t[:, g, :], Yn.t[:, :], True, True, [selG.b, Yn.b], PJ[0].b)
        pjv = PJ[0].t[0:8, :].rearrange("p (g k d) -> p g k d", g=4, k=2)
        for k in range(2):
            dve(lambda e: e.tensor_tensor(out=ytmp.t[0:n, :, :], in0=pjv[:, :, k, :], in1=bcast(sig.t[0:n, 8 + k * 4:8 + k * 4 + 4], 2, 64),
                                          op=ALU.mult), [PJ[0].b, sig.b], [ytmp.b])
            dve(lambda e: e.tensor_tensor(out=ynsa.t[0:n, k, :, :], in0=ynsa.t[0:n, k, :, :], in1=ytmp.t[0:n, :, :],
                                          op=ALU.add), [ynsa.b, ytmp.b], [ynsa.b])
        nsa_out(n)
        finish(n, xs_t, pps[b * 8:(b + 1) * 8, :], y_s[b * 8:(b + 1) * 8, :])

    for b in range(SB_PER_CORE):
        if STOP <= 4:
            break
        sample_batch(b)
        if STOP <= 5:
            break
    fw.barrier()
    ess.close()
    es0.close()
    return nc


def _consts(SEQ, NPG, h):
    NB = SEQ // 128; NP = NB // 2; NSB = NB * 2
    c = {}

    def rope_tab(pos):
        pos = np.asarray(pos, np.float32)
        out = np.zeros((len(pos), 320), np.float32)
        inv64 = (10000.0 ** (-np.arange(64, dtype=np.float32) / 64)).astype(np.float32)
        inv32 = (10000.0 ** (-np.arange(32, dtype=np.float32) / 32)).astype(np.float32)
        a64 = pos[:, None] * inv64[None, :]
        a32 = pos[:, None] * inv32[None, :]
        out[:, 0:64] = np.cos(a64); out[:, 64:128] = np.sin(a64)
        out[:, 128:192] = np.cos(a64) * np.float32(128 ** -0.5); out[:, 192:256] = np.sin(a64) * np.float32(128 ** -0.5)
        out[:, 256:288] = np.cos(a32); out[:, 288:320] = np.sin(a32)
        return out
    c["rope_kv"] = rope_tab(np.arange(SEQ))
    own_pos = np.concatenate([np.arange(128) + (2 * p + h) * 128 for p in range(NP)])
    c["rope_own"] = rope_tab(own_pos)
    c["rope_s"] = rope_tab(NPG * 128 + np.arange(8))
    ki = np.arange(128)[:, None]; qi = np.arange(128)[None, :]
    tri = np.where(ki <= qi, 0.0, NEG).astype(np.float32)
    tri2 = np.where(ki >= qi, 0.0, NEG).astype(np.float32)
    Z = np.zeros((128, 128), np.float32); M = np.full((128, 128), NEG, np.float32)
    TB = [tri, M] if h == 0 else [Z, tri]
    WB = [tri2, Z, Z, Z, tri, M] if h == 0 else [M, tri2, Z, Z, Z, tri]
    c["mTB"] = np.stack(TB, 1); c["mWB"] = np.stack(WB, 1)
    CB = np.zeros((NP, 128, 128), np.float32); FB = np.zeros((NP, 128, NSB), np.float32)
    cc = np.arange(128)[:, None]
    for p in range(NP):
        j = 2 * p + h
        qpos = j * 128 + np.arange(128)[None, :]
        CB[p] = np.where(32 * cc + 31 <= qpos, 0.0, NEG)
        FB[p, :, 0] += 1e4
        cur = (j * 128 + np.arange(128)) // 64
        FB[p, np.arange(128), cur] += 1e4
    c["mCB"] = CB; c["mCBq"] = np.ascontiguousarray(CB.transpose(0, 2, 1)); c["mFB"] = FB
    g = (1.0 - 2.0 ** (-5.0 - np.arange(4, dtype=np.float64)))
    hs = np.zeros((128, 8), np.float32)
    if h == 0:
        hs[:, 0:4] = 1.0
    else:
        hs[:, 0:4] = (g ** 128)[None, :]; hs[:, 4:8] = 1.0
    c["hsel"] = hs

    def decs(C):
        i = np.arange(C)
        dm = np.zeros((C, 4, C)); qd = np.zeros((128, 4, C)); kd = np.zeros((C, 4))
        for hh in range(4):
            d = i[None, :] - i[:, None]
            dm[:, hh, :] = np.where(d >= 0, g[hh] ** np.maximum(d, 0), 0.0)
            qd[:, hh, :] = (g[hh] ** (i + 1.0))[None, :]
            kd[:, hh] = g[hh] ** (C - 1.0 - i)
        return dm.astype(np.float32), qd.astype(np.float32), kd.astype(np.float32)
    c["dmaskT"], c["qdecT"], c["kdec"] = decs(128)
    c["dmask8T"], c["qdec8T"], c["kdec8"] = decs(8)
    c["expand"] = (np.arange(SEQ)[None, :] // 64 == np.arange(NSB)[:, None]).astype(np.float32)
    c["mWBs"] = np.where(np.arange(128)[:, None] >= np.arange(8)[None, :], 0.0, NEG).astype(np.float32)
    c["selG"] = np.ascontiguousarray((np.arange(32)[:, None, None] == (np.arange(4)[None, :, None] * 8 + np.arange(8)[None, None, :])).astype(np.float32))
    c["mNB8"] = np.where(np.arange(8)[:, None] <= np.arange(8)[None, :], 0.0, NEG).astype(np.float32)
    return c


_NC_CACHE = {}


def run(inputs, SEQ, NPG, NPOOL):
    f = lambda a: np.ascontiguousarray(np.asarray(a))
    NB = SEQ // 128; NP = NB // 2
    key = (SEQ, NPG, NPOOL)
    if key not in _NC_CACHE:
        import os
        _NC_CACHE[key] = build(SEQ, NPG, NPOOL, int(os.environ.get('KSTOP', '99')))
    nc = _NC_CACHE[key]
    shared = {
        "w_in": f(inputs["w_in"][0]), "w_out": f(inputs["w_out"][0]), "w_gate": f(inputs["w_ple_gate"][0]),
        "w_ple": f(inputs["w_ple"][0]), "norm_mix": f(inputs["norm_mix"][0]), "norm_ple": f(inputs["norm_ple"][0]),
        "norm_f": f(inputs["norm_f"]), "gn_g": f(inputs["ret_gn_g"][0]), "gn_b": f(inputs["ret_gn_b"][0]),
        "pe_k": f(inputs["cmp_pe_k"][0]).reshape(32, 128), "pe_v": f(inputs["cmp_pe_v"][0]).reshape(32, 128),
        "w1_k": f(inputs["cmp_w1_k"][0]), "w1_v": f(inputs["cmp_w1_v"][0]),
        "w2_k": f(inputs["cmp_w2_k"][0]), "w2_v": f(inputs["cmp_w2_v"][0]),
        "c_ck": f(inputs["cache_cmp_k"][0]).reshape(NPOOL * 8, 2048), "c_cv": f(inputs["cache_cmp_v"][0]).reshape(NPOOL * 8, 2048),
        "c_sk": f(inputs["cache_slc_k"][0]).reshape(NPOOL * 8, 2048), "c_sv": f(inputs["cache_slc_v"][0]).reshape(NPOOL * 8, 2048),
    }
    consts = [_consts(SEQ, NPG, 0), _consts(SEQ, NPG, 1)]
    xp = np.asarray(inputs["x_prompt"]); pp = np.asarray(inputs["p_prompt"])[0]
    in_maps = []
    for core in range(8):
        b, h = core // 2, core % 2
        m = dict(shared)
        m.update(consts[h])
        m["xb"] = f(xp[b])
        m["x_own"] = f(xp[b].reshape(NP, 2, 128, 1024)[:, h].reshape(NP * 128, 1024))
        m["pp_own"] = f(pp[b].reshape(NP, 2, 128, 256)[:, h].reshape(NP * 128, 256))
        sb = slice(core * 4, core * 4 + 4)
        m["xs"] = f(np.asarray(inputs["x_sample"])[sb].reshape(32, 1024))
        m["pps"] = f(np.asarray(inputs["p_sample"])[0, sb].reshape(32, 256))
        m["win_k"] = f(np.asarray(inputs["state_win_k"])[0, sb].reshape(4, 512, 128))
        m["win_v"] = f(np.asarray(inputs["state_win_v"])[0, sb].reshape(4, 512, 128))
        m["st_ret"] = f(np.asarray(inputs["state_ret"])[0, sb])
        m["ptab"] = f(np.asarray(inputs["page_table"])[sb].astype(np.int32))
        in_maps.append(m)
    import os
    if os.environ.get("KTRACE"):
        rr_ = run_bass_kernel_spmd(nc, in_maps, core_ids=list(range(8)), trace=True)
        print("EXEC_TIME_NS", rr_.exec_time_ns)
        res = rr_.results
    else:
        res = run_bass_kernel_spmd(nc, in_maps, core_ids=list(range(8))).results
    global LAST_RES
    LAST_RES = res
    B = 4
    y_prompt = np.zeros((B, SEQ, 1024), np.float32)
    ret_p = np.zeros((1, B, 4, 128, 128), np.float32)
    kvp = np.zeros((B, SEQ, 4, 128), np.float32)
    winp = np.zeros((B, 512, 2, 128), np.float32)
    y_s = np.zeros((32, 8, 1024), np.float32); ret_s = np.zeros((1, 32, 4, 128, 128), np.float32)
    kvs = np.zeros((32, 8, 4, 128), np.float32); wins = np.zeros((32, 2, 512, 128), np.float32)
    for core in range(8):
        b, h = core // 2, core % 2
        r = res[core]
        y_prompt[b].reshape(NP, 2, 128, 1024)[:, h] = r["y_own"].reshape(NP, 128, 1024)
        if h == 0:
            ret_p[0, b] = r["ret_p"]; kvp[b] = r["kvout"]; winp[b] = r["winout"]
        sb = slice(core * 4, core * 4 + 4)
        y_s[sb] = r["y_s"].reshape(4, 8, 1024); ret_s[0, sb] = r["ret_s"]
        kvs[sb] = r["kv_s"].reshape(4, 8, 4, 128); wins[sb] = r["win_s"]
    sh = lambda a: np.ascontiguousarray(a)[None].reshape((1,) + a.shape[:-1] + (2, 64))
    return (y_prompt, y_s, ret_p,
            sh(kvp[:, :, 0]), sh(kvp[:, :, 1]), sh(kvp[:, :, 2]), sh(kvp[:, :, 3]),
            sh(winp[:, :, 0]), sh(winp[:, :, 1]),
            ret_s, sh(kvs[:, :, 0]), sh(kvs[:, :, 1]), sh(kvs[:, :, 2]), sh(kvs[:, :, 3]),
            sh(wins[:, 0]), sh(wins[:, 1]))


def kernel(**inputs):
    SEQ = inputs["x_prompt"].shape[1]
    NPG = inputs["page_table"].shape[1]
    NPOOL = inputs["cache_cmp_k"].shape[1]
    return run(inputs, SEQ, NPG, NPOOL)
```

```python
import contextlib
import numpy as np
import concourse.bass as bass
import concourse.mybir as mybir
from concourse.bass_utils import run_bass_kernel_spmd

F32 = mybir.dt.float32
BF16 = mybir.dt.bfloat16
I32 = mybir.dt.int32
AF = mybir.ActivationFunctionType
ALU = mybir.AluOpType
AX = mybir.AxisListType

NEG = -30000.0
DEC_T = 8
SB_PER_CORE = 4
C_RQ, C_RK, C_RV, C_RG, C_NQ, C_CK, C_CV, C_SK, C_SV, C_WK, C_WV, C_NGL, C_NG = (
    0, 512, 1024, 1536, 2048, 2560, 2688, 2816, 2944, 3072, 3200, 3328, 3352)
PROJ = 3864


class Buf:
    __slots__ = ("w", "r", "const", "psum")

    def __init__(self, const=False):
        self.w = None
        self.r = []
        self.const = const
        self.psum = False


class T:
    def __init__(self, t, const=False):
        self.t = t
        self.b = Buf(const)


def bcast(ap, pos, n):
    l = [list(x) for x in ap.ap]
    l.insert(pos, [0, n])
    return bass.AP(tensor=ap.tensor, offset=ap.offset, ap=l)


import os as _os
NOSAME = bool(int(_os.environ.get('KNOSAME', '0')))


class FW:
    NDMA = 24

    def __init__(self, nc, es):
        self.nc = nc
        self.eng = {"pe": nc.tensor, "act": nc.scalar, "dve": nc.vector,
                    "pool": nc.gpsimd, "sp": nc.sync}
        self.sem = {k: es.enter_context(nc.semaphore("sem_" + k)) for k in self.eng}
        self.cnt = {k: 0 for k in self.eng}
        self.dsem = [es.enter_context(nc.semaphore("dsem%d" % i)) for i in range(self.NDMA)]
        self.dcnt = [0] * self.NDMA
        self.dnext = 0
        self.waited = {k: {} for k in self.eng}

    def sb(self, es, name, shape, dt, const=False):
        return T(es.enter_context(self.nc.sbuf_tensor("s_" + name, list(shape), dt)), const)

    def ps(self, es, name, shape, dt):
        t = T(es.enter_context(self.nc.psum_tensor("p_" + name, list(shape), dt)))
        t.b.psum = True
        return t

    def _wait(self, e, dep):
        sem, val = dep
        w = self.waited[e]
        key = id(sem)
        if w.get(key, 0) >= val:
            return
        self.eng[e].wait_ge(sem, val)
        w[key] = val

    def _deps(self, e, reads, writes):
        mysem = self.sem[e]
        nosame = NOSAME
        for b in reads:
            if b.w is not None and not ((e == "pe" or nosame) and b.w[0] is mysem):
                self._wait(e, b.w)
            if b.psum:
                for d in b.r:
                    if d[0] is not mysem:
                        self._wait(e, d)
        for b in writes:
            if b.w is not None and not ((e == "pe" or nosame) and b.w[0] is mysem):
                self._wait(e, b.w)
            for d in b.r:
                if d[0] is not mysem:
                    self._wait(e, d)

    def _rec(self, tok, reads, writes):
        for b in reads:
            if not b.const:
                b.r.append(tok)
                if len(b.r) > 64:
                    last = {}
                    for d in b.r:
                        last[id(d[0])] = d
                    b.r = list(last.values())
        for b in writes:
            b.w = tok
            b.r = []

    def op(self, e, fn, reads=(), writes=()):
        self._deps(e, reads, writes)
        ins = fn(self.eng[e])
        self.cnt[e] += 1
        ins.then_inc(self.sem[e], 1)
        self._rec((self.sem[e], self.cnt[e]), reads, writes)

    def dma(self, q, fn, reads=(), writes=()):
        i = self.dnext
        self.dnext = (self.dnext + 1) % self.NDMA
        sem = self.dsem[i]
        if self.dcnt[i] > 0:
            self._wait(q, (sem, self.dcnt[i]))
        self._deps(q, reads, writes)
        ins = fn(self.eng[q])
        self.dcnt[i] += 16
        ins.then_inc(sem, 16)
        self._rec((sem, self.dcnt[i]), reads, writes)

    def barrier(self):
        for e in self.eng:
            for p in self.eng:
                if p != e and self.cnt[p] > 0:
                    self._wait(e, (self.sem[p], self.cnt[p]))
            for i in range(self.NDMA):
                if self.dcnt[i] > 0:
                    self._wait(e, (self.dsem[i], self.dcnt[i]))


def build(SEQ, NPG, NPOOL, STOP=99):
    import os
    SUB = int(os.environ.get('KSUB', '99'))
    SUB5 = int(os.environ.get('KSUB5', '99'))
    SUBQ = int(os.environ.get('KSUBQ', '99'))
    KDBG = int(os.environ.get('KDBG', '0'))
    NB = SEQ // 128
    NP = NB // 2
    NPo = NP * 128
    NC = NB * 4
    NSB = NB * 2
    assert NC <= 128 and NSB > 16 and 2 * NPG >= 16
    nc = bass.Bass("TRN2", target_bir_lowering=False)
    es0 = contextlib.ExitStack()
    fw = FW(nc, es0)

    def din(name, shape, dt=F32):
        return nc.dram_tensor(name, list(shape), dt, kind="ExternalInput").ap()

    def dout(name, shape, dt=F32):
        return nc.dram_tensor(name, list(shape), dt, kind="ExternalOutput").ap()

    xb = din("xb", [SEQ, 1024]); x_own = din("x_own", [NPo, 1024]); pp_own = din("pp_own", [NPo, 256])
    rope_kv = din("rope_kv", [SEQ, 320]); rope_own = din("rope_own", [NPo, 320]); rope_s = din("rope_s", [8, 320])
    xs = din("xs", [32, 1024]); pps = din("pps", [32, 256])
    caches = [din(n, [NPOOL * 8, 2048]) for n in ("c_ck", "c_cv", "c_sk", "c_sv")]
    win_k = din("win_k", [4, 512, 128]); win_v = din("win_v", [4, 512, 128])
    st_ret = din("st_ret", [4, 4, 128, 128]); ptab = din("ptab", [4, NPG], I32)
    w_in = din("w_in", [1024, PROJ]); w_out = din("w_out", [1024, 1024]); w_gate = din("w_gate", [1024, 1024])
    w_ple = din("w_ple", [256, 1024])
    norm_mix = din("norm_mix", [1024]); norm_ple = din("norm_ple", [1024]); norm_f = din("norm_f", [1024])
    gn_g = din("gn_g", [512]); gn_b = din("gn_b", [512])
    pe_kv = [din("pe_k", [32, 128]), din("pe_v", [32, 128])]
    w1_kv = [din("w1_k", [2, 32, 64, 64]), din("w1_v", [2, 32, 64, 64])]
    w2_kv = [din("w2_k", [2, 64, 64]), din("w2_v", [2, 64, 64])]
    mTB = din("mTB", [128, 2, 128]); mWB = din("mWB", [128, 6, 128])
    mCB = din("mCB", [NP, 128, 128]); mCBq = din("mCBq", [NP, 128, 128]); mFB = din("mFB", [NP, 128, NSB])
    hsel_d = din("hsel", [128, 8])
    dmaskT_d = din("dmaskT", [128, 4, 128]); qdecT_d = din("qdecT", [128, 4, 128]); kdec_d = din("kdec", [128, 4])
    dmask8T_d = din("dmask8T", [8, 4, 8]); qdec8T_d = din("qdec8T", [128, 4, 8]); kdec8_d = din("kdec8", [8, 4])
    selG_d = din("selG", [32, 4, 8]); expand_d = din("expand", [NSB, SEQ]); mWBs = din("mWBs", [128, 8]); mNB8 = din("mNB8", [8, 8])
    gC = [float((1.0 - 2.0 ** (-5.0 - h)) ** 128) for h in range(4)]
    gC8 = [float((1.0 - 2.0 ** (-5.0 - h)) ** 8) for h in range(4)]

    y_own = dout("y_own", [NPo, 1024]); ret_p = dout("ret_p", [4, 128, 128])
    kvout = dout("kvout", [SEQ, 4, 128]); winout = dout("winout", [512, 2, 128])
    y_s = dout("y_s", [32, 1024]); ret_s = dout("ret_s", [4, 4, 128, 128])
    kv_s = dout("kv_s", [32, 4, 128]); win_s = dout("win_s", [4, 2, 512, 128])
    wscr = nc.dram_tensor("wscr", [2, 1024, 1024], BF16, kind="Internal").ap()
    OUTB = Buf()
    SCRB = Buf()

    dbgst = [None]

    def dbg(name, src, n, cols, rb):
        if not KDBG:
            return
        d = dout("dbg_" + name, [n, cols])
        st = dbgst[0]
        fw.op("dve", lambda e: e.tensor_copy(out=st.t[0:n, 0:cols], in_=src), [rb], [st.b])
        fw.dma("sp", lambda e: e.dma_start(out=d, in_=st.t[0:n, 0:cols]), [st.b], [OUTB])

    def mm(out, lhsT, rhs, start, stop, reads, wb):
        fw.op("pe", lambda e: e.matmul(out, lhsT=lhsT, rhs=rhs, start=start, stop=stop,
                                       skip_group_check=True), reads, [wb])

    def tr(out, in_, ident, reads, wb):
        fw.op("pe", lambda e: e.transpose(out=out, in_=in_, identity=ident), reads, [wb])

    def dve(fn, reads, writes):
        fw.op("dve", fn, reads, writes)

    def act(fn, reads, writes):
        fw.op("act", fn, reads, writes)

    def ld(out, in_, wb, q="sp", reads=()):
        fw.dma(q, lambda e: e.dma_start(out=out, in_=in_), reads, [wb])

    S = lambda name, shape, dt=F32, const=False: fw.sb(es0, name, shape, dt, const)
    Wi = S("Wi", [128, 8, PROJ], BF16, True)
    Wp = S("Wp", [128, 2, 1024], BF16, True)
    W1 = [S("W1k", [128, 32, 128], BF16, True), S("W1v", [128, 32, 128], BF16, True)]
    W2 = [S("W2k", [128, 128], BF16, True), S("W2v", [128, 128], BF16, True)]
    peT = S("peT", [128, 2, 32], F32, True)
    gmix = S("gmix", [128, 8], F32, True); gple = S("gple", [128, 8], F32, True)
    gf_rep = S("gf_rep", [128, 1024], F32, True)
    gng_rep = S("gng_rep", [128, 512], F32, True); gnb_rep = S("gnb_rep", [128, 512], F32, True)
    ident = S("ident", [128, 128], F32, True); identb = S("identb", [128, 128], BF16, True)
    zerob = S("zerob", [1, 512], BF16, True)
    dmaskT = S("dmaskT", [128, 4, 128], F32, True); qdecT = S("qdecT", [128, 4, 128], F32, True)
    kdec = S("kdec", [128, 4], F32, True); hsel = S("hsel", [128, 8], F32, True)
    dmask8T = S("dmask8T", [8, 4, 8], F32, True); qdec8T = S("qdec8T", [128, 4, 8], F32, True)
    kdec8 = S("kdec8", [8, 4], F32, True)
    wb = [S("wbA", [128, 8, 512], BF16), S("wbB", [128, 8, 512], BF16)]
    wbuf = wb[0]
    xt = S("xt", [128, 1024]); xo = S("xo", [128, 1024]); xsbf = S("xsbf", [128, 1024], BF16)
    hT = S("hT", [128, 8, 128], BF16); hT2 = S("hT2", [128, 8, 128], BF16)
    ss = S("ss", [128, 8]); ropet = S("ropet", [128, 320])
    rT1 = S("rT1", [128, 512]); rT2 = S("rT2", [128, 512]); rO = S("rO", [128, 512])
    tokbf = S("tokbf", [128, 1024], BF16)
    kvst = S("kvst", [128, 4, 128]); wst = S("wst", [128, 2, 128])
    Kd = S("Kd", [128, 4, 128], BF16); Vbf = S("Vbf", [128, 4, 128], BF16)
    Sst = S("Sst", [128, 4, 128]); Sown = S("Sown", [128, 4, 128]); Sownb = S("Sownb", [128, 4, 128], BF16)
    blkT = S("blkT", [128, 2, 128], BF16)
    gl = [S("gl%d" % i, [128, 128]) for i in range(3)]
    hidb = S("hidb", [128, 2, 128], BF16)
    vcst = S("vcst", [128, 128], BF16)
    qT = S("qT", [128, 4, 128], BF16); qdT = S("qdT", [128, 4, 128], BF16); kT = S("kT", [128, 4, 128], BF16)
    vown = S("vown", [128, 4, 128], BF16); sTm = S("sTm", [128, 4, 128], BF16)
    QTu = S("QTu", [128, 4, 128], BF16); QTr = S("QTr", [128, 4, 128], BF16)
    sig = S("sig", [128, 24]); sgn = S("sgn", [128, 512])
    ET = [S("ET0", [128, 512], BF16), S("ET1", [128, 512], BF16)]
    e2 = S("e2", [128, 4, 128]); rsum = S("rsum", [128, 8]); imp = S("imp", [128, 128])
    score = S("score", [128, 256]); sc2 = S("sc2", [128, 256]); m8 = S("m8", [128, 16]); selb = S("selb", [128, 256])
    selT = S("selT", [128, 2, 128], BF16)
    ynsa = S("ynsa", [128, 2, 4, 64]); ytmp = S("ytmp", [128, 4, 64]); fac = S("fac", [128, 8])
    ymix = S("ymix", [128, 1024], BF16)
    ppt = S("ppt", [128, 256]); ppb = S("ppb", [128, 256], BF16); ppT = S("ppT", [128, 2, 128], BF16)
    gst = S("gst", [128, 8])
    if KDBG:
        dbgst[0] = S("dbgst", [128, 1024])

    PJ = [fw.ps(es0, "PJ0", [128, 512], F32), fw.ps(es0, "PJ1", [128, 512], F32)]
    PT = fw.ps(es0, "PT", [128, 8, 128], BF16)
    PT32 = fw.ps(es0, "PT32", [128, 512], F32)
    PS = [fw.ps(es0, "PS0", [128, 512], F32), fw.ps(es0, "PS1", [128, 512], F32)]
    PO = fw.ps(es0, "PO", [128, 512], F32)
    PM = fw.ps(es0, "PM", [128, 4, 128], F32)

    esw = contextlib.ExitStack()
    stg = fw.sb(esw, "stg", [128, PROJ], F32)
    fw.op("pool", lambda e: e.memset(ident.t[:], 1.0), [], [ident.b])
    fw.op("pool", lambda e: e.affine_select(out=ident.t[:], in_=ident.t[:], pattern=[[-1, 128]],
                                            compare_op=ALU.is_equal, fill=0.0, base=0, channel_multiplier=1),
          [ident.b], [ident.b])
    dve(lambda e: e.tensor_copy(out=identb.t[:], in_=ident.t[:]), [ident.b], [identb.b])
    fw.op("pool", lambda e: e.memset(zerob.t[:], 0.0), [], [zerob.b])
    for kc in range(8):
        ld(stg.t[:, :], w_in[kc * 128:(kc + 1) * 128, :], stg.b)
        act(lambda e: e.copy(out=Wi.t[:, kc, :], in_=stg.t[:, :]), [stg.b], [Wi.b])
    for wi_, wsrc in enumerate((w_out, w_gate)):
        for nn in range(2):
            for kc in range(8):
                ld(stg.t[:, 0:512], wsrc[kc * 128:(kc + 1) * 128, nn * 512:(nn + 1) * 512], stg.b)
                dve(lambda e: e.tensor_copy(out=wbuf.t[:, kc, :], in_=stg.t[:, 0:512]), [stg.b], [wbuf.b])
            fw.dma("sp", lambda e: e.dma_start(out=wscr[wi_][:, nn * 512:(nn + 1) * 512].rearrange("(k p) n -> p k n", p=128),
                                               in_=wbuf.t[:, :, :]), [wbuf.b], [SCRB])
    for kc in range(2):
        ld(stg.t[:, 0:1024], w_ple[kc * 128:(kc + 1) * 128, :], stg.b)
        dve(lambda e: e.tensor_copy(out=Wp.t[:, kc, :], in_=stg.t[:, 0:1024]), [stg.b], [Wp.b])
    for kv in range(2):
        for jh2 in range(2):
            fw.op("pool", lambda e: e.memset(stg.t[:, 0:2048], 0.0), [], [stg.b])
            for k in range(2):
                for jq in range(2):
                    j0 = jh2 * 16 + jq * 8
                    fw.dma("sp", lambda e: e.dma_start(
                        out=stg.t[k * 64:(k + 1) * 64, 0:2048].rearrange("p (j m) -> p j m", j=16)[:, jq * 8:(jq + 1) * 8, k * 64:(k + 1) * 64],
                        in_=w1_kv[kv][k, j0:j0 + 8].rearrange("j d e -> d j e")), [], [stg.b])
            dve(lambda e: e.tensor_copy(out=W1[kv].t[:, jh2 * 16:(jh2 + 1) * 16, :].rearrange("p j m -> p (j m)"), in_=stg.t[:, 0:2048]),
                [stg.b], [W1[kv].b])
        fw.op("pool", lambda e: e.memset(stg.t[:, 0:128], 0.0), [], [stg.b])
        for k in range(2):
            ld(stg.t[k * 64:(k + 1) * 64, k * 64:(k + 1) * 64], w2_kv[kv][k], stg.b)
        dve(lambda e: e.tensor_copy(out=W2[kv].t[:, :], in_=stg.t[:, 0:128]), [stg.b], [W2[kv].b])
        ld(stg.t[0:32, 0:128], pe_kv[kv], stg.b)
        tr(PT32.t[:, 0:32], stg.t[0:32, 0:128], ident.t[0:32, 0:32], [stg.b, ident.b], PT32.b)
        act(lambda e: e.copy(out=peT.t[:, kv, :], in_=PT32.t[:, 0:32]), [PT32.b], [peT.b])
    for gt_, gd_ in ((gmix, norm_mix), (gple, norm_ple)):
        ld(stg.t[0:8, 0:128], gd_.rearrange("(k p) -> k p", p=128), stg.b)
        tr(PT32.t[:, 0:8], stg.t[0:8, 0:128], ident.t[0:8, 0:8], [stg.b, ident.b], PT32.b)
        act(lambda e: e.copy(out=gt_.t[:, :], in_=PT32.t[:, 0:8]), [PT32.b], [gt_.b])

    def rep(v, n):
        return bass.AP(tensor=v.tensor, offset=v.offset, ap=[[0, 128], [1, n]])
    ld(gf_rep.t[:, :], rep(norm_f, 1024), gf_rep.b)
    ld(gng_rep.t[:, :], rep(gn_g, 512), gng_rep.b)
    ld(gnb_rep.t[:, :], rep(gn_b, 512), gnb_rep.b)
    for t_, d_ in ((dmaskT, dmaskT_d), (qdecT, qdecT_d), (kdec, kdec_d), (hsel, hsel_d),
                   (dmask8T, dmask8T_d), (qdec8T, qdec8T_d), (kdec8, kdec8_d)):
        ld(t_.t[:], d_, t_.b)

    fw.barrier()
    esw.close()

    def norm_T(x, n, gcol, out_hT):
        act(lambda e: e.activation(out=xsbf.t[0:n, :], in_=x.t[0:n, :], func=AF.Square, accum_out=ss.t[0:n, 0:1]),
            [x.b], [xsbf.b, ss.b])
        dve(lambda e: e.tensor_scalar(out=ss.t[0:n, 0:1], in0=ss.t[0:n, 0:1], scalar1=1.0 / 1024, scalar2=1e-6,
                                      op0=ALU.mult, op1=ALU.add), [ss.b], [ss.b])
        act(lambda e: e.activation(out=ss.t[0:n, 0:1], in_=ss.t[0:n, 0:1], func=AF.Sqrt), [ss.b], [ss.b])
        dve(lambda e: e.reciprocal(out=ss.t[0:n, 0:1], in_=ss.t[0:n, 0:1]), [ss.b], [ss.b])
        dve(lambda e: e.tensor_scalar(out=xsbf.t[0:n, :], in0=x.t[0:n, :], scalar1=ss.t[0:n, 0:1], scalar2=None,
                                      op0=ALU.mult), [x.b, ss.b], [xsbf.b])
        for kc in range(8):
            tr(PT.t[:, kc, 0:n], xsbf.t[0:n, kc * 128:(kc + 1) * 128], identb.t[0:n, 0:n], [xsbf.b, identb.b], PT.b)
        dve(lambda e: e.tensor_tensor(out=out_hT.t[:, :, 0:n], in0=PT.t[:, :, 0:n], in1=bcast(gcol.t[:, :], 2, n),
                                      op=ALU.mult), [PT.b, gcol.b], [out_hT.b])

    def proj(h, n, c0, w, bank):
        for kc in range(8):
            mm(bank.t[0:n, 0:w], h.t[:, kc, 0:n], Wi.t[:, kc, c0:c0 + w], kc == 0, kc == 7, [h.b, Wi.b], bank.b)

    def rope(src, sb_, dst, db_, n, H, half, c0):
        s4 = src.rearrange("p (h t f) -> p h t f", h=H, t=2)
        d4 = dst.rearrange("p (h t f) -> p h t f", h=H, t=2)
        cs = bcast(ropet.t[0:n, c0:c0 + half], 1, H)
        sn = bcast(ropet.t[0:n, c0 + half:c0 + 2 * half], 1, H)
        t1 = rT1.t[0:n, 0:H * half].rearrange("p (h f) -> p h f", h=H)
        t2 = rT2.t[0:n, 0:H * half].rearrange("p (h f) -> p h f", h=H)
        x1, x2 = s4[:, :, 0, :], s4[:, :, 1, :]
        dve(lambda e: e.tensor_tensor(out=t1, in0=x1, in1=cs, op=ALU.mult), [sb_, ropet.b], [rT1.b])
        dve(lambda e: e.tensor_tensor(out=t2, in0=x2, in1=sn, op=ALU.mult), [sb_, ropet.b], [rT2.b])
        dve(lambda e: e.tensor_tensor(out=d4[:, :, 0, :], in0=t1, in1=t2, op=ALU.subtract), [rT1.b, rT2.b], [db_])
        dve(lambda e: e.tensor_tensor(out=t1, in0=x2, in1=cs, op=ALU.mult), [sb_, ropet.b], [rT1.b])
        dve(lambda e: e.tensor_tensor(out=t2, in0=x1, in1=sn, op=ALU.mult), [sb_, ropet.b], [rT2.b])
        dve(lambda e: e.tensor_tensor(out=d4[:, :, 1, :], in0=t1, in1=t2, op=ALU.add), [rT1.b, rT2.b], [db_])

    def gelu_to(src, sb_, dst, db_, np_, n):
        a, b, c = gl[0], gl[1], gl[2]
        act(lambda e: e.copy(out=a.t[0:np_, 0:n], in_=src), [sb_], [a.b])
        dve(lambda e: e.tensor_tensor(out=b.t[0:np_, 0:n], in0=a.t[0:np_, 0:n], in1=a.t[0:np_, 0:n], op=ALU.mult), [a.b], [b.b])
        dve(lambda e: e.tensor_scalar(out=b.t[0:np_, 0:n], in0=b.t[0:np_, 0:n], scalar1=0.044715, scalar2=1.0,
                                      op0=ALU.mult, op1=ALU.add), [b.b], [b.b])
        dve(lambda e: e.tensor_tensor(out=b.t[0:np_, 0:n], in0=b.t[0:np_, 0:n], in1=a.t[0:np_, 0:n], op=ALU.mult), [a.b, b.b], [b.b])
        act(lambda e: e.activation(out=c.t[0:np_, 0:n], in_=b.t[0:np_, 0:n], func=AF.Tanh, scale=0.7978845608028654), [b.b], [c.b])
        dve(lambda e: e.tensor_scalar(out=c.t[0:np_, 0:n], in0=c.t[0:np_, 0:n], scalar1=0.5, scalar2=0.5,
                                      op0=ALU.mult, op1=ALU.add), [c.b], [c.b])
        dve(lambda e: e.tensor_tensor(out=dst, in0=c.t[0:np_, 0:n], in1=a.t[0:np_, 0:n], op=ALU.mult), [a.b, c.b], [db_])

    def retention_q(n, dm, qd, Sb):
        for i in range(8):
            tr(PT.t[:, i, 0:n], tokbf.t[0:n, i * 128:(i + 1) * 128], identb.t[0:n, 0:n], [tokbf.b, identb.b], PT.b)
        act(lambda e: e.copy(out=qT.t[:, :, 0:n], in_=PT.t[:, 0:4, 0:n]), [PT.b], [qT.b])
        dve(lambda e: e.tensor_tensor(out=qdT.t[:, :, 0:n], in0=PT.t[:, 0:4, 0:n], in1=qd.t[:, :, 0:n], op=ALU.mult),
            [PT.b, qd.b], [qdT.b])
        act(lambda e: e.copy(out=kT.t[:, :, 0:n], in_=PT.t[:, 4:8, 0:n]), [PT.b], [kT.b])
        for h in range(4):
            mm(PM.t[0:n, h, 0:n], kT.t[:, h, 0:n], qT.t[:, h, 0:n], True, True, [kT.b, qT.b], PM.b)
        dve(lambda e: e.tensor_tensor(out=sTm.t[0:n, :, 0:n], in0=PM.t[0:n, :, 0:n], in1=dm.t[0:n, :, 0:n], op=ALU.mult),
            [PM.b, dm.b], [sTm.b])
        o = PJ[1]
        for h in range(4):
            mm(o.t[0:n, h * 128:(h + 1) * 128], sTm.t[0:n, h, 0:n], vown.t[0:n, h, :], True, False, [sTm.b, vown.b], o.b)
            mm(o.t[0:n, h * 128:(h + 1) * 128], qdT.t[:, h, 0:n], Sb.t[:, h, :], False, True, [qdT.b, Sb.b], o.b)
        ov = rT1.t[0:n, :].rearrange("p (h e) -> p h e", h=4)
        sq = rT2.t[0:n, :].rearrange("p (h e) -> p h e", h=4)
        act(lambda e: e.copy(out=rT1.t[0:n, :], in_=o.t[0:n, :]), [o.b], [rT1.b])
        dve(lambda e: e.tensor_reduce(out=gst.t[0:n, 0:4], in_=ov, axis=AX.X, op=ALU.add), [rT1.b], [gst.b])
        dve(lambda e: e.tensor_tensor(out=sq, in0=ov, in1=ov, op=ALU.mult), [rT1.b], [rT2.b])
        dve(lambda e: e.tensor_reduce(out=gst.t[0:n, 4:8], in_=sq, axis=AX.X, op=ALU.add), [rT2.b], [gst.b])
        dve(lambda e: e.tensor_scalar(out=gst.t[0:n, 0:8], in0=gst.t[0:n, 0:8], scalar1=1.0 / 128, scalar2=None, op0=ALU.mult),
            [gst.b], [gst.b])
        dve(lambda e: e.tensor_tensor(out=fac.t[0:n, 0:4], in0=gst.t[0:n, 0:4], in1=gst.t[0:n, 0:4], op=ALU.mult), [gst.b], [fac.b])
        dve(lambda e: e.tensor_tensor(out=gst.t[0:n, 4:8], in0=gst.t[0:n, 4:8], in1=fac.t[0:n, 0:4], op=ALU.subtract),
            [gst.b, fac.b], [gst.b])
        dve(lambda e: e.tensor_scalar(out=gst.t[0:n, 4:8], in0=gst.t[0:n, 4:8], scalar1=1e-5, scalar2=None, op0=ALU.add),
            [gst.b], [gst.b])
        act(lambda e: e.activation(out=gst.t[0:n, 4:8], in_=gst.t[0:n, 4:8], func=AF.Sqrt), [gst.b], [gst.b])
        dve(lambda e: e.reciprocal(out=gst.t[0:n, 4:8], in_=gst.t[0:n, 4:8]), [gst.b], [gst.b])
        dve(lambda e: e.tensor_tensor(out=ov, in0=ov, in1=bcast(gst.t[0:n, 0:4], 2, 128), op=ALU.subtract), [rT1.b, gst.b], [rT1.b])
        dve(lambda e: e.tensor_tensor(out=ov, in0=ov, in1=bcast(gst.t[0:n, 4:8], 2, 128), op=ALU.mult), [rT1.b, gst.b], [rT1.b])
        dve(lambda e: e.tensor_tensor(out=rT1.t[0:n, :], in0=rT1.t[0:n, :], in1=gng_rep.t[0:n, :], op=ALU.mult), [rT1.b, gng_rep.b], [rT1.b])
        dve(lambda e: e.tensor_tensor(out=rT1.t[0:n, :], in0=rT1.t[0:n, :], in1=gnb_rep.t[0:n, :], op=ALU.add), [rT1.b, gnb_rep.b], [rT1.b])

    def attend(n, k, Q, tiles, br, first):
        mm(PO.t[0:n, 0:260], zerob.t[0:1, 0:n], zerob.t[0:1, 0:260], True, False, [zerob.b], PO.b)
        nt = len(tiles)

        def pv(i):
            KTa, Va, nk, rds, biases = tiles[i]
            et = ET[i % 2]
            for g in range(4):
                mm(PO.t[0:n, g * 65:(g + 1) * 65], et.t[0:nk, g * n:(g + 1) * n], Va, False, i == nt - 1, rds + [et.b], PO.b)
        for i, (KTa, Va, nk, rds, biases) in enumerate(tiles):
            ps_, et = PS[i % 2], ET[i % 2]
            outv = ps_.t[0:nk, 0:4 * n].rearrange("p (g q) -> p g q", g=4)
            mm(outv, KTa, Q.t[k * 64:(k + 1) * 64, :, 0:n], True, len(biases) == 0, rds + [Q.b], ps_.b)
            for bi, (bl, br_, brd) in enumerate(biases):
                mm(outv, bl, br_, False, bi == len(biases) - 1, brd, ps_.b)
            act(lambda e: e.activation(out=et.t[0:nk, 0:4 * n], in_=ps_.t[0:nk, 0:4 * n], func=AF.Exp, scale=0.125),
                [ps_.b], [et.b])
            if i > 0:
                pv(i - 1)
        pv(nt - 1)
        pov = PO.t[0:n, 0:260].rearrange("p (g c) -> p g c", g=4)
        dve(lambda e: e.tensor_scalar(out=fac.t[0:n, 0:4], in0=pov[:, :, 64], scalar1=1e-30, scalar2=None, op0=ALU.add), [PO.b], [fac.b])
        dve(lambda e: e.reciprocal(out=fac.t[0:n, 0:4], in_=fac.t[0:n, 0:4]), [fac.b], [fac.b])
        dve(lambda e: e.tensor_tensor(out=fac.t[0:n, 0:4], in0=fac.t[0:n, 0:4], in1=sig.t[0:n, br * 8 + k * 4:br * 8 + k * 4 + 4],
                                      op=ALU.mult), [fac.b, sig.b], [fac.b])
        if first:
            dve(lambda e: e.tensor_tensor(out=ynsa.t[0:n, k, :, :], in0=pov[:, :, 0:64], in1=bcast(fac.t[0:n, 0:4], 2, 64),
                                          op=ALU.mult), [PO.b, fac.b], [ynsa.b])
        else:
            dve(lambda e: e.tensor_tensor(out=ytmp.t[0:n, :, :], in0=pov[:, :, 0:64], in1=bcast(fac.t[0:n, 0:4], 2, 64),
                                          op=ALU.mult), [PO.b, fac.b], [ytmp.b])
            dve(lambda e: e.tensor_tensor(out=ynsa.t[0:n, k, :, :], in0=ynsa.t[0:n, k, :, :], in1=ytmp.t[0:n, :, :],
                                          op=ALU.add), [ynsa.b, ytmp.b], [ynsa.b])

    def nq_prep(n, c_n):
        src = PJ[1]
        pv = lambda ap: ap.rearrange("p (two m d) -> p m two d", two=2, m=4)
        dv = lambda ap: ap.rearrange("p (m two d) -> p m two d", two=2, m=4)
        act(lambda e: e.copy(out=dv(tokbf.t[0:n, 0:512]), in_=pv(src.t[0:n, 0:512])), [src.b], [tokbf.b])
        rope(src.t[0:n, 0:512], src.b, rO.t[0:n, 0:512], rO.b, n, 8, 32, c_n)
        act(lambda e: e.copy(out=dv(tokbf.t[0:n, 512:1024]), in_=pv(rO.t[0:n, 0:512])), [rO.b], [tokbf.b])
        for i in range(8):
            tr(PT.t[:, i, 0:n], tokbf.t[0:n, i * 128:(i + 1) * 128], identb.t[0:n, 0:n], [tokbf.b, identb.b], PT.b)
        act(lambda e: e.copy(out=QTu.t[:, :, 0:n], in_=PT.t[:, 0:4, 0:n]), [PT.b], [QTu.b])
        act(lambda e: e.copy(out=QTr.t[:, :, 0:n], in_=PT.t[:, 4:8, 0:n]), [PT.b], [QTr.b])

    def prefetch_fin(n, ppsrc):
        for nn in range(2):
            ld(wb[nn].t[:, :, :], wscr[0][:, nn * 512:(nn + 1) * 512].rearrange("(k p) n -> p k n", p=128), wb[nn].b, reads=[SCRB])
        ld(ppt.t[0:n, :], ppsrc, ppt.b)

    def finish(n, x, ppsrc, ydst):
        for i in range(8):
            tr(PT.t[:, i, 0:n], ymix.t[0:n, i * 128:(i + 1) * 128], identb.t[0:n, 0:n], [ymix.b, identb.b], PT.b)
        act(lambda e: e.copy(out=hT2.t[:, :, 0:n], in_=PT.t[:, :, 0:n]), [PT.b], [hT2.b])
        for nn in range(2):
            for kc in range(8):
                mm(PJ[nn].t[0:n, :], hT2.t[:, kc, 0:n], wb[nn].t[:, kc, :], kc == 0, kc == 7,
                   [hT2.b, wb[nn].b], PJ[nn].b)
            ld(wb[nn].t[:, :, :], wscr[1][:, nn * 512:(nn + 1) * 512].rearrange("(k p) n -> p k n", p=128), wb[nn].b, reads=[SCRB])
            dve(lambda e: e.tensor_tensor(out=x.t[0:n, nn * 512:(nn + 1) * 512], in0=x.t[0:n, nn * 512:(nn + 1) * 512],
                                          in1=PJ[nn].t[0:n, :], op=ALU.add), [x.b, PJ[nn].b], [x.b])
        if KDBG and n == 128 and not hasattr(finish, "done"):
            dbg("x2", x.t[:, :], 128, 1024, x.b)
        norm_T(x, n, gple, hT)
        dve(lambda e: e.tensor_copy(out=ppb.t[0:n, :], in_=ppt.t[0:n, :]), [ppt.b], [ppb.b])
        for nn in range(2):
            for kc in range(8):
                mm(PJ[nn].t[0:n, :], hT.t[:, kc, 0:n], wb[nn].t[:, kc, :], kc == 0, kc == 7,
                   [hT.b, wb[nn].b], PJ[nn].b)
            act(lambda e: e.activation(out=xt.t[0:n, nn * 512:(nn + 1) * 512], in_=PJ[nn].t[0:n, :], func=AF.Sigmoid),
                [PJ[nn].b], [xt.b])
        for i in range(2):
            tr(PT.t[:, i, 0:n], ppb.t[0:n, i * 128:(i + 1) * 128], identb.t[0:n, 0:n], [ppb.b, identb.b], PT.b)
        act(lambda e: e.copy(out=ppT.t[:, :, 0:n], in_=PT.t[:, 0:2, 0:n]), [PT.b], [ppT.b])
        for nn in range(2):
            for kc in range(2):
                mm(PJ[nn].t[0:n, :], ppT.t[:, kc, 0:n], Wp.t[:, kc, nn * 512:(nn + 1) * 512], kc == 0, kc == 1,
                   [ppT.b, Wp.b], PJ[nn].b)
            dve(lambda e: e.tensor_tensor(out=xt.t[0:n, nn * 512:(nn + 1) * 512], in0=xt.t[0:n, nn * 512:(nn + 1) * 512],
                                          in1=PJ[nn].t[0:n, :], op=ALU.mult), [xt.b, PJ[nn].b], [xt.b])
        dve(lambda e: e.tensor_tensor(out=x.t[0:n, :], in0=x.t[0:n, :], in1=xt.t[0:n, :], op=ALU.add), [x.b, xt.b], [x.b])
        if KDBG and n == 128 and not hasattr(finish, "done"):
            dbg("x3", x.t[:, :], 128, 1024, x.b)
            finish.done = True
        act(lambda e: e.activation(out=xsbf.t[0:n, :], in_=x.t[0:n, :], func=AF.Square, accum_out=ss.t[0:n, 0:1]),
            [x.b], [xsbf.b, ss.b])
        dve(lambda e: e.tensor_scalar(out=ss.t[0:n, 0:1], in0=ss.t[0:n, 0:1], scalar1=1.0 / 1024, scalar2=1e-6,
                                      op0=ALU.mult, op1=ALU.add), [ss.b], [ss.b])
        act(lambda e: e.activation(out=ss.t[0:n, 0:1], in_=ss.t[0:n, 0:1], func=AF.Sqrt), [ss.b], [ss.b])
        dve(lambda e: e.reciprocal(out=ss.t[0:n, 0:1], in_=ss.t[0:n, 0:1]), [ss.b], [ss.b])
        dve(lambda e: e.tensor_scalar(out=xt.t[0:n, :], in0=x.t[0:n, :], scalar1=ss.t[0:n, 0:1], scalar2=None, op0=ALU.mult),
            [x.b, ss.b], [xt.b])
        dve(lambda e: e.tensor_tensor(out=xt.t[0:n, :], in0=xt.t[0:n, :], in1=gf_rep.t[0:n, :], op=ALU.mult), [xt.b, gf_rep.b], [xt.b])
        fw.dma("sp", lambda e: e.dma_start(out=ydst, in_=xt.t[0:n, :]), [xt.b], [OUTB])

    def gates_and_ret_out(n):
        proj(hT, n, C_RG, 512, PJ[0])
        act(lambda e: e.activation(out=rT2.t[0:n, :], in_=PJ[0].t[0:n, :], func=AF.Silu), [PJ[0].b], [rT2.b])
        dve(lambda e: e.tensor_tensor(out=ymix.t[0:n, 0:512], in0=rT1.t[0:n, :], in1=rT2.t[0:n, :], op=ALU.mult),
            [rT1.b, rT2.b], [ymix.b])
        proj(hT, n, C_NGL, 24, PJ[0])
        act(lambda e: e.activation(out=sig.t[0:n, :], in_=PJ[0].t[0:n, 0:24], func=AF.Sigmoid), [PJ[0].b], [sig.b])
        proj(hT, n, C_NG, 512, PJ[0])
        act(lambda e: e.activation(out=sgn.t[0:n, :], in_=PJ[0].t[0:n, :], func=AF.Silu), [PJ[0].b], [sgn.b])

    def nsa_out(n):
        dve(lambda e: e.tensor_tensor(out=ymix.t[0:n, 512:1024], in0=ynsa.t[0:n, :, :, :].rearrange("p a b c -> p (a b c)"),
                                      in1=sgn.t[0:n, :], op=ALU.mult), [ynsa.b, sgn.b], [ymix.b])

    def top_sel(n, nblk, kth):
        dve(lambda e: e.max(out=m8.t[0:n, 0:8], in_=score.t[0:n, 0:nblk]), [score.b], [m8.b])
        dve(lambda e: e.match_replace(out=sc2.t[0:n, 0:nblk], in_to_replace=m8.t[0:n, 0:8], in_values=score.t[0:n, 0:nblk],
                                      imm_value=-1e30), [score.b, m8.b], [sc2.b])
        dve(lambda e: e.max(out=m8.t[0:n, 8:16], in_=sc2.t[0:n, 0:nblk]), [sc2.b], [m8.b])
        dve(lambda e: e.tensor_scalar(out=selb.t[0:n, 0:nblk], in0=score.t[0:n, 0:nblk], scalar1=m8.t[0:n, kth - 1:kth],
                                      scalar2=NEG, op0=ALU.is_lt, op1=ALU.mult), [score.b, m8.b], [selb.b])

    esp = contextlib.ExitStack()
    P_ = lambda name, shape, dt=F32, const=False: fw.sb(esp, name, shape, dt, const)
    KTs = P_("KTs", [128, SEQ], BF16); Vs = P_("Vs", [128, NB, 2, 65], BF16)
    KTw = P_("KTw", [128, 8 * 128], BF16); Vw = P_("Vw", [128, 8, 2, 65], BF16)
    KCT = P_("KCT", [128, 128], BF16); VC = P_("VC", [128, 2, 65], BF16)
    expand = P_("expand", [128, SEQ], BF16, True)
    TB = P_("TB", [128, 2, 128], BF16, True); WB = P_("WB", [128, 6, 128], BF16, True)
    CBp = P_("CBp", [128, 128], BF16); CBqp = P_("CBqp", [128, 128], BF16); FBp = P_("FBp", [128, NSB])
    mst = P_("mst", [128, 1024])
    for tt in (Vs, Vw, VC):
        fw.op("pool", lambda e: e.memset(tt.t[:], 1.0), [], [tt.b])
    fw.op("pool", lambda e: e.memset(Sst.t[:], 0.0), [], [Sst.b])
    fw.op("pool", lambda e: e.memset(KCT.t[:], 0.0), [], [KCT.b])
    fw.op("pool", lambda e: e.memset(KTs.t[:], 0.0), [], [KTs.b])
    fw.op("pool", lambda e: e.memset(KTw.t[:], 0.0), [], [KTw.b])
    fw.op("pool", lambda e: e.memset(expand.t[:], 0.0), [], [expand.b])
    fw.op("pool", lambda e: e.memset(selT.t[:], 0.0), [], [selT.b])
    for c0 in range(0, SEQ, 1024):
        w = min(1024, SEQ - c0)
        ld(mst.t[0:NSB, 0:w], expand_d[:, c0:c0 + w], mst.b)
        dve(lambda e: e.tensor_copy(out=expand.t[0:NSB, c0:c0 + w], in_=mst.t[0:NSB, 0:w]), [mst.b], [expand.b])
    ld(mst.t[:, 0:256], mTB.rearrange("p a b -> p (a b)"), mst.b)
    dve(lambda e: e.tensor_copy(out=TB.t[:].rearrange("p a b -> p (a b)"), in_=mst.t[:, 0:256]), [mst.b], [TB.b])
    ld(mst.t[:, 0:768], mWB.rearrange("p a b -> p (a b)"), mst.b)
    dve(lambda e: e.tensor_copy(out=WB.t[:].rearrange("p a b -> p (a b)"), in_=mst.t[:, 0:768]), [mst.b], [WB.b])

    def kv_block(t, p):
        ld(xt.t[:, :], xb[t * 128:(t + 1) * 128, :], xt.b)
        ld(ropet.t[:, :], rope_kv[t * 128:(t + 1) * 128, :], ropet.b)
        norm_T(xt, 128, gmix, hT)
        if SUB <= 1:
            return
        proj(hT, 128, C_RK, 512, PJ[0])
        rope(PJ[0].t[:, :], PJ[0].b, rO.t[:, :], rO.b, 128, 4, 64, 128)
        dve(lambda e: e.tensor_tensor(out=Kd.t[:, :, :], in0=rO.t[:, :].rearrange("p (h d) -> p h d", h=4),
                                      in1=bcast(kdec.t[:, 0:4], 2, 128), op=ALU.mult), [rO.b, kdec.b], [Kd.b])
        proj(hT, 128, C_RV, 512, PJ[1])
        act(lambda e: e.copy(out=Vbf.t[:, :, :].rearrange("p h d -> p (h d)"), in_=PJ[1].t[:, :]), [PJ[1].b], [Vbf.b])
        for h in range(4):
            mm(PM.t[:, h, :], Kd.t[:, h, :], Vbf.t[:, h, :], True, True, [Kd.b, Vbf.b], PM.b)
        if SUB <= 2:
            return
        if t == 2 * p:
            for h in range(4):
                dve(lambda e: e.tensor_scalar(out=Sown.t[:, h, :], in0=Sst.t[:, h, :], scalar1=hsel.t[:, h:h + 1], scalar2=None,
                                              op0=ALU.mult), [Sst.b, hsel.b], [Sown.b])
                dve(lambda e: e.scalar_tensor_tensor(out=Sown.t[:, h, :], in0=PM.t[:, h, :], scalar=hsel.t[:, 4 + h:5 + h],
                                                     in1=Sown.t[:, h, :], op0=ALU.mult, op1=ALU.add),
                    [PM.b, hsel.b, Sown.b], [Sown.b])
            act(lambda e: e.copy(out=Sownb.t[:, :, :], in_=Sown.t[:, :, :]), [Sown.b], [Sownb.b])
        for h in range(4):
            dve(lambda e: e.scalar_tensor_tensor(out=Sst.t[:, h, :], in0=Sst.t[:, h, :], scalar=gC[h], in1=PM.t[:, h, :],
                                                 op0=ALU.mult, op1=ALU.add), [Sst.b, PM.b], [Sst.b])
        if SUB <= 3:
            return
        proj(hT, 128, C_CK, 512, PJ[0])
        act(lambda e: e.copy(out=kvst.t[:, 0:2, :], in_=PJ[0].t[:, 0:256].rearrange("p (a b) -> p a b", a=2)), [PJ[0].b], [kvst.b])
        act(lambda e: e.copy(out=kvst.t[:, 3, :], in_=PJ[0].t[:, 384:512]), [PJ[0].b], [kvst.b])
        rope(PJ[0].t[:, 256:384], PJ[0].b, kvst.t[:, 2, :], kvst.b, 128, 2, 32, 256)
        proj(hT, 128, C_WK, 256, PJ[1])
        act(lambda e: e.copy(out=wst.t[:, 1, :], in_=PJ[1].t[:, 128:256]), [PJ[1].b], [wst.b])
        rope(PJ[1].t[:, 0:128], PJ[1].b, wst.t[:, 0, :], wst.b, 128, 2, 32, 256)
        if SUB <= 4:
            return
        fw.dma("sp", lambda e: e.dma_start(out=kvout[t * 128:(t + 1) * 128, :, :], in_=kvst.t[:, :, :]), [kvst.b], [OUTB])
        if t >= NB - 4:
            tw = t - (NB - 4)
            fw.dma("sp", lambda e: e.dma_start(out=winout[tw * 128:(tw + 1) * 128, :, :], in_=wst.t[:, :, :]), [wst.b], [OUTB])
        if SUB <= 5:
            return
        dve(lambda e: e.tensor_copy(out=tokbf.t[:, 0:384], in_=kvst.t[:, 0:3, :].rearrange("p a b -> p (a b)")), [kvst.b], [tokbf.b])
        dve(lambda e: e.tensor_copy(out=tokbf.t[:, 384:512], in_=wst.t[:, 0, :]), [wst.b], [tokbf.b])
        for i in range(4):
            tr(PT.t[:, i, :], tokbf.t[:, i * 128:(i + 1) * 128], identb.t[:, :], [tokbf.b, identb.b], PT.b)
        if SUB5 <= 1:
            return
        act(lambda e: e.copy(out=KTs.t[:, t * 128:(t + 1) * 128], in_=PT.t[:, 2, :]), [PT.b], [KTs.b])
        act(lambda e: e.copy(out=KTw.t[:, (t % 8) * 128:(t % 8 + 1) * 128], in_=PT.t[:, 3, :]), [PT.b], [KTw.b])
        if SUB5 <= 2:
            return
        for kv in range(2):
            dve(lambda e: e.tensor_tensor(out=blkT.t[:, kv, :].rearrange("p (c j) -> p c j", c=4),
                                          in0=PT.t[:, kv, :].rearrange("p (c j) -> p c j", c=4),
                                          in1=bcast(peT.t[:, kv, :], 1, 4), op=ALU.add), [PT.b, peT.b], [blkT.b])
        if SUB5 <= 3:
            return
        dve(lambda e: e.tensor_copy(out=Vs.t[:, t, :, 0:64], in_=kvst.t[:, 3, :].rearrange("p (k d) -> p k d", k=2)), [kvst.b], [Vs.b])
        dve(lambda e: e.tensor_copy(out=Vw.t[:, t % 8, :, 0:64], in_=wst.t[:, 1, :].rearrange("p (k d) -> p k d", k=2)), [wst.b], [Vw.b])
        if SUB <= 6:
            return
        for kv in range(2):
            bv = blkT.t[:, kv, :].rearrange("p (c j) -> p c j", c=4)
            for j in range(32):
                mm(PT32.t[:, kv * 4:kv * 4 + 4], W1[kv].t[:, j, :], bv[:, :, j], j == 0, j == 31, [W1[kv].b, blkT.b], PT32.b)
        gelu_to(PT32.t[:, 0:8], PT32.b, hidb.t[:, 0, 0:8], hidb.b, 128, 8)
        if SUB <= 7:
            return
        mm(PT32.t[:, 8:12], W2[0].t[:, :], hidb.t[:, 0, 0:4], True, True, [W2[0].b, hidb.b], PT32.b)
        act(lambda e: e.copy(out=KCT.t[:, 4 * t:4 * t + 4], in_=PT32.t[:, 8:12]), [PT32.b], [KCT.b])
        mm(PT32.t[0:4, 16:144], hidb.t[:, 0, 4:8], W2[1].t[:, :], True, True, [W2[1].b, hidb.b], PT32.b)
        act(lambda e: e.copy(out=vcst.t[0:4, :], in_=PT32.t[0:4, 16:144]), [PT32.b], [vcst.b])
        if SUB <= 8:
            return
        fw.dma("sp", lambda e: e.dma_start(out=VC.t[4 * t:4 * t + 4, :, 0:64],
                                           in_=vcst.t[0:4, :].rearrange("p (k d) -> p k d", k=2)), [vcst.b], [VC.b])

    def q_block(p):
        n = 128
        prefetch_fin(n, pp_own[p * 128:(p + 1) * 128, :])
        ld(ropet.t[:, :], rope_own[p * 128:(p + 1) * 128, :], ropet.b)
        ld(mst.t[:, 0:128], mCB[p], mst.b)
        dve(lambda e: e.tensor_copy(out=CBp.t[:, :], in_=mst.t[:, 0:128]), [mst.b], [CBp.b])
        ld(mst.t[:, 128:256], mCBq[p], mst.b)
        dve(lambda e: e.tensor_copy(out=CBqp.t[:, :], in_=mst.t[:, 128:256]), [mst.b], [CBqp.b])
        ld(FBp.t[:, :], mFB[p], FBp.b)
        norm_T(xo, n, gmix, hT)
        proj(hT, n, C_RQ, 512, PJ[0])
        rope(PJ[0].t[:, :], PJ[0].b, tokbf.t[:, 0:512], tokbf.b, n, 4, 64, 0)
        proj(hT, n, C_RK, 512, PJ[1])
        rope(PJ[1].t[:, :], PJ[1].b, tokbf.t[:, 512:1024], tokbf.b, n, 4, 64, 128)
        proj(hT, n, C_RV, 512, PJ[0])
        act(lambda e: e.copy(out=vown.t[:, :, :].rearrange("p h d -> p (h d)"), in_=PJ[0].t[:, :]), [PJ[0].b], [vown.b])
        if SUBQ <= 1:
            return
        retention_q(n, dmaskT, qdecT, Sownb)
        if p == 0:
            dbg("retn", rT1.t[:, :], 128, 512, rT1.b)
        if SUBQ <= 2:
            return
        gates_and_ret_out(n)
        if p == 0:
            dbg("yret", ymix.t[:, 0:512], 128, 512, ymix.b)
            dbg("sig", sig.t[:, :], 128, 24, sig.b)
            dbg("sgn", sgn.t[:, :], 128, 512, sgn.b)
        proj(hT, n, C_NQ, 512, PJ[1])
        nq_prep(n, 256)
        if SUBQ <= 3:
            return
        for k in range(2):
            pj = PJ[0]
            for g in range(4):
                mm(pj.t[:, g * 128:g * 128 + NC], QTu.t[k * 64:(k + 1) * 64, g, :], KCT.t[k * 64:(k + 1) * 64, 0:NC], True, False,
                   [QTu.b, KCT.b], pj.b)
                mm(pj.t[:, g * 128:g * 128 + NC], identb.t[:, :], CBqp.t[:, 0:NC], False, True, [identb.b, CBqp.b], pj.b)
            for g in range(4):
                act(lambda e: e.activation(out=e2.t[:, g, 0:NC], in_=pj.t[:, g * 128:g * 128 + NC], func=AF.Exp, scale=0.125,
                                           accum_out=rsum.t[:, g:g + 1]), [pj.b], [e2.b, rsum.b])
            dve(lambda e: e.tensor_scalar(out=rsum.t[:, 4:8], in0=rsum.t[:, 0:4], scalar1=1e-30, scalar2=None, op0=ALU.add), [rsum.b], [rsum.b])
            dve(lambda e: e.reciprocal(out=rsum.t[:, 4:8], in_=rsum.t[:, 4:8]), [rsum.b], [rsum.b])
            dve(lambda e: e.tensor_scalar(out=imp.t[:, 0:NC], in0=e2.t[:, 0, 0:NC], scalar1=rsum.t[:, 4:5], scalar2=None, op0=ALU.mult),
                [e2.b, rsum.b], [imp.b])
            for g in range(1, 4):
                dve(lambda e: e.scalar_tensor_tensor(out=imp.t[:, 0:NC], in0=e2.t[:, g, 0:NC], scalar=rsum.t[:, 4 + g:5 + g],
                                                     in1=imp.t[:, 0:NC], op0=ALU.mult, op1=ALU.add), [e2.b, rsum.b, imp.b], [imp.b])
            iv = imp.t[:, 0:NC].rearrange("p (b two) -> p b two", two=2)
            dve(lambda e: e.tensor_tensor(out=score.t[:, 0:NSB], in0=iv[:, :, 0], in1=iv[:, :, 1], op=ALU.add), [imp.b], [score.b])
            dve(lambda e: e.tensor_tensor(out=score.t[:, 0:NSB], in0=score.t[:, 0:NSB], in1=FBp.t[:, :], op=ALU.add),
                [score.b, FBp.b], [score.b])
            top_sel(n, NSB, 16)
            tr(PT32.t[0:NSB, 256:384], selb.t[:, 0:NSB], ident.t[:, :], [selb.b, ident.b], PT32.b)
            act(lambda e: e.copy(out=selT.t[0:NSB, k, :], in_=PT32.t[0:NSB, 256:384]), [PT32.b], [selT.b])
        if SUBQ <= 4:
            return
        for k in range(2):
            ks = slice(k * 64, (k + 1) * 64)
            tiles = [(KCT.t[ks, 0:NC], VC.t[0:NC, k, :], NC, [KCT.b, VC.b],
                      [(identb.t[:, 0:NC], bcast(CBp.t[:, :], 1, 4), [identb.b, CBp.b])])]
            attend(n, k, QTu, tiles, 0, True)
            if p == 0:
                dbg("ycmp%d" % k, ynsa.t[:, k, :, :].rearrange("p a b -> p (a b)"), 128, 256, ynsa.b)
            tiles = []
            for kt in range(2 * p + 2):
                b_ = [(expand.t[:, kt * 128:(kt + 1) * 128], bcast(selT.t[:, k, :], 1, 4), [expand.b, selT.b])]
                if kt >= 2 * p:
                    b_.append((identb.t[:, :], bcast(TB.t[:, kt - 2 * p, :], 1, 4), [identb.b, TB.b]))
                tiles.append((KTs.t[ks, kt * 128:(kt + 1) * 128], Vs.t[:, kt, k, :], 128, [KTs.b, Vs.b], b_))
            attend(n, k, QTr, tiles, 1, False)
            if p == 0:
                dbg("yslc%d" % k, ynsa.t[:, k, :, :].rearrange("p a b -> p (a b)"), 128, 256, ynsa.b)
            tiles = []
            for i in range(6):
                kt = 2 * p - 4 + i
                if kt < 0:
                    continue
                r8 = kt % 8
                tiles.append((KTw.t[ks, r8 * 128:(r8 + 1) * 128], Vw.t[:, r8, k, :], 128, [KTw.b, Vw.b],
                              ([] if i in (2, 3) else [(identb.t[:, :], bcast(WB.t[:, i, :], 1, 4), [identb.b, WB.b])])))
            attend(n, k, QTr, tiles, 2, False)
        if SUBQ <= 5:
            return
        if p == 0:
            dbg("ynsa", ynsa.t[:, :, :, :].rearrange("p a b c -> p (a b c)"), 128, 512, ynsa.b)
            dbg("selb", selb.t[:, 0:NSB], 128, NSB, selb.b)
            dbg("score", score.t[:, 0:NSB], 128, NSB, score.b)
        nsa_out(n)
        if p == 0:
            dbg("ymix", ymix.t[:, :], 128, 1024, ymix.b)
        finish(n, xo, pp_own[p * 128:(p + 1) * 128, :], y_own[p * 128:(p + 1) * 128, :])

    for p in range(NP):
        if STOP <= 0:
            break
        ld(xo.t[:, :], x_own[p * 128:(p + 1) * 128, :], xo.b)
        kv_block(2 * p, p)
        if STOP <= 1:
            break
        kv_block(2 * p + 1, p)
        if STOP <= 2:
            break
        q_block(p)
        if STOP <= 3:
            break
    for h in range(4):
        fw.dma("sp", lambda e: e.dma_start(out=ret_p[h], in_=Sst.t[:, h, :]), [Sst.b], [OUTB])
    fw.barrier()
    esp.close()

    ess = contextlib.ExitStack()

    class _V0:
        pass
    Q_ = lambda name, shape, dt=F32, const=False: fw.sb(ess, name, shape, dt, const)
    G = [Q_("G%d" % i, [NPG, 2048], BF16) for i in range(3)]
    idx = Q_("idx", [128, 1], I32); idx4 = Q_("idx4", [128, 8], I32)
    bT = Q_("bT", [128, 16, NPG], BF16)
    KCTs = Q_("KCTs", [128, 4, NPG], BF16); VCs = Q_("VCs", [NPG, 4, 2, 65], BF16)
    Vb = Q_("Vb", [NPG, 16, 2, 65], BF16); KTr = [Q_("KTr0", [128, 4, NPG], BF16), Q_("KTr1", [128, 4, NPG], BF16)]
    e2s = _V0(); e2s.t = e2.t[0:8, :, :].rearrange("p a b -> p (a b)")[:, 0:4 * NPG]; e2s.b = e2.b; imps = Q_("imps", [8, 4 * NPG]); scs = Q_("scs", [8, 2, 2 * NPG])
    selTs = Q_("selTs", [128, 2, 2, 8], BF16)
    WBs = Q_("WBs", [128, 8], BF16, True); NB8 = Q_("NB8", [128, 8], BF16, True)
    xs_t = _V0(); xs_t.t = xo.t[0:8, :]; xs_t.b = xo.b
    class _V:
        pass
    winK = _V(); winK.t = xt.t[:, 0:512].rearrange("p (t c) -> p t c", t=4); winK.b = xt.b
    winV = _V(); winV.t = xt.t[:, 512:1024].rearrange("p (t c) -> p t c", t=4); winV.b = xt.b
    winKb = Q_("winKb", [128, 4, 128], BF16); KTwin = Q_("KTwin", [128, 4, 128], BF16); Vwin = Q_("Vwin", [128, 4, 2, 65], BF16)
    newK = Q_("newK", [128, 2, 8], BF16); newV = Q_("newV", [8, 2, 2, 65], BF16)
    mst2 = Q_("mst2", [128, 8])
    Yn = Q_("Yn", [32, 128]); selG = Q_("selG", [32, 4, 8], F32, True)
    ld(selG.t[:], selG_d, selG.b)
    ld(mst2.t[:, :], mWBs, mst2.b)
    dve(lambda e: e.tensor_copy(out=WBs.t[:, :], in_=mst2.t[:, :]), [mst2.b], [WBs.b])
    fw.op("pool", lambda e: e.memset(NB8.t[:], 0.0), [], [NB8.b])
    fw.op("pool", lambda e: e.memset(selTs.t[:], 0.0), [], [selTs.b])
    ld(mst2.t[0:8, :], mNB8, mst2.b)
    dve(lambda e: e.tensor_copy(out=NB8.t[0:8, :], in_=mst2.t[0:8, :]), [mst2.b], [NB8.b])
    for tt in (VCs, Vb, Vwin, newV):
        fw.op("pool", lambda e: e.memset(tt.t[:], 1.0), [], [tt.b])
    gi = [0]
    rc = [0]

    def gather(cache, q4):
        g = G[gi[0] % 3]
        gi[0] += 1
        fw.dma("pool", lambda e: e.indirect_dma_start(out=g.t[:, :], out_offset=None, in_=cache,
                                                      in_offset=bass.IndirectOffsetOnAxis(ap=idx4.t[0:NPG, q4:q4 + 1], axis=0)),
               [idx4.b], [g.b])
        return g

    def sample_batch(b):
        n = 8
        ld(idx.t[0:NPG, :], ptab[b].rearrange("(p o) -> p o", o=1), idx.b)
        for q4 in range(8):
            dve(lambda e: e.tensor_scalar(out=idx4.t[0:NPG, q4:q4 + 1], in0=idx.t[0:NPG, 0:1], scalar1=8, scalar2=q4,
                                          op0=ALU.mult, op1=ALU.add), [idx.b], [idx4.b])
        ld(xs_t.t[:, :], xs[b * 8:(b + 1) * 8, :], xs_t.b)
        prefetch_fin(n, pps[b * 8:(b + 1) * 8, :])
        ld(ropet.t[0:8, :], rope_s[:, :], ropet.b)
        norm_T(xs_t, n, gmix, hT)
        proj(hT, n, C_CK, 512, PJ[0])
        act(lambda e: e.copy(out=kvst.t[0:n, 0:2, :], in_=PJ[0].t[0:n, 0:256].rearrange("p (a b) -> p a b", a=2)), [PJ[0].b], [kvst.b])
        act(lambda e: e.copy(out=kvst.t[0:n, 3, :], in_=PJ[0].t[0:n, 384:512]), [PJ[0].b], [kvst.b])
        rope(PJ[0].t[0:n, 256:384], PJ[0].b, kvst.t[0:n, 2, :], kvst.b, n, 2, 32, 256)
        proj(hT, n, C_WK, 256, PJ[1])
        act(lambda e: e.copy(out=wst.t[0:n, 1, :], in_=PJ[1].t[0:n, 128:256]), [PJ[1].b], [wst.b])
        rope(PJ[1].t[0:n, 0:128], PJ[1].b, wst.t[0:n, 0, :], wst.b, n, 2, 32, 256)
        fw.dma("sp", lambda e: e.dma_start(out=kv_s[b * 8:(b + 1) * 8, :, :], in_=kvst.t[0:n, :, :]), [kvst.b], [OUTB])
        fw.dma("sp", lambda e: e.dma_start(out=win_s[b, :, 504:512, :].rearrange("a t c -> t a c"), in_=wst.t[0:n, :, :]), [wst.b], [OUTB])
        fw.dma("sp", lambda e: e.dma_start(out=win_s[b, 0, 0:504, :], in_=win_k[b, 8:512, :]), [], [OUTB])
        fw.dma("sp", lambda e: e.dma_start(out=win_s[b, 1, 0:504, :], in_=win_v[b, 8:512, :]), [], [OUTB])
        dve(lambda e: e.tensor_copy(out=tokbf.t[0:n, 0:128], in_=kvst.t[0:n, 2, :]), [kvst.b], [tokbf.b])
        dve(lambda e: e.tensor_copy(out=tokbf.t[0:n, 128:256], in_=wst.t[0:n, 0, :]), [wst.b], [tokbf.b])
        for i in range(2):
            tr(PT.t[:, i, 0:n], tokbf.t[0:n, i * 128:(i + 1) * 128], identb.t[0:n, 0:n], [tokbf.b, identb.b], PT.b)
        act(lambda e: e.copy(out=newK.t[:, :, :], in_=PT.t[:, 0:2, 0:n]), [PT.b], [newK.b])
        dve(lambda e: e.tensor_copy(out=newV.t[0:n, 0, :, 0:64], in_=kvst.t[0:n, 3, :].rearrange("p (k d) -> p k d", k=2)), [kvst.b], [newV.b])
        dve(lambda e: e.tensor_copy(out=newV.t[0:n, 1, :, 0:64], in_=wst.t[0:n, 1, :].rearrange("p (k d) -> p k d", k=2)), [wst.b], [newV.b])
        ld(Sst.t[:, :, :], st_ret[b].rearrange("h d e -> d h e"), Sst.b)
        act(lambda e: e.copy(out=Sownb.t[:, :, :], in_=Sst.t[:, :, :]), [Sst.b], [Sownb.b])
        proj(hT, n, C_RQ, 512, PJ[0])
        rope(PJ[0].t[0:n, :], PJ[0].b, tokbf.t[0:n, 0:512], tokbf.b, n, 4, 64, 0)
        proj(hT, n, C_RK, 512, PJ[1])
        rope(PJ[1].t[0:n, :], PJ[1].b, rO.t[0:n, :], rO.b, n, 4, 64, 128)
        dve(lambda e: e.tensor_copy(out=tokbf.t[0:n, 512:1024], in_=rO.t[0:n, :]), [rO.b], [tokbf.b])
        dve(lambda e: e.tensor_tensor(out=Kd.t[0:n, :, :], in0=rO.t[0:n, :].rearrange("p (h d) -> p h d", h=4),
                                      in1=bcast(kdec8.t[0:n, 0:4], 2, 128), op=ALU.mult), [rO.b, kdec8.b], [Kd.b])
        proj(hT, n, C_RV, 512, PJ[0])
        act(lambda e: e.copy(out=vown.t[0:n, :, :].rearrange("p h d -> p (h d)"), in_=PJ[0].t[0:n, :]), [PJ[0].b], [vown.b])
        for h in range(4):
            mm(PM.t[:, h, :], Kd.t[0:n, h, :], vown.t[0:n, h, :], True, True, [Kd.b, vown.b], PM.b)
        for h in range(4):
            dve(lambda e: e.scalar_tensor_tensor(out=Sst.t[:, h, :], in0=Sst.t[:, h, :], scalar=gC8[h], in1=PM.t[:, h, :],
                                                 op0=ALU.mult, op1=ALU.add), [Sst.b, PM.b], [Sst.b])
        fw.dma("sp", lambda e: e.dma_start(out=ret_s[b].rearrange("h d e -> d h e"), in_=Sst.t[:, :, :]), [Sst.b], [OUTB])
        retention_q(n, dmask8T, qdec8T, Sownb)
        gates_and_ret_out(n)
        proj(hT, n, C_NQ, 512, PJ[1])
        nq_prep(n, 256)
        for kv in range(2):
            for q4 in range(4):
                for h2 in range(2):
                    g = gather(caches[kv], q4 * 2 + h2)
                    for j0 in range(0, 16, 4):
                        bank = (PT32, PJ[0])[rc[0] % 2]
                        rc[0] += 1
                        for jj in range(4):
                            j = j0 + jj
                            tr(bank.t[:, :].bitcast(BF16)[:, jj * 128:jj * 128 + NPG], g.t[:, j * 128:(j + 1) * 128], identb.t[0:NPG, 0:NPG],
                               [g.b, identb.b], bank.b)
                        jg0 = h2 * 16 + j0
                        dve(lambda e: e.tensor_tensor(out=bT.t[:, j0:j0 + 4, :],
                                                      in0=bank.t[:, :].bitcast(BF16)[:, 0:512].rearrange("p (a b) -> p a b", a=4)[:, :, 0:NPG],
                                                      in1=bcast(peT.t[:, kv, jg0:jg0 + 4], 2, NPG), op=ALU.add),
                            [bank.b, peT.b], [bT.b])
                    for j in range(16):
                        jg = h2 * 16 + j
                        mm(PS[0].t[:, 0:NPG], W1[kv].t[:, jg, :], bT.t[:, j, :], jg == 0, jg == 31, [W1[kv].b, bT.b], PS[0].b)
                gelu_to(PS[0].t[:, 0:NPG], PS[0].b, hidb.t[:, 0, 0:NPG], hidb.b, 128, NPG)
                if kv == 0:
                    mm(PS[1].t[:, 0:NPG], W2[0].t[:, :], hidb.t[:, 0, 0:NPG], True, True, [W2[0].b, hidb.b], PS[1].b)
                    act(lambda e: e.copy(out=KCTs.t[:, q4, :], in_=PS[1].t[:, 0:NPG]), [PS[1].b], [KCTs.b])
                else:
                    mm(PS[1].t[0:NPG, 0:128], hidb.t[:, 0, 0:NPG], W2[1].t[:, :], True, True, [W2[1].b, hidb.b], PS[1].b)
                    act(lambda e: e.copy(out=VCs.t[:, q4, :, 0:64], in_=PS[1].t[0:NPG, 0:128].rearrange("p (k d) -> p k d", k=2)),
                        [PS[1].b], [VCs.b])
        for k in range(2):
            ks = slice(k * 64, (k + 1) * 64)
            for g in range(4):
                pj = PJ[g % 2]
                mm(pj.t[0:n, 0:4 * NPG].rearrange("p (a b) -> p a b", a=4), QTu.t[ks, g, 0:n], KCTs.t[ks, :, :], True, True, [QTu.b, KCTs.b], pj.b)
                act(lambda e: e.activation(out=e2s.t[:, :], in_=pj.t[0:n, 0:4 * NPG], func=AF.Exp, scale=0.125,
                                           accum_out=rsum.t[0:n, g:g + 1]), [pj.b], [e2s.b, rsum.b])
                dve(lambda e: e.reciprocal(out=rsum.t[0:n, 4 + g:5 + g], in_=rsum.t[0:n, g:g + 1]), [rsum.b], [rsum.b])
                if g == 0:
                    dve(lambda e: e.tensor_scalar(out=imps.t[:, :], in0=e2s.t[:, :], scalar1=rsum.t[0:n, 4:5], scalar2=None, op0=ALU.mult),
                        [e2s.b, rsum.b], [imps.b])
                else:
                    dve(lambda e: e.scalar_tensor_tensor(out=imps.t[:, :], in0=e2s.t[:, :], scalar=rsum.t[0:n, 4 + g:5 + g],
                                                         in1=imps.t[:, :], op0=ALU.mult, op1=ALU.add), [e2s.b, rsum.b, imps.b], [imps.b])
            iv = imps.t[:, :].rearrange("p (hf two g) -> p hf two g", hf=2, two=2)
            dve(lambda e: e.tensor_tensor(out=scs.t[:, k, :].rearrange("p (hf g) -> p hf g", hf=2), in0=iv[:, :, 0, :], in1=iv[:, :, 1, :],
                                          op=ALU.add), [imps.b], [scs.b])
            dve(lambda e: e.tensor_scalar(out=scs.t[:, k, 0:1], in0=scs.t[:, k, 0:1], scalar1=1e4, scalar2=None, op0=ALU.add),
                [scs.b], [scs.b])
            dve(lambda e: e.tensor_copy(out=score.t[0:n, 0:2 * NPG], in_=scs.t[:, k, :]), [scs.b], [score.b])
            top_sel(n, 2 * NPG, 15)
            dve(lambda e: e.tensor_scalar(out=selb.t[0:n, 0:2 * NPG], in0=selb.t[0:n, 0:2 * NPG], scalar1=0.0, scalar2=None,
                                          op0=ALU.is_equal), [selb.b], [selb.b])
            for hf in range(2):
                tr(PT32.t[0:NPG, (hf * 2 + k) * 8:(hf * 2 + k) * 8 + 8], selb.t[0:n, hf * NPG:(hf + 1) * NPG], ident.t[0:n, 0:n],
                   [selb.b, ident.b], PT32.b)
        act(lambda e: e.copy(out=selTs.t[0:NPG, :, :, :].rearrange("p a b c -> p (a b c)"), in_=PT32.t[0:NPG, 0:32]), [PT32.b], [selTs.b])
        ld(winK.t[:, :, :], win_k[b].rearrange("(t p) c -> p t c", p=128), winK.b)
        ld(winV.t[:, :, :], win_v[b].rearrange("(t p) c -> p t c", p=128), winV.b)
        dve(lambda e: e.tensor_copy(out=winKb.t[:, :, :], in_=winK.t[:, :, :]), [winK.b], [winKb.b])
        for i in range(4):
            tr(PT.t[:, i, :], winKb.t[:, i, :], identb.t[:, :], [winKb.b, identb.b], PT.b)
        act(lambda e: e.copy(out=KTwin.t[:, :, :], in_=PT.t[:, 0:4, :]), [PT.b], [KTwin.b])
        dve(lambda e: e.tensor_copy(out=Vwin.t[:, :, :, 0:64], in_=winV.t[:, :, :].rearrange("p t (k d) -> p t k d", k=2)), [winV.b], [Vwin.b])
        for k in range(2):
            ks = slice(k * 64, (k + 1) * 64)
            tiles = [(KCTs.t[ks, q4, :], VCs.t[:, q4, k, :], NPG, [KCTs.b, VCs.b], []) for q4 in range(4)]
            attend(n, k, QTu, tiles, 0, True)
            tiles = [(KTwin.t[ks, i, :], Vwin.t[:, i, k, :], 128, [KTwin.b, Vwin.b],
                      ([(identb.t[:, :], bcast(WBs.t[:, :], 1, 4), [identb.b, WBs.b])] if i == 0 else [])) for i in range(4)]
            tiles.append((newK.t[ks, 1, :], newV.t[0:8, 1, k, :], 8, [newK.b, newV.b],
                          [(identb.t[:, 0:8], bcast(NB8.t[:, :], 1, 4), [identb.b, NB8.b])]))
            attend(n, k, QTr, tiles, 2, False)
        slc_rows(b, n)

    def slc_rows(b, n):
        mm(PO.t[0:32, 0:130], zerob.t[0:1, 0:32], zerob.t[0:1, 0:130], True, False, [zerob.b], PO.b)
        rounds = [(o, r0) for o in range(8) for r0 in range(0, 16, 4)]
        gks = {}

        def T_(ri):
            o, r0 = rounds[ri]
            if r0 == 0:
                gks[o] = gather(caches[2], o)
                gv = gather(caches[3], o)
                dve(lambda e: e.tensor_copy(out=Vb.t[:, :, :, 0:64], in_=gv.t[:, :].rearrange("p (r k d) -> p r k d", r=16, k=2)),
                    [gv.b], [Vb.b])
            gk = gks[o]
            bank = (PT32, PJ[1])[ri % 2]
            ktr = KTr[ri % 2]
            for rr in range(4):
                r = r0 + rr
                tr(bank.t[:, :].bitcast(BF16)[:, rr * 128:rr * 128 + NPG], gk.t[:, r * 128:(r + 1) * 128], identb.t[0:NPG, 0:NPG], [gk.b, identb.b], bank.b)
            act(lambda e: e.copy(out=ktr.t[:, 0:4, :], in_=bank.t[:, :].bitcast(BF16)[:, 0:512].rearrange("p (a b) -> p a b", a=4)[:, :, 0:NPG]),
                [bank.b], [ktr.b])

        def S_(ri):
            o, r0 = rounds[ri]
            hf = o // 4
            ktr = KTr[ri % 2]
            for k in range(2):
                ks = slice(k * 64, (k + 1) * 64)
                for rr in range(4):
                    outv = PS[k].t[0:NPG, rr * 32:rr * 32 + 32].rearrange("p (g q) -> p g q", g=4)
                    mm(outv, ktr.t[ks, rr, :], QTr.t[ks, :, 0:n], True, True, [ktr.b, QTr.b], PS[k].b)
                act(lambda e: e.activation(out=ET[k].t[0:NPG, 0:128], in_=PS[k].t[0:NPG, 0:128], func=AF.Exp, scale=0.125),
                    [PS[k].b], [ET[k].b])
                ev = ET[k].t[0:NPG, 0:128].rearrange("p (r g q) -> p r g q", r=4, g=4)
                dve(lambda e: e.tensor_tensor(out=ev, in0=ev, in1=bcast(bcast(selTs.t[0:NPG, hf, k, :], 1, 4), 1, 4), op=ALU.mult),
                    [ET[k].b, selTs.b], [ET[k].b])

        def PV_(ri):
            o, r0 = rounds[ri]
            for rr in range(4):
                r = r0 + rr
                for k in range(2):
                    mm(PO.t[0:32, k * 65:(k + 1) * 65], ET[k].t[0:NPG, rr * 32:(rr + 1) * 32], Vb.t[:, r, k, :], False, False,
                       [ET[k].b, Vb.b], PO.b)
        T_(0)
        for ri in range(len(rounds)):
            S_(ri)
            nxt_same = ri + 1 < len(rounds) and rounds[ri + 1][1] != 0
            if nxt_same:
                T_(ri + 1)
            PV_(ri)
            if ri + 1 < len(rounds) and not nxt_same:
                T_(ri + 1)
        for k in range(2):
            ks = slice(k * 64, (k + 1) * 64)
            outv = PS[k].t[0:8, 0:32].rearrange("p (g q) -> p g q", g=4)
            mm(outv, newK.t[ks, 0, :], QTr.t[ks, :, 0:n], True, False, [newK.b, QTr.b], PS[k].b)
            mm(outv, identb.t[:, 0:8], bcast(NB8.t[:, :], 1, 4), False, True, [identb.b, NB8.b], PS[k].b)
            act(lambda e: e.activation(out=ET[k].t[0:8, 0:32], in_=PS[k].t[0:8, 0:32], func=AF.Exp, scale=0.125), [PS[k].b], [ET[k].b])
            mm(PO.t[0:32, k * 65:(k + 1) * 65], ET[k].t[0:8, 0:32], newV.t[0:8, 0, k, :], False, True, [ET[k].b, newV.b], PO.b)
        pov = PO.t[0:32, 0:130].rearrange("p (k c) -> p k c", k=2)
        dve(lambda e: e.tensor_scalar(out=fac.t[0:32, 0:2], in0=pov[:, :, 64], scalar1=1e-30, scalar2=None, op0=ALU.add), [PO.b], [fac.b])
        dve(lambda e: e.reciprocal(out=fac.t[0:32, 0:2], in_=fac.t[0:32, 0:2]), [fac.b], [fac.b])
        dve(lambda e: e.tensor_tensor(out=Yn.t[:, :].rearrange("p (k d) -> p k d", k=2), in0=pov[:, :, 0:64],
                                      in1=bcast(fac.t[0:32, 0:2], 2, 64), op=ALU.mult), [PO.b, fac.b], [Yn.b])
        for g in range(4):
            mm(PJ[0].t[0:8, g * 128:(g + 1) * 128], selG.t[:, g, :], Yn.t[:, :], True, True, [selG.b, Yn.b], PJ[0].b)
        pjv = PJ[0].t[0:8, :].rearrange("p (g k d) -> p g k d", g=4, k=2)
        for k in range(2):
            dve(lambda e: e.tensor_tensor(out=ytmp.t[0:n, :, :], in0=pjv[:, :, k, :], in1=bcast(sig.t[0:n, 8 + k * 4:8 + k * 4 + 4], 2, 64),
                                          op=ALU.mult), [PJ[0].b, sig.b], [ytmp.b])
            dve(lambda e: e.tensor_tensor(out=ynsa.t[0:n, k, :, :], in0=ynsa.t[0:n, k, :, :], in1=ytmp.t[0:n, :, :],
                                          op=ALU.add), [ynsa.b, ytmp.b], [ynsa.b])
        nsa_out(n)
        finish(n, xs_t, pps[b * 8:(b + 1) * 8, :], y_s[b * 8:(b + 1) * 8, :])

    for b in range(SB_PER_CORE):
        if STOP <= 4:
            break
        sample_batch(b)
        if STOP <= 5:
            break
    fw.barrier()
    ess.close()
    es0.close()
    return nc


def _consts(SEQ, NPG, h):
    NB = SEQ // 128; NP = NB // 2; NSB = NB * 2
    c = {}

    def rope_tab(pos):
        pos = np.asarray(pos, np.float32)
        out = np.zeros((len(pos), 320), np.float32)
        inv64 = (10000.0 ** (-np.arange(64, dtype=np.float32) / 64)).astype(np.float32)
        inv32 = (10000.0 ** (-np.arange(32, dtype=np.float32) / 32)).astype(np.float32)
        a64 = pos[:, None] * inv64[None, :]
        a32 = pos[:, None] * inv32[None, :]
        out[:, 0:64] = np.cos(a64); out[:, 64:128] = np.sin(a64)
        out[:, 128:192] = np.cos(a64) * np.float32(128 ** -0.5); out[:, 192:256] = np.sin(a64) * np.float32(128 ** -0.5)
        out[:, 256:288] = np.cos(a32); out[:, 288:320] = np.sin(a32)
        return out
    c["rope_kv"] = rope_tab(np.arange(SEQ))
    own_pos = np.concatenate([np.arange(128) + (2 * p + h) * 128 for p in range(NP)])
    c["rope_own"] = rope_tab(own_pos)
    c["rope_s"] = rope_tab(NPG * 128 + np.arange(8))
    ki = np.arange(128)[:, None]; qi = np.arange(128)[None, :]
    tri = np.where(ki <= qi, 0.0, NEG).astype(np.float32)
    tri2 = np.where(ki >= qi, 0.0, NEG).astype(np.float32)
    Z = np.zeros((128, 128), np.float32); M = np.full((128, 128), NEG, np.float32)
    TB = [tri, M] if h == 0 else [Z, tri]
    WB = [tri2, Z, Z, Z, tri, M] if h == 0 else [M, tri2, Z, Z, Z, tri]
    c["mTB"] = np.stack(TB, 1); c["mWB"] = np.stack(WB, 1)
    CB = np.zeros((NP, 128, 128), np.float32); FB = np.zeros((NP, 128, NSB), np.float32)
    cc = np.arange(128)[:, None]
    for p in range(NP):
        j = 2 * p + h
        qpos = j * 128 + np.arange(128)[None, :]
        CB[p] = np.where(32 * cc + 31 <= qpos, 0.0, NEG)
        FB[p, :, 0] += 1e4
        cur = (j * 128 + np.arange(128)) // 64
        FB[p, np.arange(128), cur] += 1e4
    c["mCB"] = CB; c["mCBq"] = np.ascontiguousarray(CB.transpose(0, 2, 1)); c["mFB"] = FB
    g = (1.0 - 2.0 ** (-5.0 - np.arange(4, dtype=np.float64)))
    hs = np.zeros((128, 8), np.float32)
    if h == 0:
        hs[:, 0:4] = 1.0
    else:
        hs[:, 0:4] = (g ** 128)[None, :]; hs[:, 4:8] = 1.0
    c["hsel"] = hs

    def decs(C):
        i = np.arange(C)
        dm = np.zeros((C, 4, C)); qd = np.zeros((128, 4, C)); kd = np.zeros((C, 4))
        for hh in range(4):
            d = i[None, :] - i[:, None]
            dm[:, hh, :] = np.where(d >= 0, g[hh] ** np.maximum(d, 0), 0.0)
            qd[:, hh, :] = (g[hh] ** (i + 1.0))[None, :]
            kd[:, hh] = g[hh] ** (C - 1.0 - i)
        return dm.astype(np.float32), qd.astype(np.float32), kd.astype(np.float32)
    c["dmaskT"], c["qdecT"], c["kdec"] = decs(128)
    c["dmask8T"], c["qdec8T"], c["kdec8"] = decs(8)
    c["expand"] = (np.arange(SEQ)[None, :] // 64 == np.arange(NSB)[:, None]).astype(np.float32)
    c["mWBs"] = np.where(np.arange(128)[:, None] >= np.arange(8)[None, :], 0.0, NEG).astype(np.float32)
    c["selG"] = np.ascontiguousarray((np.arange(32)[:, None, None] == (np.arange(4)[None, :, None] * 8 + np.arange(8)[None, None, :])).astype(np.float32))
    c["mNB8"] = np.where(np.arange(8)[:, None] <= np.arange(8)[None, :], 0.0, NEG).astype(np.float32)
    return c


_NC_CACHE = {}


def run(inputs, SEQ, NPG, NPOOL):
    f = lambda a: np.ascontiguousarray(np.asarray(a))
    NB = SEQ // 128; NP = NB // 2
    key = (SEQ, NPG, NPOOL)
    if key not in _NC_CACHE:
        import os
        _NC_CACHE[key] = build(SEQ, NPG, NPOOL, int(os.environ.get('KSTOP', '99')))
    nc = _NC_CACHE[key]
    shared = {
        "w_in": f(inputs["w_in"][0]), "w_out": f(inputs["w_out"][0]), "w_gate": f(inputs["w_ple_gate"][0]),
        "w_ple": f(inputs["w_ple"][0]), "norm_mix": f(inputs["norm_mix"][0]), "norm_ple": f(inputs["norm_ple"][0]),
        "norm_f": f(inputs["norm_f"]), "gn_g": f(inputs["ret_gn_g"][0]), "gn_b": f(inputs["ret_gn_b"][0]),
        "pe_k": f(inputs["cmp_pe_k"][0]).reshape(32, 128), "pe_v": f(inputs["cmp_pe_v"][0]).reshape(32, 128),
        "w1_k": f(inputs["cmp_w1_k"][0]), "w1_v": f(inputs["cmp_w1_v"][0]),
        "w2_k": f(inputs["cmp_w2_k"][0]), "w2_v": f(inputs["cmp_w2_v"][0]),
        "c_ck": f(inputs["cache_cmp_k"][0]).reshape(NPOOL * 8, 2048), "c_cv": f(inputs["cache_cmp_v"][0]).reshape(NPOOL * 8, 2048),
        "c_sk": f(inputs["cache_slc_k"][0]).reshape(NPOOL * 8, 2048), "c_sv": f(inputs["cache_slc_v"][0]).reshape(NPOOL * 8, 2048),
    }
    consts = [_consts(SEQ, NPG, 0), _consts(SEQ, NPG, 1)]
    xp = np.asarray(inputs["x_prompt"]); pp = np.asarray(inputs["p_prompt"])[0]
    in_maps = []
    for core in range(8):
        b, h = core // 2, core % 2
        m = dict(shared)
        m.update(consts[h])
        m["xb"] = f(xp[b])
        m["x_own"] = f(xp[b].reshape(NP, 2, 128, 1024)[:, h].reshape(NP * 128, 1024))
        m["pp_own"] = f(pp[b].reshape(NP, 2, 128, 256)[:, h].reshape(NP * 128, 256))
        sb = slice(core * 4, core * 4 + 4)
        m["xs"] = f(np.asarray(inputs["x_sample"])[sb].reshape(32, 1024))
        m["pps"] = f(np.asarray(inputs["p_sample"])[0, sb].reshape(32, 256))
        m["win_k"] = f(np.asarray(inputs["state_win_k"])[0, sb].reshape(4, 512, 128))
        m["win_v"] = f(np.asarray(inputs["state_win_v"])[0, sb].reshape(4, 512, 128))
        m["st_ret"] = f(np.asarray(inputs["state_ret"])[0, sb])
        m["ptab"] = f(np.asarray(inputs["page_table"])[sb].astype(np.int32))
        in_maps.append(m)
    import os
    if os.environ.get("KTRACE"):
        rr_ = run_bass_kernel_spmd(nc, in_maps, core_ids=list(range(8)), trace=True)
        print("EXEC_TIME_NS", rr_.exec_time_ns)
        res = rr_.results
    else:
        res = run_bass_kernel_spmd(nc, in_maps, core_ids=list(range(8))).results
    global LAST_RES
    LAST_RES = res
    B = 4
    y_prompt = np.zeros((B, SEQ, 1024), np.float32)
    ret_p = np.zeros((1, B, 4, 128, 128), np.float32)
    kvp = np.zeros((B, SEQ, 4, 128), np.float32)
    winp = np.zeros((B, 512, 2, 128), np.float32)
    y_s = np.zeros((32, 8, 1024), np.float32); ret_s = np.zeros((1, 32, 4, 128, 128), np.float32)
    kvs = np.zeros((32, 8, 4, 128), np.float32); wins = np.zeros((32, 2, 512, 128), np.float32)
    for core in range(8):
        b, h = core // 2, core % 2
        r = res[core]
        y_prompt[b].reshape(NP, 2, 128, 1024)[:, h] = r["y_own"].reshape(NP, 128, 1024)
        if h == 0:
            ret_p[0, b] = r["ret_p"]; kvp[b] = r["kvout"]; winp[b] = r["winout"]
        sb = slice(core * 4, core * 4 + 4)
        y_s[sb] = r["y_s"].reshape(4, 8, 1024); ret_s[0, sb] = r["ret_s"]
        kvs[sb] = r["kv_s"].reshape(4, 8, 4, 128); wins[sb] = r["win_s"]
    sh = lambda a: np.ascontiguousarray(a)[None].reshape((1,) + a.shape[:-1] + (2, 64))
    return (y_prompt, y_s, ret_p,
            sh(kvp[:, :, 0]), sh(kvp[:, :, 1]), sh(kvp[:, :, 2]), sh(kvp[:, :, 3]),
            sh(winp[:, :, 0]), sh(winp[:, :, 1]),
            ret_s, sh(kvs[:, :, 0]), sh(kvs[:, :, 1]), sh(kvs[:, :, 2]), sh(kvs[:, :, 3]),
            sh(wins[:, 0]), sh(wins[:, 1]))


def kernel(**inputs):
    SEQ = inputs["x_prompt"].shape[1]
    NPG = inputs["page_table"].shape[1]
    NPOOL = inputs["cache_cmp_k"].shape[1]
    return run(inputs, SEQ, NPG, NPOOL)
```

```python
import contextlib
import numpy as np
import concourse.bass as bass
import concourse.mybir as mybir
from concourse.bass_utils import run_bass_kernel_spmd

F32 = mybir.dt.float32
BF16 = mybir.dt.bfloat16
I32 = mybir.dt.int32
AF = mybir.ActivationFunctionType
ALU = mybir.AluOpType
AX = mybir.AxisListType

NEG = -30000.0
DEC_T = 8
SB_PER_CORE = 4
C_RQ, C_RK, C_RV, C_RG, C_NQ, C_CK, C_CV, C_SK, C_SV, C_WK, C_WV, C_NGL, C_NG = (
    0, 512, 1024, 1536, 2048, 2560, 2688, 2816, 2944, 3072, 3200, 3328, 3352)
PROJ = 3864


class Buf:
    __slots__ = ("w", "r", "const", "psum")

    def __init__(self, const=False):
        self.w = None
        self.r = []
        self.const = const
        self.psum = False


class T:
    def __init__(self, t, const=False):
        self.t = t
        self.b = Buf(const)


def bcast(ap, pos, n):
    l = [list(x) for x in ap.ap]
    l.insert(pos, [0, n])
    return bass.AP(tensor=ap.tensor, offset=ap.offset, ap=l)


import os as _os
NOSAME = bool(int(_os.environ.get('KNOSAME', '0')))


class FW:
    NDMA = 24

    def __init__(self, nc, es):
        self.nc = nc
        self.eng = {"pe": nc.tensor, "act": nc.scalar, "dve": nc.vector,
                    "pool": nc.gpsimd, "sp": nc.sync}
        self.sem = {k: es.enter_context(nc.semaphore("sem_" + k)) for k in self.eng}
        self.cnt = {k: 0 for k in self.eng}
        self.dsem = [es.enter_context(nc.semaphore("dsem%d" % i)) for i in range(self.NDMA)]
        self.dcnt = [0] * self.NDMA
        self.dnext = 0
        self.waited = {k: {} for k in self.eng}

    def sb(self, es, name, shape, dt, const=False):
        return T(es.enter_context(self.nc.sbuf_tensor("s_" + name, list(shape), dt)), const)

    def ps(self, es, name, shape, dt):
        t = T(es.enter_context(self.nc.psum_tensor("p_" + name, list(shape), dt)))
        t.b.psum = True
        return t

    def _wait(self, e, dep):
        sem, val = dep
        w = self.waited[e]
        key = id(sem)
        if w.get(key, 0) >= val:
            return
        self.eng[e].wait_ge(sem, val)
        w[key] = val

    def _deps(self, e, reads, writes):
        mysem = self.sem[e]
        nosame = NOSAME
        for b in reads:
            if b.w is not None and not ((e == "pe" or nosame) and b.w[0] is mysem):
                self._wait(e, b.w)
            if b.psum:
                for d in b.r:
                    if d[0] is not mysem:
                        self._wait(e, d)
        for b in writes:
            if b.w is not None and not ((e == "pe" or nosame) and b.w[0] is mysem):
                self._wait(e, b.w)
            for d in b.r:
                if d[0] is not mysem:
                    self._wait(e, d)

    def _rec(self, tok, reads, writes):
        for b in reads:
            if not b.const:
                b.r.append(tok)
                if len(b.r) > 64:
                    last = {}
                    for d in b.r:
                        last[id(d[0])] = d
                    b.r = list(last.values())
        for b in writes:
            b.w = tok
            b.r = []

    def op(self, e, fn, reads=(), writes=()):
        self._deps(e, reads, writes)
        ins = fn(self.eng[e])
        self.cnt[e] += 1
        ins.then_inc(self.sem[e], 1)
        self._rec((self.sem[e], self.cnt[e]), reads, writes)

    def dma(self, q, fn, reads=(), writes=()):
        i = self.dnext
        self.dnext = (self.dnext + 1) % self.NDMA
        sem = self.dsem[i]
        if self.dcnt[i] > 0:
            self._wait(q, (sem, self.dcnt[i]))
        self._deps(q, reads, writes)
        ins = fn(self.eng[q])
        self.dcnt[i] += 16
        ins.then_inc(sem, 16)
        self._rec((sem, self.dcnt[i]), reads, writes)

    def barrier(self):
        for e in self.eng:
            for p in self.eng:
                if p != e and self.cnt[p] > 0:
                    self._wait(e, (self.sem[p], self.cnt[p]))
            for i in range(self.NDMA):
                if self.dcnt[i] > 0:
                    self._wait(e, (self.dsem[i], self.dcnt[i]))


def build(SEQ, NPG, NPOOL, STOP=99):
    import os
    SUB = int(os.environ.get('KSUB', '99'))
    SUB5 = int(os.environ.get('KSUB5', '99'))
    SUBQ = int(os.environ.get('KSUBQ', '99'))
    KDBG = int(os.environ.get('KDBG', '0'))
    NB = SEQ // 128
    NP = NB // 2
    NPo = NP * 128
    NC = NB * 4
    NSB = NB * 2
    assert NC <= 128 and NSB > 16 and 2 * NPG >= 16
    nc = bass.Bass("TRN2", target_bir_lowering=False)
    es0 = contextlib.ExitStack()
    fw = FW(nc, es0)

    def din(name, shape, dt=F32):
        return nc.dram_tensor(name, list(shape), dt, kind="ExternalInput").ap()

    def dout(name, shape, dt=F32):
        return nc.dram_tensor(name, list(shape), dt, kind="ExternalOutput").ap()

    xb = din("xb", [SEQ, 1024]); x_own = din("x_own", [NPo, 1024]); pp_own = din("pp_own", [NPo, 256])
    rope_kv = din("rope_kv", [SEQ, 320]); rope_own = din("rope_own", [NPo, 320]); rope_s = din("rope_s", [8, 320])
    xs = din("xs", [32, 1024]); pps = din("pps", [32, 256])
    caches = [din(n, [NPOOL * 8, 2048]) for n in ("c_ck", "c_cv", "c_sk", "c_sv")]
    win_k = din("win_k", [4, 512, 128]); win_v = din("win_v", [4, 512, 128])
    st_ret = din("st_ret", [4, 4, 128, 128]); ptab = din("ptab", [4, NPG], I32)
    w_in = din("w_in", [1024, PROJ]); w_out = din("w_out", [1024, 1024]); w_gate = din("w_gate", [1024, 1024])
    w_ple = din("w_ple", [256, 1024])
    norm_mix = din("norm_mix", [1024]); norm_ple = din("norm_ple", [1024]); norm_f = din("norm_f", [1024])
    gn_g = din("gn_g", [512]); gn_b = din("gn_b", [512])
    pe_kv = [din("pe_k", [32, 128]), din("pe_v", [32, 128])]
    w1_kv = [din("w1_k", [2, 32, 64, 64]), din("w1_v", [2, 32, 64, 64])]
    w2_kv = [din("w2_k", [2, 64, 64]), din("w2_v", [2, 64, 64])]
    mTB = din("mTB", [128, 2, 128]); mWB = din("mWB", [128, 6, 128])
    mCB = din("mCB", [NP, 128, 128]); mCBq = din("mCBq", [NP, 128, 128]); mFB = din("mFB", [NP, 128, NSB])
    hsel_d = din("hsel", [128, 8])
    dmaskT_d = din("dmaskT", [128, 4, 128]); qdecT_d = din("qdecT", [128, 4, 128]); kdec_d = din("kdec", [128, 4])
    dmask8T_d = din("dmask8T", [8, 4, 8]); qdec8T_d = din("qdec8T", [128, 4, 8]); kdec8_d = din("kdec8", [8, 4])
    selG_d = din("selG", [32, 4, 8]); expand_d = din("expand", [NSB, SEQ]); mWBs = din("mWBs", [128, 8]); mNB8 = din("mNB8", [8, 8])
    gC = [float((1.0 - 2.0 ** (-5.0 - h)) ** 128) for h in range(4)]
    gC8 = [float((1.0 - 2.0 ** (-5.0 - h)) ** 8) for h in range(4)]

    y_own = dout("y_own", [NPo, 1024]); ret_p = dout("ret_p", [4, 128, 128])
    kvout = dout("kvout", [SEQ, 4, 128]); winout = dout("winout", [512, 2, 128])
    y_s = dout("y_s", [32, 1024]); ret_s = dout("ret_s", [4, 4, 128, 128])
    kv_s = dout("kv_s", [32, 4, 128]); win_s = dout("win_s", [4, 2, 512, 128])
    wscr = nc.dram_tensor("wscr", [2, 1024, 1024], BF16, kind="Internal").ap()
    OUTB = Buf()
    SCRB = Buf()

    dbgst = [None]

    def dbg(name, src, n, cols, rb):
        if not KDBG:
            return
        d = dout("dbg_" + name, [n, cols])
        st = dbgst[0]
        fw.op("dve", lambda e: e.tensor_copy(out=st.t[0:n, 0:cols], in_=src), [rb], [st.b])
        fw.dma("sp", lambda e: e.dma_start(out=d, in_=st.t[0:n, 0:cols]), [st.b], [OUTB])

    def mm(out, lhsT, rhs, start, stop, reads, wb):
        fw.op("pe", lambda e: e.matmul(out, lhsT=lhsT, rhs=rhs, start=start, stop=stop,
                                       skip_group_check=True), reads, [wb])

    def tr(out, in_, ident, reads, wb):
        fw.op("pe", lambda e: e.transpose(out=out, in_=in_, identity=ident), reads, [wb])

    def dve(fn, reads, writes):
        fw.op("dve", fn, reads, writes)

    def act(fn, reads, writes):
        fw.op("act", fn, reads, writes)

    def ld(out, in_, wb, q="sp", reads=()):
        fw.dma(q, lambda e: e.dma_start(out=out, in_=in_), reads, [wb])

    S = lambda name, shape, dt=F32, const=False: fw.sb(es0, name, shape, dt, const)
    Wi = S("Wi", [128, 8, PROJ], BF16, True)
    Wp = S("Wp", [128, 2, 1024], BF16, True)
    W1 = [S("W1k", [128, 32, 128], BF16, True), S("W1v", [128, 32, 128], BF16, True)]
    W2 = [S("W2k", [128, 128], BF16, True), S("W2v", [128, 128], BF16, True)]
    peT = S("peT", [128, 2, 32], F32, True)
    gmix = S("gmix", [128, 8], F32, True); gple = S("gple", [128, 8], F32, True)
    gf_rep = S("gf_rep", [128, 1024], F32, True)
    gng_rep = S("gng_rep", [128, 512], F32, True); gnb_rep = S("gnb_rep", [128, 512], F32, True)
    ident = S("ident", [128, 128], F32, True); identb = S("identb", [128, 128], BF16, True)
    zerob = S("zerob", [1, 512], BF16, True)
    dmaskT = S("dmaskT", [128, 4, 128], F32, True); qdecT = S("qdecT", [128, 4, 128], F32, True)
    kdec = S("kdec", [128, 4], F32, True); hsel = S("hsel", [128, 8], F32, True)
    dmask8T = S("dmask8T", [8, 4, 8], F32, True); qdec8T = S("qdec8T", [128, 4, 8], F32, True)
    kdec8 = S("kdec8", [8, 4], F32, True)
    wb = [S("wbA", [128, 8, 512], BF16), S("wbB", [128, 8, 512], BF16)]
    wbuf = wb[0]
    xt = S("xt", [128, 1024]); xo = S("xo", [128, 1024]); xsbf = S("xsbf", [128, 1024], BF16)
    hT = S("hT", [128, 8, 128], BF16); hT2 = S("hT2", [128, 8, 128], BF16)
    ss = S("ss", [128, 8]); ropet = S("ropet", [128, 320])
    rT1 = S("rT1", [128, 512]); rT2 = S("rT2", [128, 512]); rO = S("rO", [128, 512])
    tokbf = S("tokbf", [128, 1024], BF16)
    kvst = S("kvst", [128, 4, 128]); wst = S("wst", [128, 2, 128])
    Kd = S("Kd", [128, 4, 128], BF16); Vbf = S("Vbf", [128, 4, 128], BF16)
    Sst = S("Sst", [128, 4, 128]); Sown = S("Sown", [128, 4, 128]); Sownb = S("Sownb", [128, 4, 128], BF16)
    blkT = S("blkT", [128, 2, 128], BF16)
    gl = [S("gl%d" % i, [128, 128]) for i in range(3)]
    hidb = S("hidb", [128, 2, 128], BF16)
    vcst = S("vcst", [128, 128], BF16)
    qT = S("qT", [128, 4, 128], BF16); qdT = S("qdT", [128, 4, 128], BF16); kT = S("kT", [128, 4, 128], BF16)
    vown = S("vown", [128, 4, 128], BF16); sTm = S("sTm", [128, 4, 128], BF16)
    QTu = S("QTu", [128, 4, 128], BF16); QTr = S("QTr", [128, 4, 128], BF16)
    sig = S("sig", [128, 24]); sgn = S("sgn", [128, 512])
    ET = [S("ET0", [128, 512], BF16), S("ET1", [128, 512], BF16)]
    e2 = S("e2", [128, 4, 128]); rsum = S("rsum", [128, 8]); imp = S("imp", [128, 128])
    score = S("score", [128, 256]); sc2 = S("sc2", [128, 256]); m8 = S("m8", [128, 16]); selb = S("selb", [128, 256])
    selT = S("selT", [128, 2, 128], BF16)
    ynsa = S("ynsa", [128, 2, 4, 64]); ytmp = S("ytmp", [128, 4, 64]); fac = S("fac", [128, 8])
    ymix = S("ymix", [128, 1024], BF16)
    ppt = S("ppt", [128, 256]); ppb = S("ppb", [128, 256], BF16); ppT = S("ppT", [128, 2, 128], BF16)
    gst = S("gst", [128, 8])
    if KDBG:
        dbgst[0] = S("dbgst", [128, 1024])

    PJ = [fw.ps(es0, "PJ0", [128, 512], F32), fw.ps(es0, "PJ1", [128, 512], F32)]
    PT = fw.ps(es0, "PT", [128, 8, 128], BF16)
    PT32 = fw.ps(es0, "PT32", [128, 512], F32)
    PS = [fw.ps(es0, "PS0", [128, 512], F32), fw.ps(es0, "PS1", [128, 512], F32)]
    PO = fw.ps(es0, "PO", [128, 512], F32)
    PM = fw.ps(es0, "PM", [128, 4, 128], F32)

    esw = contextlib.ExitStack()
    stg = fw.sb(esw, "stg", [128, PROJ], F32)
    fw.op("pool", lambda e: e.memset(ident.t[:], 1.0), [], [ident.b])
    fw.op("pool", lambda e: e.affine_select(out=ident.t[:], in_=ident.t[:], pattern=[[-1, 128]],
                                            compare_op=ALU.is_equal, fill=0.0, base=0, channel_multiplier=1),
          [ident.b], [ident.b])
    dve(lambda e: e.tensor_copy(out=identb.t[:], in_=ident.t[:]), [ident.b], [identb.b])
    fw.op("pool", lambda e: e.memset(zerob.t[:], 0.0), [], [zerob.b])
    stgB = Buf()
    halves = [(0, stg.b), (1932, stgB)]
    for kc in range(8):
        for c0, hb in halves:
            ld(stg.t[:, c0:c0 + 1932], w_in[kc * 128:(kc + 1) * 128, c0:c0 + 1932], hb)
            act(lambda e: e.copy(out=Wi.t[:, kc, c0:c0 + 1932], in_=stg.t[:, c0:c0 + 1932]), [hb], [Wi.b])
    for wi_, wsrc in enumerate((w_out, w_gate)):
        for nn in range(2):
            for kc in range(8):
                r0, hb = halves[kc % 2]
                ld(stg.t[:, r0:r0 + 512], wsrc[kc * 128:(kc + 1) * 128, nn * 512:(nn + 1) * 512], hb)
                dve(lambda e: e.tensor_copy(out=wbuf.t[:, kc, :], in_=stg.t[:, r0:r0 + 512]), [hb], [wbuf.b])
            fw.dma("sp", lambda e: e.dma_start(out=wscr[wi_][:, nn * 512:(nn + 1) * 512].rearrange("(k p) n -> p k n", p=128),
                                               in_=wbuf.t[:, :, :]), [wbuf.b], [SCRB])
    for kc in range(2):
        r0, hb = halves[kc % 2]
        ld(stg.t[:, r0:r0 + 1024], w_ple[kc * 128:(kc + 1) * 128, :], hb)
        dve(lambda e: e.tensor_copy(out=Wp.t[:, kc, :], in_=stg.t[:, r0:r0 + 1024]), [hb], [Wp.b])
    for kv in range(2):
        for jh2 in range(2):
            fw.op("pool", lambda e: e.memset(stg.t[:, 0:2048], 0.0), [], [stg.b, stgB])
            for k in range(2):
                for jq in range(2):
                    j0 = jh2 * 16 + jq * 8
                    fw.dma("sp", lambda e: e.dma_start(
                        out=stg.t[k * 64:(k + 1) * 64, 0:2048].rearrange("p (j m) -> p j m", j=16)[:, jq * 8:(jq + 1) * 8, k * 64:(k + 1) * 64],
                        in_=w1_kv[kv][k, j0:j0 + 8].rearrange("j d e -> d j e")), [], [stg.b])
            dve(lambda e: e.tensor_copy(out=W1[kv].t[:, jh2 * 16:(jh2 + 1) * 16, :].rearrange("p j m -> p (j m)"), in_=stg.t[:, 0:2048]),
                [stg.b], [W1[kv].b])
        fw.op("pool", lambda e: e.memset(stg.t[:, 0:128], 0.0), [], [stg.b])
        for k in range(2):
            ld(stg.t[k * 64:(k + 1) * 64, k * 64:(k + 1) * 64], w2_kv[kv][k], stg.b)
        dve(lambda e: e.tensor_copy(out=W2[kv].t[:, :], in_=stg.t[:, 0:128]), [stg.b], [W2[kv].b])
        ld(stg.t[0:32, 0:128], pe_kv[kv], stg.b)
        tr(PT32.t[:, 0:32], stg.t[0:32, 0:128], ident.t[0:32, 0:32], [stg.b, ident.b], PT32.b)
        act(lambda e: e.copy(out=peT.t[:, kv, :], in_=PT32.t[:, 0:32]), [PT32.b], [peT.b])
    for gt_, gd_ in ((gmix, norm_mix), (gple, norm_ple)):
        ld(stg.t[0:8, 0:128], gd_.rearrange("(k p) -> k p", p=128), stg.b)
        tr(PT32.t[:, 0:8], stg.t[0:8, 0:128], ident.t[0:8, 0:8], [stg.b, ident.b], PT32.b)
        act(lambda e: e.copy(out=gt_.t[:, :], in_=PT32.t[:, 0:8]), [PT32.b], [gt_.b])

    def rep(v, n):
        return bass.AP(tensor=v.tensor, offset=v.offset, ap=[[0, 128], [1, n]])
    ld(gf_rep.t[:, :], rep(norm_f, 1024), gf_rep.b)
    ld(gng_rep.t[:, :], rep(gn_g, 512), gng_rep.b)
    ld(gnb_rep.t[:, :], rep(gn_b, 512), gnb_rep.b)
    for t_, d_ in ((dmaskT, dmaskT_d), (qdecT, qdecT_d), (kdec, kdec_d), (hsel, hsel_d),
                   (dmask8T, dmask8T_d), (qdec8T, qdec8T_d), (kdec8, kdec8_d)):
        ld(t_.t[:], d_, t_.b)

    fw.barrier()
    esw.close()

    def norm_T(x, n, gcol, out_hT):
        act(lambda e: e.activation(out=xsbf.t[0:n, :], in_=x.t[0:n, :], func=AF.Square, accum_out=ss.t[0:n, 0:1]),
            [x.b], [xsbf.b, ss.b])
        dve(lambda e: e.tensor_scalar(out=ss.t[0:n, 0:1], in0=ss.t[0:n, 0:1], scalar1=1.0 / 1024, scalar2=1e-6,
                                      op0=ALU.mult, op1=ALU.add), [ss.b], [ss.b])
        act(lambda e: e.activation(out=ss.t[0:n, 0:1], in_=ss.t[0:n, 0:1], func=AF.Sqrt), [ss.b], [ss.b])
        dve(lambda e: e.reciprocal(out=ss.t[0:n, 0:1], in_=ss.t[0:n, 0:1]), [ss.b], [ss.b])
        dve(lambda e: e.tensor_scalar(out=xsbf.t[0:n, :], in0=x.t[0:n, :], scalar1=ss.t[0:n, 0:1], scalar2=None,
                                      op0=ALU.mult), [x.b, ss.b], [xsbf.b])
        for kc in range(8):
            tr(PT.t[:, kc, 0:n], xsbf.t[0:n, kc * 128:(kc + 1) * 128], identb.t[0:n, 0:n], [xsbf.b, identb.b], PT.b)
        dve(lambda e: e.tensor_tensor(out=out_hT.t[:, :, 0:n], in0=PT.t[:, :, 0:n], in1=bcast(gcol.t[:, :], 2, n),
                                      op=ALU.mult), [PT.b, gcol.b], [out_hT.b])

    def proj(h, n, c0, w, bank):
        for kc in range(8):
            mm(bank.t[0:n, 0:w], h.t[:, kc, 0:n], Wi.t[:, kc, c0:c0 + w], kc == 0, kc == 7, [h.b, Wi.b], bank.b)

    def rope(src, sb_, dst, db_, n, H, half, c0):
        s4 = src.rearrange("p (h t f) -> p h t f", h=H, t=2)
        d4 = dst.rearrange("p (h t f) -> p h t f", h=H, t=2)
        cs = bcast(ropet.t[0:n, c0:c0 + half], 1, H)
        sn = bcast(ropet.t[0:n, c0 + half:c0 + 2 * half], 1, H)
        t1 = rT1.t[0:n, 0:H * half].rearrange("p (h f) -> p h f", h=H)
        t2 = rT2.t[0:n, 0:H * half].rearrange("p (h f) -> p h f", h=H)
        x1, x2 = s4[:, :, 0, :], s4[:, :, 1, :]
        dve(lambda e: e.tensor_tensor(out=t1, in0=x1, in1=cs, op=ALU.mult), [sb_, ropet.b], [rT1.b])
        dve(lambda e: e.tensor_tensor(out=t2, in0=x2, in1=sn, op=ALU.mult), [sb_, ropet.b], [rT2.b])
        dve(lambda e: e.tensor_tensor(out=d4[:, :, 0, :], in0=t1, in1=t2, op=ALU.subtract), [rT1.b, rT2.b], [db_])
        dve(lambda e: e.tensor_tensor(out=t1, in0=x2, in1=cs, op=ALU.mult), [sb_, ropet.b], [rT1.b])
        dve(lambda e: e.tensor_tensor(out=t2, in0=x1, in1=sn, op=ALU.mult), [sb_, ropet.b], [rT2.b])
        dve(lambda e: e.tensor_tensor(out=d4[:, :, 1, :], in0=t1, in1=t2, op=ALU.add), [rT1.b, rT2.b], [db_])

    def gelu_to(src, sb_, dst, db_, np_, n):
        a, b, c = gl[0], gl[1], gl[2]
        act(lambda e: e.copy(out=a.t[0:np_, 0:n], in_=src), [sb_], [a.b])
        dve(lambda e: e.tensor_tensor(out=b.t[0:np_, 0:n], in0=a.t[0:np_, 0:n], in1=a.t[0:np_, 0:n], op=ALU.mult), [a.b], [b.b])
        dve(lambda e: e.tensor_scalar(out=b.t[0:np_, 0:n], in0=b.t[0:np_, 0:n], scalar1=0.044715, scalar2=1.0,
                                      op0=ALU.mult, op1=ALU.add), [b.b], [b.b])
        dve(lambda e: e.tensor_tensor(out=b.t[0:np_, 0:n], in0=b.t[0:np_, 0:n], in1=a.t[0:np_, 0:n], op=ALU.mult), [a.b, b.b], [b.b])
        act(lambda e: e.activation(out=c.t[0:np_, 0:n], in_=b.t[0:np_, 0:n], func=AF.Tanh, scale=0.7978845608028654), [b.b], [c.b])
        dve(lambda e: e.tensor_scalar(out=c.t[0:np_, 0:n], in0=c.t[0:np_, 0:n], scalar1=0.5, scalar2=0.5,
                                      op0=ALU.mult, op1=ALU.add), [c.b], [c.b])
        dve(lambda e: e.tensor_tensor(out=dst, in0=c.t[0:np_, 0:n], in1=a.t[0:np_, 0:n], op=ALU.mult), [a.b, c.b], [db_])

    def retention_q(n, dm, qd, Sb):
        for i in range(8):
            tr(PT.t[:, i, 0:n], tokbf.t[0:n, i * 128:(i + 1) * 128], identb.t[0:n, 0:n], [tokbf.b, identb.b], PT.b)
        act(lambda e: e.copy(out=qT.t[:, :, 0:n], in_=PT.t[:, 0:4, 0:n]), [PT.b], [qT.b])
        dve(lambda e: e.tensor_tensor(out=qdT.t[:, :, 0:n], in0=PT.t[:, 0:4, 0:n], in1=qd.t[:, :, 0:n], op=ALU.mult),
            [PT.b, qd.b], [qdT.b])
        act(lambda e: e.copy(out=kT.t[:, :, 0:n], in_=PT.t[:, 4:8, 0:n]), [PT.b], [kT.b])
        for h in range(4):
            mm(PM.t[0:n, h, 0:n], kT.t[:, h, 0:n], qT.t[:, h, 0:n], True, True, [kT.b, qT.b], PM.b)
        dve(lambda e: e.tensor_tensor(out=sTm.t[0:n, :, 0:n], in0=PM.t[0:n, :, 0:n], in1=dm.t[0:n, :, 0:n], op=ALU.mult),
            [PM.b, dm.b], [sTm.b])
        o = PJ[1]
        for h in range(4):
            mm(o.t[0:n, h * 128:(h + 1) * 128], sTm.t[0:n, h, 0:n], vown.t[0:n, h, :], True, False, [sTm.b, vown.b], o.b)
            mm(o.t[0:n, h * 128:(h + 1) * 128], qdT.t[:, h, 0:n], Sb.t[:, h, :], False, True, [qdT.b, Sb.b], o.b)
        ov = rT1.t[0:n, :].rearrange("p (h e) -> p h e", h=4)
        sq = rT2.t[0:n, :].rearrange("p (h e) -> p h e", h=4)
        act(lambda e: e.copy(out=rT1.t[0:n, :], in_=o.t[0:n, :]), [o.b], [rT1.b])
        dve(lambda e: e.tensor_reduce(out=gst.t[0:n, 0:4], in_=ov, axis=AX.X, op=ALU.add), [rT1.b], [gst.b])
        dve(lambda e: e.tensor_tensor(out=sq, in0=ov, in1=ov, op=ALU.mult), [rT1.b], [rT2.b])
        dve(lambda e: e.tensor_reduce(out=gst.t[0:n, 4:8], in_=sq, axis=AX.X, op=ALU.add), [rT2.b], [gst.b])
        dve(lambda e: e.tensor_scalar(out=gst.t[0:n, 0:8], in0=gst.t[0:n, 0:8], scalar1=1.0 / 128, scalar2=None, op0=ALU.mult),
            [gst.b], [gst.b])
        dve(lambda e: e.tensor_tensor(out=fac.t[0:n, 0:4], in0=gst.t[0:n, 0:4], in1=gst.t[0:n, 0:4], op=ALU.mult), [gst.b], [fac.b])
        dve(lambda e: e.tensor_tensor(out=gst.t[0:n, 4:8], in0=gst.t[0:n, 4:8], in1=fac.t[0:n, 0:4], op=ALU.subtract),
            [gst.b, fac.b], [gst.b])
        dve(lambda e: e.tensor_scalar(out=gst.t[0:n, 4:8], in0=gst.t[0:n, 4:8], scalar1=1e-5, scalar2=None, op0=ALU.add),
            [gst.b], [gst.b])
        act(lambda e: e.activation(out=gst.t[0:n, 4:8], in_=gst.t[0:n, 4:8], func=AF.Sqrt), [gst.b], [gst.b])
        dve(lambda e: e.reciprocal(out=gst.t[0:n, 4:8], in_=gst.t[0:n, 4:8]), [gst.b], [gst.b])
        dve(lambda e: e.tensor_tensor(out=ov, in0=ov, in1=bcast(gst.t[0:n, 0:4], 2, 128), op=ALU.subtract), [rT1.b, gst.b], [rT1.b])
        dve(lambda e: e.tensor_tensor(out=ov, in0=ov, in1=bcast(gst.t[0:n, 4:8], 2, 128), op=ALU.mult), [rT1.b, gst.b], [rT1.b])
        dve(lambda e: e.tensor_tensor(out=rT1.t[0:n, :], in0=rT1.t[0:n, :], in1=gng_rep.t[0:n, :], op=ALU.mult), [rT1.b, gng_rep.b], [rT1.b])
        dve(lambda e: e.tensor_tensor(out=rT1.t[0:n, :], in0=rT1.t[0:n, :], in1=gnb_rep.t[0:n, :], op=ALU.add), [rT1.b, gnb_rep.b], [rT1.b])

    def attend(n, k, Q, tiles, br, first):
        mm(PO.t[0:n, 0:260], zerob.t[0:1, 0:n], zerob.t[0:1, 0:260], True, False, [zerob.b], PO.b)
        nt = len(tiles)

        def pv(i):
            KTa, Va, nk, rds, biases = tiles[i]
            et = ET[i % 2]
            for g in range(4):
                mm(PO.t[0:n, g * 65:(g + 1) * 65], et.t[0:nk, g * n:(g + 1) * n], Va, False, i == nt - 1, rds + [et.b], PO.b)
        for i, (KTa, Va, nk, rds, biases) in enumerate(tiles):
            ps_, et = PS[i % 2], ET[i % 2]
            outv = ps_.t[0:nk, 0:4 * n].rearrange("p (g q) -> p g q", g=4)
            mm(outv, KTa, Q.t[k * 64:(k + 1) * 64, :, 0:n], True, len(biases) == 0, rds + [Q.b], ps_.b)
            for bi, (bl, br_, brd) in enumerate(biases):
                mm(outv, bl, br_, False, bi == len(biases) - 1, brd, ps_.b)
            act(lambda e: e.activation(out=et.t[0:nk, 0:4 * n], in_=ps_.t[0:nk, 0:4 * n], func=AF.Exp, scale=0.125),
                [ps_.b], [et.b])
            if i > 0:
                pv(i - 1)
        pv(nt - 1)
        pov = PO.t[0:n, 0:260].rearrange("p (g c) -> p g c", g=4)
        dve(lambda e: e.tensor_scalar(out=fac.t[0:n, 0:4], in0=pov[:, :, 64], scalar1=1e-30, scalar2=None, op0=ALU.add), [PO.b], [fac.b])
        dve(lambda e: e.reciprocal(out=fac.t[0:n, 0:4], in_=fac.t[0:n, 0:4]), [fac.b], [fac.b])
        dve(lambda e: e.tensor_tensor(out=fac.t[0:n, 0:4], in0=fac.t[0:n, 0:4], in1=sig.t[0:n, br * 8 + k * 4:br * 8 + k * 4 + 4],
                                      op=ALU.mult), [fac.b, sig.b], [fac.b])
        if first:
            dve(lambda e: e.tensor_tensor(out=ynsa.t[0:n, k, :, :], in0=pov[:, :, 0:64], in1=bcast(fac.t[0:n, 0:4], 2, 64),
                                          op=ALU.mult), [PO.b, fac.b], [ynsa.b])
        else:
            dve(lambda e: e.tensor_tensor(out=ytmp.t[0:n, :, :], in0=pov[:, :, 0:64], in1=bcast(fac.t[0:n, 0:4], 2, 64),
                                          op=ALU.mult), [PO.b, fac.b], [ytmp.b])
            dve(lambda e: e.tensor_tensor(out=ynsa.t[0:n, k, :, :], in0=ynsa.t[0:n, k, :, :], in1=ytmp.t[0:n, :, :],
                                          op=ALU.add), [ynsa.b, ytmp.b], [ynsa.b])

    def nq_prep(n, c_n):
        src = PJ[1]
        pv = lambda ap: ap.rearrange("p (two m d) -> p m two d", two=2, m=4)
        dv = lambda ap: ap.rearrange("p (m two d) -> p m two d", two=2, m=4)
        act(lambda e: e.copy(out=dv(tokbf.t[0:n, 0:512]), in_=pv(src.t[0:n, 0:512])), [src.b], [tokbf.b])
        rope(src.t[0:n, 0:512], src.b, rO.t[0:n, 0:512], rO.b, n, 8, 32, c_n)
        act(lambda e: e.copy(out=dv(tokbf.t[0:n, 512:1024]), in_=pv(rO.t[0:n, 0:512])), [rO.b], [tokbf.b])
        for i in range(8):
            tr(PT.t[:, i, 0:n], tokbf.t[0:n, i * 128:(i + 1) * 128], identb.t[0:n, 0:n], [tokbf.b, identb.b], PT.b)
        act(lambda e: e.copy(out=QTu.t[:, :, 0:n], in_=PT.t[:, 0:4, 0:n]), [PT.b], [QTu.b])
        act(lambda e: e.copy(out=QTr.t[:, :, 0:n], in_=PT.t[:, 4:8, 0:n]), [PT.b], [QTr.b])

    def prefetch_fin(n, ppsrc):
        for nn in range(2):
            ld(wb[nn].t[:, :, :], wscr[0][:, nn * 512:(nn + 1) * 512].rearrange("(k p) n -> p k n", p=128), wb[nn].b, reads=[SCRB])
        ld(ppt.t[0:n, :], ppsrc, ppt.b)

    def finish(n, x, ppsrc, ydst):
        for i in range(8):
            tr(PT.t[:, i, 0:n], ymix.t[0:n, i * 128:(i + 1) * 128], identb.t[0:n, 0:n], [ymix.b, identb.b], PT.b)
        act(lambda e: e.copy(out=hT2.t[:, :, 0:n], in_=PT.t[:, :, 0:n]), [PT.b], [hT2.b])
        for nn in range(2):
            for kc in range(8):
                mm(PJ[nn].t[0:n, :], hT2.t[:, kc, 0:n], wb[nn].t[:, kc, :], kc == 0, kc == 7,
                   [hT2.b, wb[nn].b], PJ[nn].b)
            ld(wb[nn].t[:, :, :], wscr[1][:, nn * 512:(nn + 1) * 512].rearrange("(k p) n -> p k n", p=128), wb[nn].b, reads=[SCRB])
            dve(lambda e: e.tensor_tensor(out=x.t[0:n, nn * 512:(nn + 1) * 512], in0=x.t[0:n, nn * 512:(nn + 1) * 512],
                                          in1=PJ[nn].t[0:n, :], op=ALU.add), [x.b, PJ[nn].b], [x.b])
        if KDBG and n == 128 and not hasattr(finish, "done"):
            dbg("x2", x.t[:, :], 128, 1024, x.b)
        norm_T(x, n, gple, hT)
        dve(lambda e: e.tensor_copy(out=ppb.t[0:n, :], in_=ppt.t[0:n, :]), [ppt.b], [ppb.b])
        for nn in range(2):
            for kc in range(8):
                mm(PJ[nn].t[0:n, :], hT.t[:, kc, 0:n], wb[nn].t[:, kc, :], kc == 0, kc == 7,
                   [hT.b, wb[nn].b], PJ[nn].b)
            act(lambda e: e.activation(out=xt.t[0:n, nn * 512:(nn + 1) * 512], in_=PJ[nn].t[0:n, :], func=AF.Sigmoid),
                [PJ[nn].b], [xt.b])
        for i in range(2):
            tr(PT.t[:, i, 0:n], ppb.t[0:n, i * 128:(i + 1) * 128], identb.t[0:n, 0:n], [ppb.b, identb.b], PT.b)
        act(lambda e: e.copy(out=ppT.t[:, :, 0:n], in_=PT.t[:, 0:2, 0:n]), [PT.b], [ppT.b])
        for nn in range(2):
            for kc in range(2):
                mm(PJ[nn].t[0:n, :], ppT.t[:, kc, 0:n], Wp.t[:, kc, nn * 512:(nn + 1) * 512], kc == 0, kc == 1,
                   [ppT.b, Wp.b], PJ[nn].b)
            dve(lambda e: e.tensor_tensor(out=xt.t[0:n, nn * 512:(nn + 1) * 512], in0=xt.t[0:n, nn * 512:(nn + 1) * 512],
                                          in1=PJ[nn].t[0:n, :], op=ALU.mult), [xt.b, PJ[nn].b], [xt.b])
        dve(lambda e: e.tensor_tensor(out=x.t[0:n, :], in0=x.t[0:n, :], in1=xt.t[0:n, :], op=ALU.add), [x.b, xt.b], [x.b])
        if KDBG and n == 128 and not hasattr(finish, "done"):
            dbg("x3", x.t[:, :], 128, 1024, x.b)
            finish.done = True
        act(lambda e: e.activation(out=xsbf.t[0:n, :], in_=x.t[0:n, :], func=AF.Square, accum_out=ss.t[0:n, 0:1]),
            [x.b], [xsbf.b, ss.b])
        dve(lambda e: e.tensor_scalar(out=ss.t[0:n, 0:1], in0=ss.t[0:n, 0:1], scalar1=1.0 / 1024, scalar2=1e-6,
                                      op0=ALU.mult, op1=ALU.add), [ss.b], [ss.b])
        act(lambda e: e.activation(out=ss.t[0:n, 0:1], in_=ss.t[0:n, 0:1], func=AF.Sqrt), [ss.b], [ss.b])
        dve(lambda e: e.reciprocal(out=ss.t[0:n, 0:1], in_=ss.t[0:n, 0:1]), [ss.b], [ss.b])
        dve(lambda e: e.tensor_scalar(out=xt.t[0:n, :], in0=x.t[0:n, :], scalar1=ss.t[0:n, 0:1], scalar2=None, op0=ALU.mult),
            [x.b, ss.b], [xt.b])
        dve(lambda e: e.tensor_tensor(out=xt.t[0:n, :], in0=xt.t[0:n, :], in1=gf_rep.t[0:n, :], op=ALU.mult), [xt.b, gf_rep.b], [xt.b])
        fw.dma("sp", lambda e: e.dma_start(out=ydst, in_=xt.t[0:n, :]), [xt.b], [OUTB])

    def gates_and_ret_out(n):
        proj(hT, n, C_RG, 512, PJ[0])
        act(lambda e: e.activation(out=rT2.t[0:n, :], in_=PJ[0].t[0:n, :], func=AF.Silu), [PJ[0].b], [rT2.b])
        dve(lambda e: e.tensor_tensor(out=ymix.t[0:n, 0:512], in0=rT1.t[0:n, :], in1=rT2.t[0:n, :], op=ALU.mult),
            [rT1.b, rT2.b], [ymix.b])
        proj(hT, n, C_NGL, 24, PJ[0])
        act(lambda e: e.activation(out=sig.t[0:n, :], in_=PJ[0].t[0:n, 0:24], func=AF.Sigmoid), [PJ[0].b], [sig.b])
        proj(hT, n, C_NG, 512, PJ[0])
        act(lambda e: e.activation(out=sgn.t[0:n, :], in_=PJ[0].t[0:n, :], func=AF.Silu), [PJ[0].b], [sgn.b])

    def nsa_out(n):
        dve(lambda e: e.tensor_tensor(out=ymix.t[0:n, 512:1024], in0=ynsa.t[0:n, :, :, :].rearrange("p a b c -> p (a b c)"),
                                      in1=sgn.t[0:n, :], op=ALU.mult), [ynsa.b, sgn.b], [ymix.b])

    def top_sel(n, nblk, kth):
        dve(lambda e: e.max(out=m8.t[0:n, 0:8], in_=score.t[0:n, 0:nblk]), [score.b], [m8.b])
        dve(lambda e: e.match_replace(out=sc2.t[0:n, 0:nblk], in_to_replace=m8.t[0:n, 0:8], in_values=score.t[0:n, 0:nblk],
                                      imm_value=-1e30), [score.b, m8.b], [sc2.b])
        dve(lambda e: e.max(out=m8.t[0:n, 8:16], in_=sc2.t[0:n, 0:nblk]), [sc2.b], [m8.b])
        dve(lambda e: e.tensor_scalar(out=selb.t[0:n, 0:nblk], in0=score.t[0:n, 0:nblk], scalar1=m8.t[0:n, kth - 1:kth],
                                      scalar2=NEG, op0=ALU.is_lt, op1=ALU.mult), [score.b, m8.b], [selb.b])

    esp = contextlib.ExitStack()
    P_ = lambda name, shape, dt=F32, const=False: fw.sb(esp, name, shape, dt, const)
    KTs = P_("KTs", [128, SEQ], BF16); Vs = P_("Vs", [128, NB, 2, 65], BF16)
    KTw = P_("KTw", [128, 8 * 128], BF16); Vw = P_("Vw", [128, 8, 2, 65], BF16)
    KCT = P_("KCT", [128, 128], BF16); VC = P_("VC", [128, 2, 65], BF16)
    expand = P_("expand", [128, SEQ], BF16, True)
    TB = P_("TB", [128, 2, 128], BF16, True); WB = P_("WB", [128, 6, 128], BF16, True)
    CBp = P_("CBp", [128, 128], BF16); CBqp = P_("CBqp", [128, 128], BF16); FBp = P_("FBp", [128, NSB])
    mst = P_("mst", [128, 1024])
    for tt in (Vs, Vw, VC):
        fw.op("pool", lambda e: e.memset(tt.t[:], 1.0), [], [tt.b])
    fw.op("pool", lambda e: e.memset(Sst.t[:], 0.0), [], [Sst.b])
    fw.op("pool", lambda e: e.memset(KCT.t[:], 0.0), [], [KCT.b])
    fw.op("pool", lambda e: e.memset(KTs.t[:], 0.0), [], [KTs.b])
    fw.op("pool", lambda e: e.memset(KTw.t[:], 0.0), [], [KTw.b])
    fw.op("pool", lambda e: e.memset(expand.t[:], 0.0), [], [expand.b])
    fw.op("pool", lambda e: e.memset(selT.t[:], 0.0), [], [selT.b])
    for c0 in range(0, SEQ, 1024):
        w = min(1024, SEQ - c0)
        ld(mst.t[0:NSB, 0:w], expand_d[:, c0:c0 + w], mst.b)
        dve(lambda e: e.tensor_copy(out=expand.t[0:NSB, c0:c0 + w], in_=mst.t[0:NSB, 0:w]), [mst.b], [expand.b])
    ld(mst.t[:, 0:256], mTB.rearrange("p a b -> p (a b)"), mst.b)
    dve(lambda e: e.tensor_copy(out=TB.t[:].rearrange("p a b -> p (a b)"), in_=mst.t[:, 0:256]), [mst.b], [TB.b])
    ld(mst.t[:, 0:768], mWB.rearrange("p a b -> p (a b)"), mst.b)
    dve(lambda e: e.tensor_copy(out=WB.t[:].rearrange("p a b -> p (a b)"), in_=mst.t[:, 0:768]), [mst.b], [WB.b])

    def kv_block(t, p):
        ld(xt.t[:, :], xb[t * 128:(t + 1) * 128, :], xt.b)
        ld(ropet.t[:, :], rope_kv[t * 128:(t + 1) * 128, :], ropet.b)
        norm_T(xt, 128, gmix, hT)
        if SUB <= 1:
            return
        proj(hT, 128, C_RK, 512, PJ[0])
        rope(PJ[0].t[:, :], PJ[0].b, rO.t[:, :], rO.b, 128, 4, 64, 128)
        dve(lambda e: e.tensor_tensor(out=Kd.t[:, :, :], in0=rO.t[:, :].rearrange("p (h d) -> p h d", h=4),
                                      in1=bcast(kdec.t[:, 0:4], 2, 128), op=ALU.mult), [rO.b, kdec.b], [Kd.b])
        proj(hT, 128, C_RV, 512, PJ[1])
        act(lambda e: e.copy(out=Vbf.t[:, :, :].rearrange("p h d -> p (h d)"), in_=PJ[1].t[:, :]), [PJ[1].b], [Vbf.b])
        for h in range(4):
            mm(PM.t[:, h, :], Kd.t[:, h, :], Vbf.t[:, h, :], True, True, [Kd.b, Vbf.b], PM.b)
        if SUB <= 2:
            return
        if t == 2 * p:
            for h in range(4):
                dve(lambda e: e.tensor_scalar(out=Sown.t[:, h, :], in0=Sst.t[:, h, :], scalar1=hsel.t[:, h:h + 1], scalar2=None,
                                              op0=ALU.mult), [Sst.b, hsel.b], [Sown.b])
                dve(lambda e: e.scalar_tensor_tensor(out=Sown.t[:, h, :], in0=PM.t[:, h, :], scalar=hsel.t[:, 4 + h:5 + h],
                                                     in1=Sown.t[:, h, :], op0=ALU.mult, op1=ALU.add),
                    [PM.b, hsel.b, Sown.b], [Sown.b])
            act(lambda e: e.copy(out=Sownb.t[:, :, :], in_=Sown.t[:, :, :]), [Sown.b], [Sownb.b])
        for h in range(4):
            dve(lambda e: e.scalar_tensor_tensor(out=Sst.t[:, h, :], in0=Sst.t[:, h, :], scalar=gC[h], in1=PM.t[:, h, :],
                                                 op0=ALU.mult, op1=ALU.add), [Sst.b, PM.b], [Sst.b])
        if SUB <= 3:
            return
        proj(hT, 128, C_CK, 512, PJ[0])
        act(lambda e: e.copy(out=kvst.t[:, 0:2, :], in_=PJ[0].t[:, 0:256].rearrange("p (a b) -> p a b", a=2)), [PJ[0].b], [kvst.b])
        act(lambda e: e.copy(out=kvst.t[:, 3, :], in_=PJ[0].t[:, 384:512]), [PJ[0].b], [kvst.b])
        rope(PJ[0].t[:, 256:384], PJ[0].b, kvst.t[:, 2, :], kvst.b, 128, 2, 32, 256)
        proj(hT, 128, C_WK, 256, PJ[1])
        act(lambda e: e.copy(out=wst.t[:, 1, :], in_=PJ[1].t[:, 128:256]), [PJ[1].b], [wst.b])
        rope(PJ[1].t[:, 0:128], PJ[1].b, wst.t[:, 0, :], wst.b, 128, 2, 32, 256)
        if SUB <= 4:
            return
        fw.dma("sp", lambda e: e.dma_start(out=kvout[t * 128:(t + 1) * 128, :, :], in_=kvst.t[:, :, :]), [kvst.b], [OUTB])
        if t >= NB - 4:
            tw = t - (NB - 4)
            fw.dma("sp", lambda e: e.dma_start(out=winout[tw * 128:(tw + 1) * 128, :, :], in_=wst.t[:, :, :]), [wst.b], [OUTB])
        if SUB <= 5:
            return
        dve(lambda e: e.tensor_copy(out=tokbf.t[:, 0:384], in_=kvst.t[:, 0:3, :].rearrange("p a b -> p (a b)")), [kvst.b], [tokbf.b])
        dve(lambda e: e.tensor_copy(out=tokbf.t[:, 384:512], in_=wst.t[:, 0, :]), [wst.b], [tokbf.b])
        for i in range(4):
            tr(PT.t[:, i, :], tokbf.t[:, i * 128:(i + 1) * 128], identb.t[:, :], [tokbf.b, identb.b], PT.b)
        if SUB5 <= 1:
            return
        act(lambda e: e.copy(out=KTs.t[:, t * 128:(t + 1) * 128], in_=PT.t[:, 2, :]), [PT.b], [KTs.b])
        act(lambda e: e.copy(out=KTw.t[:, (t % 8) * 128:(t % 8 + 1) * 128], in_=PT.t[:, 3, :]), [PT.b], [KTw.b])
        if SUB5 <= 2:
            return
        for kv in range(2):
            dve(lambda e: e.tensor_tensor(out=blkT.t[:, kv, :].rearrange("p (c j) -> p c j", c=4),
                                          in0=PT.t[:, kv, :].rearrange("p (c j) -> p c j", c=4),
                                          in1=bcast(peT.t[:, kv, :], 1, 4), op=ALU.add), [PT.b, peT.b], [blkT.b])
        if SUB5 <= 3:
            return
        dve(lambda e: e.tensor_copy(out=Vs.t[:, t, :, 0:64], in_=kvst.t[:, 3, :].rearrange("p (k d) -> p k d", k=2)), [kvst.b], [Vs.b])
        dve(lambda e: e.tensor_copy(out=Vw.t[:, t % 8, :, 0:64], in_=wst.t[:, 1, :].rearrange("p (k d) -> p k d", k=2)), [wst.b], [Vw.b])
        if SUB <= 6:
            return
        for kv in range(2):
            bv = blkT.t[:, kv, :].rearrange("p (c j) -> p c j", c=4)
            for j in range(32):
                mm(PT32.t[:, kv * 4:kv * 4 + 4], W1[kv].t[:, j, :], bv[:, :, j], j == 0, j == 31, [W1[kv].b, blkT.b], PT32.b)
        gelu_to(PT32.t[:, 0:8], PT32.b, hidb.t[:, 0, 0:8], hidb.b, 128, 8)
        if SUB <= 7:
            return
        mm(PT32.t[:, 8:12], W2[0].t[:, :], hidb.t[:, 0, 0:4], True, True, [W2[0].b, hidb.b], PT32.b)
        act(lambda e: e.copy(out=KCT.t[:, 4 * t:4 * t + 4], in_=PT32.t[:, 8:12]), [PT32.b], [KCT.b])
        mm(PT32.t[0:4, 16:144], hidb.t[:, 0, 4:8], W2[1].t[:, :], True, True, [W2[1].b, hidb.b], PT32.b)
        act(lambda e: e.copy(out=vcst.t[0:4, :], in_=PT32.t[0:4, 16:144]), [PT32.b], [vcst.b])
        if SUB <= 8:
            return
        fw.dma("sp", lambda e: e.dma_start(out=VC.t[4 * t:4 * t + 4, :, 0:64],
                                           in_=vcst.t[0:4, :].rearrange("p (k d) -> p k d", k=2)), [vcst.b], [VC.b])

    def q_block(p):
        n = 128
        prefetch_fin(n, pp_own[p * 128:(p + 1) * 128, :])
        ld(ropet.t[:, :], rope_own[p * 128:(p + 1) * 128, :], ropet.b)
        ld(mst.t[:, 0:128], mCB[p], mst.b)
        dve(lambda e: e.tensor_copy(out=CBp.t[:, :], in_=mst.t[:, 0:128]), [mst.b], [CBp.b])
        ld(mst.t[:, 128:256], mCBq[p], mst.b)
        dve(lambda e: e.tensor_copy(out=CBqp.t[:, :], in_=mst.t[:, 128:256]), [mst.b], [CBqp.b])
        ld(FBp.t[:, :], mFB[p], FBp.b)
        norm_T(xo, n, gmix, hT)
        proj(hT, n, C_RQ, 512, PJ[0])
        rope(PJ[0].t[:, :], PJ[0].b, tokbf.t[:, 0:512], tokbf.b, n, 4, 64, 0)
        proj(hT, n, C_RK, 512, PJ[1])
        rope(PJ[1].t[:, :], PJ[1].b, tokbf.t[:, 512:1024], tokbf.b, n, 4, 64, 128)
        proj(hT, n, C_RV, 512, PJ[0])
        act(lambda e: e.copy(out=vown.t[:, :, :].rearrange("p h d -> p (h d)"), in_=PJ[0].t[:, :]), [PJ[0].b], [vown.b])
        if SUBQ <= 1:
            return
        retention_q(n, dmaskT, qdecT, Sownb)
        if p == 0:
            dbg("retn", rT1.t[:, :], 128, 512, rT1.b)
        if SUBQ <= 2:
            return
        gates_and_ret_out(n)
        if p == 0:
            dbg("yret", ymix.t[:, 0:512], 128, 512, ymix.b)
            dbg("sig", sig.t[:, :], 128, 24, sig.b)
            dbg("sgn", sgn.t[:, :], 128, 512, sgn.b)
        proj(hT, n, C_NQ, 512, PJ[1])
        nq_prep(n, 256)
        if SUBQ <= 3:
            return
        for k in range(2):
            pj = PJ[0]
            for g in range(4):
                mm(pj.t[:, g * 128:g * 128 + NC], QTu.t[k * 64:(k + 1) * 64, g, :], KCT.t[k * 64:(k + 1) * 64, 0:NC], True, False,
                   [QTu.b, KCT.b], pj.b)
                mm(pj.t[:, g * 128:g * 128 + NC], identb.t[:, :], CBqp.t[:, 0:NC], False, True, [identb.b, CBqp.b], pj.b)
            for g in range(4):
                act(lambda e: e.activation(out=e2.t[:, g, 0:NC], in_=pj.t[:, g * 128:g * 128 + NC], func=AF.Exp, scale=0.125,
                                           accum_out=rsum.t[:, g:g + 1]), [pj.b], [e2.b, rsum.b])
            dve(lambda e: e.tensor_scalar(out=rsum.t[:, 4:8], in0=rsum.t[:, 0:4], scalar1=1e-30, scalar2=None, op0=ALU.add), [rsum.b], [rsum.b])
            dve(lambda e: e.reciprocal(out=rsum.t[:, 4:8], in_=rsum.t[:, 4:8]), [rsum.b], [rsum.b])
            dve(lambda e: e.tensor_scalar(out=imp.t[:, 0:NC], in0=e2.t[:, 0, 0:NC], scalar1=rsum.t[:, 4:5], scalar2=None, op0=ALU.mult),
                [e2.b, rsum.b], [imp.b])
            for g in range(1, 4):
                dve(lambda e: e.scalar_tensor_tensor(out=imp.t[:, 0:NC], in0=e2.t[:, g, 0:NC], scalar=rsum.t[:, 4 + g:5 + g],
                                                     in1=imp.t[:, 0:NC], op0=ALU.mult, op1=ALU.add), [e2.b, rsum.b, imp.b], [imp.b])
            iv = imp.t[:, 0:NC].rearrange("p (b two) -> p b two", two=2)
            dve(lambda e: e.tensor_tensor(out=score.t[:, 0:NSB], in0=iv[:, :, 0], in1=iv[:, :, 1], op=ALU.add), [imp.b], [score.b])
            dve(lambda e: e.tensor_tensor(out=score.t[:, 0:NSB], in0=score.t[:, 0:NSB], in1=FBp.t[:, :], op=ALU.add),
                [score.b, FBp.b], [score.b])
            top_sel(n, NSB, 16)
            tr(PT32.t[0:NSB, 256:384], selb.t[:, 0:NSB], ident.t[:, :], [selb.b, ident.b], PT32.b)
            act(lambda e: e.copy(out=selT.t[0:NSB, k, :], in_=PT32.t[0:NSB, 256:384]), [PT32.b], [selT.b])
        if SUBQ <= 4:
            return
        for k in range(2):
            ks = slice(k * 64, (k + 1) * 64)
            tiles = [(KCT.t[ks, 0:NC], VC.t[0:NC, k, :], NC, [KCT.b, VC.b],
                      [(identb.t[:, 0:NC], bcast(CBp.t[:, :], 1, 4), [identb.b, CBp.b])])]
            attend(n, k, QTu, tiles, 0, True)
            if p == 0:
                dbg("ycmp%d" % k, ynsa.t[:, k, :, :].rearrange("p a b -> p (a b)"), 128, 256, ynsa.b)
            tiles = []
            for kt in range(2 * p + 2):
                b_ = [(expand.t[:, kt * 128:(kt + 1) * 128], bcast(selT.t[:, k, :], 1, 4), [expand.b, selT.b])]
                if kt >= 2 * p:
                    b_.append((identb.t[:, :], bcast(TB.t[:, kt - 2 * p, :], 1, 4), [identb.b, TB.b]))
                tiles.append((KTs.t[ks, kt * 128:(kt + 1) * 128], Vs.t[:, kt, k, :], 128, [KTs.b, Vs.b], b_))
            attend(n, k, QTr, tiles, 1, False)
            if p == 0:
                dbg("yslc%d" % k, ynsa.t[:, k, :, :].rearrange("p a b -> p (a b)"), 128, 256, ynsa.b)
            tiles = []
            for i in range(6):
                kt = 2 * p - 4 + i
                if kt < 0:
                    continue
                r8 = kt % 8
                tiles.append((KTw.t[ks, r8 * 128:(r8 + 1) * 128], Vw.t[:, r8, k, :], 128, [KTw.b, Vw.b],
                              ([] if i in (2, 3) else [(identb.t[:, :], bcast(WB.t[:, i, :], 1, 4), [identb.b, WB.b])])))
            attend(n, k, QTr, tiles, 2, False)
        if SUBQ <= 5:
            return
        if p == 0:
            dbg("ynsa", ynsa.t[:, :, :, :].rearrange("p a b c -> p (a b c)"), 128, 512, ynsa.b)
            dbg("selb", selb.t[:, 0:NSB], 128, NSB, selb.b)
            dbg("score", score.t[:, 0:NSB], 128, NSB, score.b)
        nsa_out(n)
        if p == 0:
            dbg("ymix", ymix.t[:, :], 128, 1024, ymix.b)
        finish(n, xo, pp_own[p * 128:(p + 1) * 128, :], y_own[p * 128:(p + 1) * 128, :])

    for p in range(NP):
        if STOP <= 0:
            break
        ld(xo.t[:, :], x_own[p * 128:(p + 1) * 128, :], xo.b)
        kv_block(2 * p, p)
        if STOP <= 1:
            break
        kv_block(2 * p + 1, p)
        if STOP <= 2:
            break
        q_block(p)
        if STOP <= 3:
            break
    for h in range(4):
        fw.dma("sp", lambda e: e.dma_start(out=ret_p[h], in_=Sst.t[:, h, :]), [Sst.b], [OUTB])
    fw.barrier()
    esp.close()

    ess = contextlib.ExitStack()

    class _V0:
        pass
    Q_ = lambda name, shape, dt=F32, const=False: fw.sb(ess, name, shape, dt, const)
    G = [Q_("G%d" % i, [NPG, 2048], BF16) for i in range(3)]
    idx = Q_("idx", [128, 1], I32); idx4 = Q_("idx4", [128, 8], I32)
    bT = Q_("bT", [128, 16, NPG], BF16)
    KCTs = Q_("KCTs", [128, 4, NPG], BF16); VCs = Q_("VCs", [NPG, 4, 2, 65], BF16)
    Vb = Q_("Vb", [NPG, 16, 2, 65], BF16); KTr = [Q_("KTr0", [128, 4, NPG], BF16), Q_("KTr1", [128, 4, NPG], BF16)]
    e2s = _V0(); e2s.t = e2.t[0:8, :, :].rearrange("p a b -> p (a b)")[:, 0:4 * NPG]; e2s.b = e2.b; imps = Q_("imps", [8, 4 * NPG]); scs = Q_("scs", [8, 2, 2 * NPG])
    selTs = Q_("selTs", [128, 2, 2, 8], BF16)
    WBs = Q_("WBs", [128, 8], BF16, True); NB8 = Q_("NB8", [128, 8], BF16, True)
    xs_t = _V0(); xs_t.t = xo.t[0:8, :]; xs_t.b = xo.b
    class _V:
        pass
    winK = _V(); winK.t = xt.t[:, 0:512].rearrange("p (t c) -> p t c", t=4); winK.b = xt.b
    winV = _V(); winV.t = xt.t[:, 512:1024].rearrange("p (t c) -> p t c", t=4); winV.b = xt.b
    winKb = Q_("winKb", [128, 4, 128], BF16); KTwin = Q_("KTwin", [128, 4, 128], BF16); Vwin = Q_("Vwin", [128, 4, 2, 65], BF16)
    newK = Q_("newK", [128, 2, 8], BF16); newV = Q_("newV", [8, 2, 2, 65], BF16)
    mst2 = Q_("mst2", [128, 8])
    Yn = Q_("Yn", [32, 128]); selG = Q_("selG", [32, 4, 8], F32, True)
    ld(selG.t[:], selG_d, selG.b)
    ld(mst2.t[:, :], mWBs, mst2.b)
    dve(lambda e: e.tensor_copy(out=WBs.t[:, :], in_=mst2.t[:, :]), [mst2.b], [WBs.b])
    fw.op("pool", lambda e: e.memset(NB8.t[:], 0.0), [], [NB8.b])
    fw.op("pool", lambda e: e.memset(selTs.t[:], 0.0), [], [selTs.b])
    ld(mst2.t[0:8, :], mNB8, mst2.b)
    dve(lambda e: e.tensor_copy(out=NB8.t[0:8, :], in_=mst2.t[0:8, :]), [mst2.b], [NB8.b])
    for tt in (VCs, Vb, Vwin, newV):
        fw.op("pool", lambda e: e.memset(tt.t[:], 1.0), [], [tt.b])
    gi = [0]
    rc = [0]

    def gather(cache, q4):
        g = G[gi[0] % 3]
        gi[0] += 1
        fw.dma("pool", lambda e: e.indirect_dma_start(out=g.t[:, :], out_offset=None, in_=cache,
                                                      in_offset=bass.IndirectOffsetOnAxis(ap=idx4.t[0:NPG, q4:q4 + 1], axis=0)),
               [idx4.b], [g.b])
        return g

    def sample_batch(b):
        n = 8
        ld(idx.t[0:NPG, :], ptab[b].rearrange("(p o) -> p o", o=1), idx.b)
        for q4 in range(8):
            dve(lambda e: e.tensor_scalar(out=idx4.t[0:NPG, q4:q4 + 1], in0=idx.t[0:NPG, 0:1], scalar1=8, scalar2=q4,
                                          op0=ALU.mult, op1=ALU.add), [idx.b], [idx4.b])
        ld(xs_t.t[:, :], xs[b * 8:(b + 1) * 8, :], xs_t.b)
        prefetch_fin(n, pps[b * 8:(b + 1) * 8, :])
        ld(ropet.t[0:8, :], rope_s[:, :], ropet.b)
        norm_T(xs_t, n, gmix, hT)
        proj(hT, n, C_CK, 512, PJ[0])
        act(lambda e: e.copy(out=kvst.t[0:n, 0:2, :], in_=PJ[0].t[0:n, 0:256].rearrange("p (a b) -> p a b", a=2)), [PJ[0].b], [kvst.b])
        act(lambda e: e.copy(out=kvst.t[0:n, 3, :], in_=PJ[0].t[0:n, 384:512]), [PJ[0].b], [kvst.b])
        rope(PJ[0].t[0:n, 256:384], PJ[0].b, kvst.t[0:n, 2, :], kvst.b, n, 2, 32, 256)
        proj(hT, n, C_WK, 256, PJ[1])
        act(lambda e: e.copy(out=wst.t[0:n, 1, :], in_=PJ[1].t[0:n, 128:256]), [PJ[1].b], [wst.b])
        rope(PJ[1].t[0:n, 0:128], PJ[1].b, wst.t[0:n, 0, :], wst.b, n, 2, 32, 256)
        fw.dma("sp", lambda e: e.dma_start(out=kv_s[b * 8:(b + 1) * 8, :, :], in_=kvst.t[0:n, :, :]), [kvst.b], [OUTB])
        fw.dma("sp", lambda e: e.dma_start(out=win_s[b, :, 504:512, :].rearrange("a t c -> t a c"), in_=wst.t[0:n, :, :]), [wst.b], [OUTB])
        fw.dma("sp", lambda e: e.dma_start(out=win_s[b, 0, 0:504, :], in_=win_k[b, 8:512, :]), [], [OUTB])
        fw.dma("sp", lambda e: e.dma_start(out=win_s[b, 1, 0:504, :], in_=win_v[b, 8:512, :]), [], [OUTB])
        dve(lambda e: e.tensor_copy(out=tokbf.t[0:n, 0:128], in_=kvst.t[0:n, 2, :]), [kvst.b], [tokbf.b])
        dve(lambda e: e.tensor_copy(out=tokbf.t[0:n, 128:256], in_=wst.t[0:n, 0, :]), [wst.b], [tokbf.b])
        for i in range(2):
            tr(PT.t[:, i, 0:n], tokbf.t[0:n, i * 128:(i + 1) * 128], identb.t[0:n, 0:n], [tokbf.b, identb.b], PT.b)
        act(lambda e: e.copy(out=newK.t[:, :, :], in_=PT.t[:, 0:2, 0:n]), [PT.b], [newK.b])
        dve(lambda e: e.tensor_copy(out=newV.t[0:n, 0, :, 0:64], in_=kvst.t[0:n, 3, :].rearrange("p (k d) -> p k d", k=2)), [kvst.b], [newV.b])
        dve(lambda e: e.tensor_copy(out=newV.t[0:n, 1, :, 0:64], in_=wst.t[0:n, 1, :].rearrange("p (k d) -> p k d", k=2)), [wst.b], [newV.b])
        ld(Sst.t[:, :, :], st_ret[b].rearrange("h d e -> d h e"), Sst.b)
        act(lambda e: e.copy(out=Sownb.t[:, :, :], in_=Sst.t[:, :, :]), [Sst.b], [Sownb.b])
        proj(hT, n, C_RQ, 512, PJ[0])
        rope(PJ[0].t[0:n, :], PJ[0].b, tokbf.t[0:n, 0:512], tokbf.b, n, 4, 64, 0)
        proj(hT, n, C_RK, 512, PJ[1])
        rope(PJ[1].t[0:n, :], PJ[1].b, rO.t[0:n, :], rO.b, n, 4, 64, 128)
        dve(lambda e: e.tensor_copy(out=tokbf.t[0:n, 512:1024], in_=rO.t[0:n, :]), [rO.b], [tokbf.b])
        dve(lambda e: e.tensor_tensor(out=Kd.t[0:n, :, :], in0=rO.t[0:n, :].rearrange("p (h d) -> p h d", h=4),
                                      in1=bcast(kdec8.t[0:n, 0:4], 2, 128), op=ALU.mult), [rO.b, kdec8.b], [Kd.b])
        proj(hT, n, C_RV, 512, PJ[0])
        act(lambda e: e.copy(out=vown.t[0:n, :, :].rearrange("p h d -> p (h d)"), in_=PJ[0].t[0:n, :]), [PJ[0].b], [vown.b])
        for h in range(4):
            mm(PM.t[:, h, :], Kd.t[0:n, h, :], vown.t[0:n, h, :], True, True, [Kd.b, vown.b], PM.b)
        for h in range(4):
            dve(lambda e: e.scalar_tensor_tensor(out=Sst.t[:, h, :], in0=Sst.t[:, h, :], scalar=gC8[h], in1=PM.t[:, h, :],
                                                 op0=ALU.mult, op1=ALU.add), [Sst.b, PM.b], [Sst.b])
        fw.dma("sp", lambda e: e.dma_start(out=ret_s[b].rearrange("h d e -> d h e"), in_=Sst.t[:, :, :]), [Sst.b], [OUTB])
        retention_q(n, dmask8T, qdec8T, Sownb)
        gates_and_ret_out(n)
        proj(hT, n, C_NQ, 512, PJ[1])
        nq_prep(n, 256)
        for kv in range(2):
            for q4 in range(4):
                for h2 in range(2):
                    g = gather(caches[kv], q4 * 2 + h2)
                    for j0 in range(0, 16, 4):
                        bank = (PT32, PJ[0])[rc[0] % 2]
                        rc[0] += 1
                        for jj in range(4):
                            j = j0 + jj
                            tr(bank.t[:, :].bitcast(BF16)[:, jj * 128:jj * 128 + NPG], g.t[:, j * 128:(j + 1) * 128], identb.t[0:NPG, 0:NPG],
                               [g.b, identb.b], bank.b)
                        jg0 = h2 * 16 + j0
                        dve(lambda e: e.tensor_tensor(out=bT.t[:, j0:j0 + 4, :],
                                                      in0=bank.t[:, :].bitcast(BF16)[:, 0:512].rearrange("p (a b) -> p a b", a=4)[:, :, 0:NPG],
                                                      in1=bcast(peT.t[:, kv, jg0:jg0 + 4], 2, NPG), op=ALU.add),
                            [bank.b, peT.b], [bT.b])
                    for j in range(16):
                        jg = h2 * 16 + j
                        mm(PS[0].t[:, 0:NPG], W1[kv].t[:, jg, :], bT.t[:, j, :], jg == 0, jg == 31, [W1[kv].b, bT.b], PS[0].b)
                gelu_to(PS[0].t[:, 0:NPG], PS[0].b, hidb.t[:, 0, 0:NPG], hidb.b, 128, NPG)
                if kv == 0:
                    mm(PS[1].t[:, 0:NPG], W2[0].t[:, :], hidb.t[:, 0, 0:NPG], True, True, [W2[0].b, hidb.b], PS[1].b)
                    act(lambda e: e.copy(out=KCTs.t[:, q4, :], in_=PS[1].t[:, 0:NPG]), [PS[1].b], [KCTs.b])
                else:
                    mm(PS[1].t[0:NPG, 0:128], hidb.t[:, 0, 0:NPG], W2[1].t[:, :], True, True, [W2[1].b, hidb.b], PS[1].b)
                    act(lambda e: e.copy(out=VCs.t[:, q4, :, 0:64], in_=PS[1].t[0:NPG, 0:128].rearrange("p (k d) -> p k d", k=2)),
                        [PS[1].b], [VCs.b])
        for k in range(2):
            ks = slice(k * 64, (k + 1) * 64)
            for g in range(4):
                pj = PJ[g % 2]
                mm(pj.t[0:n, 0:4 * NPG].rearrange("p (a b) -> p a b", a=4), QTu.t[ks, g, 0:n], KCTs.t[ks, :, :], True, True, [QTu.b, KCTs.b], pj.b)
                act(lambda e: e.activation(out=e2s.t[:, :], in_=pj.t[0:n, 0:4 * NPG], func=AF.Exp, scale=0.125,
                                           accum_out=rsum.t[0:n, g:g + 1]), [pj.b], [e2s.b, rsum.b])
                dve(lambda e: e.reciprocal(out=rsum.t[0:n, 4 + g:5 + g], in_=rsum.t[0:n, g:g + 1]), [rsum.b], [rsum.b])
                if g == 0:
                    dve(lambda e: e.tensor_scalar(out=imps.t[:, :], in0=e2s.t[:, :], scalar1=rsum.t[0:n, 4:5], scalar2=None, op0=ALU.mult),
                        [e2s.b, rsum.b], [imps.b])
                else:
                    dve(lambda e: e.scalar_tensor_tensor(out=imps.t[:, :], in0=e2s.t[:, :], scalar=rsum.t[0:n, 4 + g:5 + g],
                                                         in1=imps.t[:, :], op0=ALU.mult, op1=ALU.add), [e2s.b, rsum.b, imps.b], [imps.b])
            iv = imps.t[:, :].rearrange("p (hf two g) -> p hf two g", hf=2, two=2)
            dve(lambda e: e.tensor_tensor(out=scs.t[:, k, :].rearrange("p (hf g) -> p hf g", hf=2), in0=iv[:, :, 0, :], in1=iv[:, :, 1, :],
                                          op=ALU.add), [imps.b], [scs.b])
            dve(lambda e: e.tensor_scalar(out=scs.t[:, k, 0:1], in0=scs.t[:, k, 0:1], scalar1=1e4, scalar2=None, op0=ALU.add),
                [scs.b], [scs.b])
            dve(lambda e: e.tensor_copy(out=score.t[0:n, 0:2 * NPG], in_=scs.t[:, k, :]), [scs.b], [score.b])
            top_sel(n, 2 * NPG, 15)
            dve(lambda e: e.tensor_scalar(out=selb.t[0:n, 0:2 * NPG], in0=selb.t[0:n, 0:2 * NPG], scalar1=0.0, scalar2=None,
                                          op0=ALU.is_equal), [selb.b], [selb.b])
            for hf in range(2):
                tr(PT32.t[0:NPG, (hf * 2 + k) * 8:(hf * 2 + k) * 8 + 8], selb.t[0:n, hf * NPG:(hf + 1) * NPG], ident.t[0:n, 0:n],
                   [selb.b, ident.b], PT32.b)
        act(lambda e: e.copy(out=selTs.t[0:NPG, :, :, :].rearrange("p a b c -> p (a b c)"), in_=PT32.t[0:NPG, 0:32]), [PT32.b], [selTs.b])
        ld(winK.t[:, :, :], win_k[b].rearrange("(t p) c -> p t c", p=128), winK.b)
        ld(winV.t[:, :, :], win_v[b].rearrange("(t p) c -> p t c", p=128), winV.b)
        dve(lambda e: e.tensor_copy(out=winKb.t[:, :, :], in_=winK.t[:, :, :]), [winK.b], [winKb.b])
        for i in range(4):
            tr(PT.t[:, i, :], winKb.t[:, i, :], identb.t[:, :], [winKb.b, identb.b], PT.b)
        act(lambda e: e.copy(out=KTwin.t[:, :, :], in_=PT.t[:, 0:4, :]), [PT.b], [KTwin.b])
        dve(lambda e: e.tensor_copy(out=Vwin.t[:, :, :, 0:64], in_=winV.t[:, :, :].rearrange("p t (k d) -> p t k d", k=2)), [winV.b], [Vwin.b])
        for k in range(2):
            ks = slice(k * 64, (k + 1) * 64)
            tiles = [(KCTs.t[ks, q4, :], VCs.t[:, q4, k, :], NPG, [KCTs.b, VCs.b], []) for q4 in range(4)]
            attend(n, k, QTu, tiles, 0, True)
            tiles = [(KTwin.t[ks, i, :], Vwin.t[:, i, k, :], 128, [KTwin.b, Vwin.b],
                      ([(identb.t[:, :], bcast(WBs.t[:, :], 1, 4), [identb.b, WBs.b])] if i == 0 else [])) for i in range(4)]
            tiles.append((newK.t[ks, 1, :], newV.t[0:8, 1, k, :], 8, [newK.b, newV.b],
                          [(identb.t[:, 0:8], bcast(NB8.t[:, :], 1, 4), [identb.b, NB8.b])]))
            attend(n, k, QTr, tiles, 2, False)
        slc_rows(b, n)

    def slc_rows(b, n):
        mm(PO.t[0:32, 0:130], zerob.t[0:1, 0:32], zerob.t[0:1, 0:130], True, False, [zerob.b], PO.b)
        rounds = [(o, r0) for o in range(8) for r0 in range(0, 16, 4)]
        gks = {}

        def T_(ri):
            o, r0 = rounds[ri]
            if r0 == 0:
                gks[o] = gather(caches[2], o)
                gv = gather(caches[3], o)
                dve(lambda e: e.tensor_copy(out=Vb.t[:, :, :, 0:64], in_=gv.t[:, :].rearrange("p (r k d) -> p r k d", r=16, k=2)),
                    [gv.b], [Vb.b])
            gk = gks[o]
            bank = (PT32, PJ[1])[ri % 2]
            ktr = KTr[ri % 2]
            for rr in range(4):
                r = r0 + rr
                tr(bank.t[:, :].bitcast(BF16)[:, rr * 128:rr * 128 + NPG], gk.t[:, r * 128:(r + 1) * 128], identb.t[0:NPG, 0:NPG], [gk.b, identb.b], bank.b)
            act(lambda e: e.copy(out=ktr.t[:, 0:4, :], in_=bank.t[:, :].bitcast(BF16)[:, 0:512].rearrange("p (a b) -> p a b", a=4)[:, :, 0:NPG]),
                [bank.b], [ktr.b])

        def S_(ri):
            o, r0 = rounds[ri]
            hf = o // 4
            ktr = KTr[ri % 2]
            for k in range(2):
                ks = slice(k * 64, (k + 1) * 64)
                for rr in range(4):
                    outv = PS[k].t[0:NPG, rr * 32:rr * 32 + 32].rearrange("p (g q) -> p g q", g=4)
                    mm(outv, ktr.t[ks, rr, :], QTr.t[ks, :, 0:n], True, True, [ktr.b, QTr.b], PS[k].b)
                act(lambda e: e.activation(out=ET[k].t[0:NPG, 0:128], in_=PS[k].t[0:NPG, 0:128], func=AF.Exp, scale=0.125),
                    [PS[k].b], [ET[k].b])
                ev = ET[k].t[0:NPG, 0:128].rearrange("p (r g q) -> p r g q", r=4, g=4)
                dve(lambda e: e.tensor_tensor(out=ev, in0=ev, in1=bcast(bcast(selTs.t[0:NPG, hf, k, :], 1, 4), 1, 4), op=ALU.mult),
                    [ET[k].b, selTs.b], [ET[k].b])

        def PV_(ri):
            o, r0 = rounds[ri]
            for rr in range(4):
                r = r0 + rr
                for k in range(2):
                    mm(PO.t[0:32, k * 65:(k + 1) * 65], ET[k].t[0:NPG, rr * 32:(rr + 1) * 32], Vb.t[:, r, k, :], False, False,
                       [ET[k].b, Vb.b], PO.b)
        T_(0)
        for ri in range(len(rounds)):
            S_(ri)
            nxt_same = ri + 1 < len(rounds) and rounds[ri + 1][1] != 0
            if nxt_same:
                T_(ri + 1)
            PV_(ri)
            if ri + 1 < len(rounds) and not nxt_same:
                T_(ri + 1)
        for k in range(2):
            ks = slice(k * 64, (k + 1) * 64)
            outv = PS[k].t[0:8, 0:32].rearrange("p (g q) -> p g q", g=4)
            mm(outv, newK.t[ks, 0, :], QTr.t[ks, :, 0:n], True, False, [newK.b, QTr.b], PS[k].b)
            mm(outv, identb.t[:, 0:8], bcast(NB8.t[:, :], 1, 4), False, True, [identb.b, NB8.b], PS[k].b)
            act(lambda e: e.activation(out=ET[k].t[0:8, 0:32], in_=PS[k].t[0:8, 0:32], func=AF.Exp, scale=0.125), [PS[k].b], [ET[k].b])
            mm(PO.t[0:32, k * 65:(k + 1) * 65], ET[k].t[0:8, 0:32], newV.t[0:8, 0, k, :], False, True, [ET[k].b, newV.b], PO.b)
        pov = PO.t[0:32, 0:130].rearrange("p (k c) -> p k c", k=2)
        dve(lambda e: e.tensor_scalar(out=fac.t[0:32, 0:2], in0=pov[:, :, 64], scalar1=1e-30, scalar2=None, op0=ALU.add), [PO.b], [fac.b])
        dve(lambda e: e.reciprocal(out=fac.t[0:32, 0:2], in_=fac.t[0:32, 0:2]), [fac.b], [fac.b])
        dve(lambda e: e.tensor_tensor(out=Yn.t[:, :].rearrange("p (k d) -> p k d", k=2), in0=pov[:, :, 0:64],
                                      in1=bcast(fac.t[0:32, 0:2], 2, 64), op=ALU.mult), [PO.b, fac.b], [Yn.b])
        for g in range(4):
            mm(PJ[0].t[0:8, g * 128:(g + 1) * 128], selG.t[:, g, :], Yn.t[:, :], True, True, [selG.b, Yn.b], PJ[0].b)
        pjv = PJ[0].t[0:8, :].rearrange("p (g k d) -> p g k d", g=4, k=2)
        for k in range(2):
            dve(lambda e: e.tensor_tensor(out=ytmp.t[0:n, :, :], in0=pjv[:, :, k, :], in1=bcast(sig.t[0:n, 8 + k * 4:8 + k * 4 + 4], 2, 64),
                                          op=ALU.mult), [PJ[0].b, sig.b], [ytmp.b])
            dve(lambda e: e.tensor_tensor(out=ynsa.t[0:n, k, :, :], in0=ynsa.t[0:n, k, :, :], in1=ytmp.t[0:n, :, :],
                                          op=ALU.add), [ynsa.b, ytmp.b], [ynsa.b])
        nsa_out(n)
        finish(n, xs_t, pps[b * 8:(b + 1) * 8, :], y_s[b * 8:(b + 1) * 8, :])

    for b in range(SB_PER_CORE):
        if STOP <= 4:
            break
        sample_batch(b)
        if STOP <= 5:
            break
    fw.barrier()
    ess.close()
    es0.close()
    return nc


def _consts(SEQ, NPG, h):
    NB = SEQ // 128; NP = NB // 2; NSB = NB * 2
    c = {}

    def rope_tab(pos):
        pos = np.asarray(pos, np.float32)
        out = np.zeros((len(pos), 320), np.float32)
        inv64 = (10000.0 ** (-np.arange(64, dtype=np.float32) / 64)).astype(np.float32)
        inv32 = (10000.0 ** (-np.arange(32, dtype=np.float32) / 32)).astype(np.float32)
        a64 = pos[:, None] * inv64[None, :]
        a32 = pos[:, None] * inv32[None, :]
        out[:, 0:64] = np.cos(a64); out[:, 64:128] = np.sin(a64)
        out[:, 128:192] = np.cos(a64) * np.float32(128 ** -0.5); out[:, 192:256] = np.sin(a64) * np.float32(128 ** -0.5)
        out[:, 256:288] = np.cos(a32); out[:, 288:320] = np.sin(a32)
        return out
    c["rope_kv"] = rope_tab(np.arange(SEQ))
    own_pos = np.concatenate([np.arange(128) + (2 * p + h) * 128 for p in range(NP)])
    c["rope_own"] = rope_tab(own_pos)
    c["rope_s"] = rope_tab(NPG * 128 + np.arange(8))
    ki = np.arange(128)[:, None]; qi = np.arange(128)[None, :]
    tri = np.where(ki <= qi, 0.0, NEG).astype(np.float32)
    tri2 = np.where(ki >= qi, 0.0, NEG).astype(np.float32)
    Z = np.zeros((128, 128), np.float32); M = np.full((128, 128), NEG, np.float32)
    TB = [tri, M] if h == 0 else [Z, tri]
    WB = [tri2, Z, Z, Z, tri, M] if h == 0 else [M, tri2, Z, Z, Z, tri]
    c["mTB"] = np.stack(TB, 1); c["mWB"] = np.stack(WB, 1)
    CB = np.zeros((NP, 128, 128), np.float32); FB = np.zeros((NP, 128, NSB), np.float32)
    cc = np.arange(128)[:, None]
    for p in range(NP):
        j = 2 * p + h
        qpos = j * 128 + np.arange(128)[None, :]
        CB[p] = np.where(32 * cc + 31 <= qpos, 0.0, NEG)
        FB[p, :, 0] += 1e4
        cur = (j * 128 + np.arange(128)) // 64
        FB[p, np.arange(128), cur] += 1e4
    c["mCB"] = CB; c["mCBq"] = np.ascontiguousarray(CB.transpose(0, 2, 1)); c["mFB"] = FB
    g = (1.0 - 2.0 ** (-5.0 - np.arange(4, dtype=np.float64)))
    hs = np.zeros((128, 8), np.float32)
    if h == 0:
        hs[:, 0:4] = 1.0
    else:
        hs[:, 0:4] = (g ** 128)[None, :]; hs[:, 4:8] = 1.0
    c["hsel"] = hs

    def decs(C):
        i = np.arange(C)
        dm = np.zeros((C, 4, C)); qd = np.zeros((128, 4, C)); kd = np.zeros((C, 4))
        for hh in range(4):
            d = i[None, :] - i[:, None]
            dm[:, hh, :] = np.where(d >= 0, g[hh] ** np.maximum(d, 0), 0.0)
            qd[:, hh, :] = (g[hh] ** (i + 1.0))[None, :]
            kd[:, hh] = g[hh] ** (C - 1.0 - i)
        return dm.astype(np.float32), qd.astype(np.float32), kd.astype(np.float32)
    c["dmaskT"], c["qdecT"], c["kdec"] = decs(128)
    c["dmask8T"], c["qdec8T"], c["kdec8"] = decs(8)
    c["expand"] = (np.arange(SEQ)[None, :] // 64 == np.arange(NSB)[:, None]).astype(np.float32)
    c["mWBs"] = np.where(np.arange(128)[:, None] >= np.arange(8)[None, :], 0.0, NEG).astype(np.float32)
    c["selG"] = np.ascontiguousarray((np.arange(32)[:, None, None] == (np.arange(4)[None, :, None] * 8 + np.arange(8)[None, None, :])).astype(np.float32))
    c["mNB8"] = np.where(np.arange(8)[:, None] <= np.arange(8)[None, :], 0.0, NEG).astype(np.float32)
    return c


_NC_CACHE = {}


def run(inputs, SEQ, NPG, NPOOL):
    f = lambda a: np.ascontiguousarray(np.asarray(a))
    NB = SEQ // 128; NP = NB // 2
    key = (SEQ, NPG, NPOOL)
    if key not in _NC_CACHE:
        import os
        _NC_CACHE[key] = build(SEQ, NPG, NPOOL, int(os.environ.get('KSTOP', '99')))
    nc = _NC_CACHE[key]
    shared = {
        "w_in": f(inputs["w_in"][0]), "w_out": f(inputs["w_out"][0]), "w_gate": f(inputs["w_ple_gate"][0]),
        "w_ple": f(inputs["w_ple"][0]), "norm_mix": f(inputs["norm_mix"][0]), "norm_ple": f(inputs["norm_ple"][0]),
        "norm_f": f(inputs["norm_f"]), "gn_g": f(inputs["ret_gn_g"][0]), "gn_b": f(inputs["ret_gn_b"][0]),
        "pe_k": f(inputs["cmp_pe_k"][0]).reshape(32, 128), "pe_v": f(inputs["cmp_pe_v"][0]).reshape(32, 128),
        "w1_k": f(inputs["cmp_w1_k"][0]), "w1_v": f(inputs["cmp_w1_v"][0]),
        "w2_k": f(inputs["cmp_w2_k"][0]), "w2_v": f(inputs["cmp_w2_v"][0]),
        "c_ck": f(inputs["cache_cmp_k"][0]).reshape(NPOOL * 8, 2048), "c_cv": f(inputs["cache_cmp_v"][0]).reshape(NPOOL * 8, 2048),
        "c_sk": f(inputs["cache_slc_k"][0]).reshape(NPOOL * 8, 2048), "c_sv": f(inputs["cache_slc_v"][0]).reshape(NPOOL * 8, 2048),
    }
    consts = [_consts(SEQ, NPG, 0), _consts(SEQ, NPG, 1)]
    xp = np.asarray(inputs["x_prompt"]); pp = np.asarray(inputs["p_prompt"])[0]
    in_maps = []
    for core in range(8):
        b, h = core // 2, core % 2
        m = dict(shared)
        m.update(consts[h])
        m["xb"] = f(xp[b])
        m["x_own"] = f(xp[b].reshape(NP, 2, 128, 1024)[:, h].reshape(NP * 128, 1024))
        m["pp_own"] = f(pp[b].reshape(NP, 2, 128, 256)[:, h].reshape(NP * 128, 256))
        sb = slice(core * 4, core * 4 + 4)
        m["xs"] = f(np.asarray(inputs["x_sample"])[sb].reshape(32, 1024))
        m["pps"] = f(np.asarray(inputs["p_sample"])[0, sb].reshape(32, 256))
        m["win_k"] = f(np.asarray(inputs["state_win_k"])[0, sb].reshape(4, 512, 128))
        m["win_v"] = f(np.asarray(inputs["state_win_v"])[0, sb].reshape(4, 512, 128))
        m["st_ret"] = f(np.asarray(inputs["state_ret"])[0, sb])
        m["ptab"] = f(np.asarray(inputs["page_table"])[sb].astype(np.int32))
        in_maps.append(m)
    import os
    if os.environ.get("KTRACE"):
        rr_ = run_bass_kernel_spmd(nc, in_maps, core_ids=list(range(8)), trace=True)
        print("EXEC_TIME_NS", rr_.exec_time_ns)
        res = rr_.results
    else:
        res = run_bass_kernel_spmd(nc, in_maps, core_ids=list(range(8))).results
    global LAST_RES
    LAST_RES = res
    B = 4
    y_prompt = np.zeros((B, SEQ, 1024), np.float32)
    ret_p = np.zeros((1, B, 4, 128, 128), np.float32)
    kvp = np.zeros((B, SEQ, 4, 128), np.float32)
    winp = np.zeros((B, 512, 2, 128), np.float32)
    y_s = np.zeros((32, 8, 1024), np.float32); ret_s = np.zeros((1, 32, 4, 128, 128), np.float32)
    kvs = np.zeros((32, 8, 4, 128), np.float32); wins = np.zeros((32, 2, 512, 128), np.float32)
    for core in range(8):
        b, h = core // 2, core % 2
        r = res[core]
        y_prompt[b].reshape(NP, 2, 128, 1024)[:, h] = r["y_own"].reshape(NP, 128, 1024)
        if h == 0:
            ret_p[0, b] = r["ret_p"]; kvp[b] = r["kvout"]; winp[b] = r["winout"]
        sb = slice(core * 4, core * 4 + 4)
        y_s[sb] = r["y_s"].reshape(4, 8, 1024); ret_s[0, sb] = r["ret_s"]
        kvs[sb] = r["kv_s"].reshape(4, 8, 4, 128); wins[sb] = r["win_s"]
    sh = lambda a: np.ascontiguousarray(a)[None].reshape((1,) + a.shape[:-1] + (2, 64))
    return (y_prompt, y_s, ret_p,
            sh(kvp[:, :, 0]), sh(kvp[:, :, 1]), sh(kvp[:, :, 2]), sh(kvp[:, :, 3]),
            sh(winp[:, :, 0]), sh(winp[:, :, 1]),
            ret_s, sh(kvs[:, :, 0]), sh(kvs[:, :, 1]), sh(kvs[:, :, 2]), sh(kvs[:, :, 3]),
            sh(wins[:, 0]), sh(wins[:, 1]))


def kernel(**inputs):
    SEQ = inputs["x_prompt"].shape[1]
    NPG = inputs["page_table"].shape[1]
    NPOOL = inputs["cache_cmp_k"].shape[1]
    return run(inputs, SEQ, NPG, NPOOL)
```
